# Optimizing a Trainium2 kernel written in Bass

```python
import math
import jax, jax.numpy as jnp
from jax import lax
import numpy as np

D_MODEL = 1024
BATCH = 4
SEQ = 8192
DEPTH = 4
DEC_BATCH = 4
DEC_SEQ = 4096
PAST_LEN = 128

N_MIXERS = 2
N_LAYERS_A = (DEPTH + 1) // 2
N_LAYERS_B = DEPTH // 2

DA_HEADS = 8
DA_HEAD_DIM = 64
DA_V_DIM = 2 * DA_HEAD_DIM
DA_QK_WIDTH = DA_HEADS * DA_HEAD_DIM
DA_OUT_WIDTH = DA_HEADS * DA_V_DIM
DA_IN_WIDTH = 4 * DA_QK_WIDTH + DA_OUT_WIDTH
Q_BLOCK = 128

DIL_PATTERNS = ((128, 1), (512, 4), (2048, 16))
DIL_GROUPS = len(DIL_PATTERNS)
DIL_HEADS = 16
DIL_HEAD_DIM = 64
DIL_GROUP_WIDTH = DIL_HEADS * DIL_HEAD_DIM
DIL_IN_WIDTH = DIL_GROUPS * 3 * DIL_GROUP_WIDTH

ROPE_THETA = 500000.0
ROPE_FRACTION_DIV = 4

LN_EPS = 1e-5
DEEPNORM_ALPHA = (2.0 * DEPTH) ** 0.25
DEEPNORM_BETA = (8.0 * DEPTH) ** -0.25

MOE_GROUPS = 4
MOE_EXPERTS = 4
MOE_TOP_K = 2
MOE_FF = 512

NEG_INF = -1e30

kernel_name = 'hybrid_diff_dilated_hmoe_encoder'


def layer_norm(x, g, b):
    xf = x.astype(jnp.float32)
    mu = jnp.mean(xf, axis=-1, keepdims=True)
    var = jnp.mean(jnp.square(xf - mu), axis=-1, keepdims=True)
    y = (xf - mu) * lax.rsqrt(var + LN_EPS) * g.astype(jnp.float32) + b.astype(jnp.float32)
    return y.astype(x.dtype)


def partial_rope(x):
    S, dh = x.shape[1], x.shape[-1]
    rot = dh // ROPE_FRACTION_DIV
    half = rot // 2
    inv = ROPE_THETA ** (-jnp.arange(half, dtype=jnp.float32) * (2.0 / rot))
    ang = jnp.arange(S, dtype=jnp.float32)[:, None] * inv[None, :]
    cos = jnp.cos(ang)[None, :, None, :]
    sin = jnp.sin(ang)[None, :, None, :]
    xr = x[..., :rot].astype(jnp.float32)
    x1, x2 = xr[..., :half], xr[..., half:]
    r = jnp.concatenate([x1 * cos - x2 * sin, x2 * cos + x1 * sin], axis=-1).astype(x.dtype)
    return jnp.concatenate([r, x[..., rot:]], axis=-1)


def diff_attention(x, w_in, w_out, lam_q1, lam_k1, lam_q2, lam_k2, subln_g, lambda_init):
    B, S, _ = x.shape
    H, dh = DA_HEADS, DA_HEAD_DIM
    nblk = S // Q_BLOCK
    proj = x @ w_in
    W = DA_QK_WIDTH
    q1, q2, k1, k2, v = jnp.split(proj, [W, 2 * W, 3 * W, 4 * W], axis=-1)
    hd = lambda t: partial_rope(t.reshape(B, S, H, dh))
    q = jnp.stack([hd(q1), hd(q2)], axis=0)
    k = jnp.stack([hd(k1), hd(k2)], axis=0).transpose(0, 1, 3, 2, 4)
    v = v.reshape(B, S, H, DA_V_DIM).transpose(0, 2, 1, 3)
    lam = (jnp.exp(jnp.sum(lam_q1.astype(jnp.float32) * lam_k1.astype(jnp.float32)))
           - jnp.exp(jnp.sum(lam_q2.astype(jnp.float32) * lam_k2.astype(jnp.float32)))
           + lambda_init)
    scale = dh ** -0.5
    qb = q.reshape(2, B, nblk, Q_BLOCK, H, dh).transpose(2, 0, 1, 4, 3, 5)

    def block(qblk):
        s = jnp.einsum('mbhqd,mbhkd->mbhqk', qblk, k).astype(jnp.float32) * scale
        p = jax.nn.softmax(s, axis=-1)
        a = p[0] - lam * p[1]
        return jnp.einsum('bhqk,bhkd->bhqd', a.astype(v.dtype), v)

    o = lax.map(block, qb)
    o = o.transpose(1, 0, 3, 2, 4).reshape(B, S, H, DA_V_DIM)
    of = o.astype(jnp.float32)
    of = of * lax.rsqrt(jnp.mean(jnp.square(of), axis=-1, keepdims=True) + LN_EPS)
    of = of * subln_g.astype(jnp.float32) * (1.0 - lambda_init)
    return of.astype(x.dtype).reshape(B, S, DA_OUT_WIDTH) @ w_out


def dilated_group(q, k, v, dilation, half):
    B, S, H, dh = q.shape
    L = S // dilation
    nb = -(-L // half)
    Lp = nb * half

    def to_sub(t, extra):
        t = t.reshape(B, L, dilation, H, dh).transpose(0, 2, 1, 3, 4)
        return jnp.pad(t, ((0, 0), (0, 0), (extra, Lp - L + extra), (0, 0), (0, 0)))

    def neighbours(t):
        return jnp.concatenate([t[:, :, :-2], t[:, :, 1:-1], t[:, :, 2:]], axis=3)

    qs = to_sub(q, 0).reshape(B, dilation, nb, half, H, dh)
    kw = neighbours(to_sub(k, half).reshape(B, dilation, nb + 2, half, H, dh))
    vw = neighbours(to_sub(v, half).reshape(B, dilation, nb + 2, half, H, dh))
    s = jnp.einsum('brnqhd,brnkhd->brnhqk', qs, kw).astype(jnp.float32) * (dh ** -0.5)
    qi = jnp.arange(half)[:, None]
    kj = jnp.arange(3 * half)[None, :]
    rel = kj - half - qi
    kpos = (jnp.arange(nb)[:, None, None] - 1) * half + kj[None]
    valid = (jnp.abs(rel) <= half)[None] & (kpos >= 0) & (kpos < L)
    s = jnp.where(valid[None, None, :, None], s, NEG_INF)
    lse = jax.nn.logsumexp(s, axis=-1)
    p = jnp.exp(s - lse[..., None])
    o = jnp.einsum('brnhqk,brnkhd->brnqhd', p.astype(v.dtype), vw)
    o = o.reshape(B, dilation, Lp, H, dh)[:, :, :L].transpose(0, 2, 1, 3, 4).reshape(B, S, H, dh)
    lse = lse.transpose(0, 1, 2, 4, 3).reshape(B, dilation, Lp, H)[:, :, :L]
    lse = lse.transpose(0, 2, 1, 3).reshape(B, S, H)
    return o, lse


def dilated_attention(x, w_in, w_out):
    B, S, _ = x.shape
    G, H, dh = DIL_GROUPS, DIL_HEADS, DIL_HEAD_DIM
    proj = (x @ w_in).reshape(B, S, G, 3, H, dh)
    q = partial_rope(proj[:, :, :, 0].reshape(B, S, G * H, dh)).reshape(B, S, G, H, dh)
    k = partial_rope(proj[:, :, :, 1].reshape(B, S, G * H, dh)).reshape(B, S, G, H, dh)
    v = proj[:, :, :, 2]
    outs, lses = [], []
    for g, (window, dilation) in enumerate(DIL_PATTERNS):
        half = window // (2 * dilation)
        o, l = dilated_group(q[:, :, g], k[:, :, g], v[:, :, g], dilation, half)
        outs.append(o)
        lses.append(l)
    wts = jax.nn.softmax(jnp.stack(lses, axis=0), axis=0)
    o = jnp.sum(wts[..., None].astype(v.dtype) * jnp.stack(outs, axis=0), axis=0)
    return o.reshape(B, S, DIL_GROUP_WIDTH) @ w_out


def hier_moe(x, w_router_group, w_router_expert, w_gate, w_up, w_down):
    B, S, D = x.shape
    t = x.reshape(B * S, D)
    gp = jax.nn.softmax((t @ w_router_group).astype(jnp.float32), axis=-1)
    gw, gsel = lax.top_k(gp, 1)
    gmask = jax.nn.one_hot(gsel[:, 0], MOE_GROUPS, dtype=jnp.float32)
    el = jnp.einsum('td,dge->tge', t, w_router_expert).astype(jnp.float32)
    el_sel = jnp.einsum('tge,tg->te', el, gmask)
    topv, topi = lax.top_k(el_sel, MOE_TOP_K)
    tw = jax.nn.softmax(topv, axis=-1) * gw
    ew = jnp.sum(jax.nn.one_hot(topi, MOE_EXPERTS, dtype=jnp.float32) * tw[..., None], axis=1)
    combine = gmask[:, :, None] * ew[:, None, :]
    y = jnp.zeros_like(t)
    for g in range(MOE_GROUPS):
        h = jax.nn.silu(jnp.einsum('td,edf->tef', t, w_gate[g])) * jnp.einsum('td,edf->tef', t, w_up[g])
        h = h * combine[:, g, :, None].astype(h.dtype)
        y = y + jnp.einsum('tef,efd->td', h, w_down[g])
    return y.reshape(B, S, D)


def trunk(x, da_w_in, da_w_out, da_lambda_q1, da_lambda_k1, da_lambda_q2, da_lambda_k2, da_subln_g,
          dl_w_in, dl_w_out, ln1_g, ln1_b, ln2_g, ln2_b,
          moe_router_group, moe_router_expert, moe_w_gate, moe_w_up, moe_w_down):
    for i in range(DEPTH):
        j = i // N_MIXERS
        if i % N_MIXERS == 0:
            lambda_init = 0.8 - 0.6 * math.exp(-0.3 * i)
            h = diff_attention(x, da_w_in[j], da_w_out[j], da_lambda_q1[j], da_lambda_k1[j],
                               da_lambda_q2[j], da_lambda_k2[j], da_subln_g[j], lambda_init)
        else:
            h = dilated_attention(x, dl_w_in[j], dl_w_out[j])
        x = layer_norm(DEEPNORM_ALPHA * x + h, ln1_g[i], ln1_b[i])
        f = hier_moe(x, moe_router_group[i], moe_router_expert[i], moe_w_gate[i], moe_w_up[i], moe_w_down[i])
        x = layer_norm(DEEPNORM_ALPHA * x + f, ln2_g[i], ln2_b[i])
    return x


def setup_inputs(seed: int = 0) -> dict:
    key = jax.random.key(seed)
    ks = jax.random.split(key, 24)
    nrm = lambda k, shape: jax.random.normal(k, shape, dtype=jnp.float32)
    D = D_MODEL
    beta = DEEPNORM_BETA
    da_col = jnp.concatenate([jnp.ones((4 * DA_QK_WIDTH,), jnp.float32),
                              jnp.full((DA_OUT_WIDTH,), beta, jnp.float32)])
    dl_col = jnp.ones((DIL_GROUPS, 3, DIL_GROUP_WIDTH), jnp.float32).at[:, 2].set(beta).reshape(-1)
    return {
        'x_prompt': nrm(ks[0], (BATCH, SEQ, D)),
        'x_sample': nrm(ks[1], (DEC_BATCH, DEC_SEQ, D)),
        'da_w_in': nrm(ks[2], (N_LAYERS_A, D, DA_IN_WIDTH)) * (D ** -0.5) * da_col,
        'da_w_out': nrm(ks[3], (N_LAYERS_A, DA_OUT_WIDTH, D)) * (DA_OUT_WIDTH ** -0.5) * beta,
        'da_lambda_q1': nrm(ks[4], (N_LAYERS_A, DA_HEAD_DIM)) * 0.1,
        'da_lambda_k1': nrm(ks[5], (N_LAYERS_A, DA_HEAD_DIM)) * 0.1,
        'da_lambda_q2': nrm(ks[6], (N_LAYERS_A, DA_HEAD_DIM)) * 0.1,
        'da_lambda_k2': nrm(ks[7], (N_LAYERS_A, DA_HEAD_DIM)) * 0.1,
        'da_subln_g': 1.0 + 0.02 * nrm(ks[8], (N_LAYERS_A, DA_V_DIM)),
        'dl_w_in': nrm(ks[9], (N_LAYERS_B, D, DIL_IN_WIDTH)) * (D ** -0.5) * dl_col,
        'dl_w_out': nrm(ks[10], (N_LAYERS_B, DIL_GROUP_WIDTH, D)) * (DIL_GROUP_WIDTH ** -0.5) * beta,
        'ln1_g': 1.0 + 0.02 * nrm(ks[11], (DEPTH, D)),
        'ln1_b': 0.02 * nrm(ks[12], (DEPTH, D)),
        'ln2_g': 1.0 + 0.02 * nrm(ks[13], (DEPTH, D)),
        'ln2_b': 0.02 * nrm(ks[14], (DEPTH, D)),
        'moe_router_group': nrm(ks[15], (DEPTH, D, MOE_GROUPS)) * (D ** -0.5),
        'moe_router_expert': nrm(ks[16], (DEPTH, D, MOE_GROUPS, MOE_EXPERTS)) * (D ** -0.5),
        'moe_w_gate': nrm(ks[17], (DEPTH, MOE_GROUPS, MOE_EXPERTS, D, MOE_FF)) * (D ** -0.5),
        'moe_w_up': nrm(ks[18], (DEPTH, MOE_GROUPS, MOE_EXPERTS, D, MOE_FF)) * (D ** -0.5) * beta,
        'moe_w_down': nrm(ks[19], (DEPTH, MOE_GROUPS, MOE_EXPERTS, MOE_FF, D)) * (MOE_FF ** -0.5) * beta,
    }


def reference(x_prompt, x_sample, da_w_in, da_w_out, da_lambda_q1, da_lambda_k1, da_lambda_q2,
              da_lambda_k2, da_subln_g, dl_w_in, dl_w_out, ln1_g, ln1_b, ln2_g, ln2_b,
              moe_router_group, moe_router_expert, moe_w_gate, moe_w_up, moe_w_down):
    y_prompt = trunk(x_prompt, da_w_in, da_w_out, da_lambda_q1, da_lambda_k1, da_lambda_q2, da_lambda_k2,
                     da_subln_g, dl_w_in, dl_w_out, ln1_g, ln1_b, ln2_g, ln2_b,
                     moe_router_group, moe_router_expert, moe_w_gate, moe_w_up, moe_w_down)
    y_sample = trunk(x_sample, da_w_in, da_w_out, da_lambda_q1, da_lambda_k1, da_lambda_q2, da_lambda_k2,
                     da_subln_g, dl_w_in, dl_w_out, ln1_g, ln1_b, ln2_g, ln2_b,
                     moe_router_group, moe_router_expert, moe_w_gate, moe_w_up, moe_w_down)
    return (y_prompt, y_sample)
```

```python
import math
from contextlib import ExitStack
import numpy as np
import concourse.bass as bass
import concourse.mybir as mybir
from concourse.bass_utils import run_bass_kernel_spmd

F32 = mybir.dt.float32
BF16 = mybir.dt.bfloat16
AF = mybir.ActivationFunctionType
ALU = mybir.AluOpType
AX = mybir.AxisListType

D = 1024
DEPTH = 4
LN_EPS = 1e-5
ALPHA = (2.0 * DEPTH) ** 0.25
ROPE_THETA = 500000.0
DIL = (1, 4, 16)
NEG = -30000.0
import os
KSTOP = int(os.environ.get('KSTOP', '3'))
DA_FIN = int(os.environ.get('DA_FIN', '1'))
DA_LOOP = int(os.environ.get('DA_LOOP', '1'))


class Buf:
    __slots__ = ("name", "w", "r", "ds")

    def __init__(self, name):
        self.name = name
        self.w = {}
        self.r = {}
        self.ds = None


class Trk:
    def __init__(self, nc):
        self.nc = nc
        self.eng = {"pe": nc.tensor, "act": nc.scalar, "dve": nc.vector, "pool": nc.gpsimd, "sp": nc.sync}
        self.esem = {k: nc.alloc_semaphore(name="e_" + k) for k in ("pe", "act", "dve", "pool")}
        self.ecnt = {k: 0 for k in self.esem}
        self.pending = {k: [] for k in self.esem}
        self.waited = {e: {} for e in self.eng}
        self.dsems = []
        self.free_ds = []
        self.ninstr = 0

    def get_ds(self):
        if self.free_ds:
            return self.free_ds.pop()
        h = self.nc.alloc_semaphore(name="d%d" % len(self.dsems))
        self.dsems.append([h, 0])
        return len(self.dsems) - 1

    def release_ds(self, bufs):
        for b in bufs:
            if b.ds is not None:
                self.free_ds.append(b.ds)
                b.ds = None

    def _semval(self, key, tok):
        if key[0] == "E":
            assert tok[1] is not None, "dependency on unsignaled instr of %s" % key[1]
            return self.esem[key[1]], tok[1]
        h, tot = self.dsems[key[1]]
        return h, tot

    def _waits(self, e, deps, skip_same, raw=None):
        for key, tok in deps.items():
            if skip_same and key == ("E", e):
                if e == "pe" or raw is None or key not in raw:
                    continue
                tok = raw[key]
            h, val = self._semval(key, tok)
            if self.waited[e].get(key, 0) >= val:
                continue
            self.eng[e].wait_ge(h, val)
            self.waited[e][key] = val
            self.ninstr += 1

    @staticmethod
    def _collect(reads, writes):
        deps = {}

        def add(k, t):
            o = deps.get(k)
            if o is None or t[1] is None or (o[1] is not None and t[1] > o[1]):
                deps[k] = t

        for b in reads:
            for k, t in b.w.items():
                add(k, t)
        for b in writes:
            for k, t in b.w.items():
                add(k, t)
            for k, t in b.r.items():
                add(k, t)
        return deps

    @staticmethod
    def _record(key, tok, reads, writes):
        for b in reads:
            o = b.r.get(key)
            if o is None or tok[1] is None or (o[1] is not None and tok[1] >= o[1]):
                b.r[key] = tok
        for b in writes:
            b.w = {key: tok}
            b.r = {}

    def op(self, e, fn, reads=(), writes=(), sig=True):
        deps = self._collect(reads, writes)
        raw = self._collect(reads, ())
        self._waits(e, deps, True, raw)
        ins = fn()
        self.ninstr += 1
        key = ("E", e)
        tok = [key, None]
        if sig:
            self.ecnt[e] += 1
            ins.then_inc(self.esem[e], 1)
            tok[1] = self.ecnt[e]
            for t in self.pending[e]:
                t[1] = self.ecnt[e]
            self.pending[e] = []
        else:
            self.pending[e].append(tok)
        self._record(key, tok, reads, writes)
        return ins

    def dma(self, q, out, in_, reads=(), writes=(), sb=None):
        deps = self._collect(reads, writes)
        self._waits(q, deps, False)
        if sb.ds is None:
            sb.ds = self.get_ds()
        ins = self.eng[q].dma_start(out=out, in_=in_)
        self.ninstr += 1
        d = self.dsems[sb.ds]
        d[1] += 16
        ins.then_inc(d[0], 16)
        key = ("D", sb.ds)
        tok = [key, d[1]]
        self._record(key, tok, reads, writes)
        return ins

    def barrier(self):
        for e in self.eng:
            assert not self.pending.get(e, [])
        for e in self.eng:
            for k in self.esem:
                if k == e:
                    continue
                key = ("E", k)
                val = self.ecnt[k]
                if val > 0 and self.waited[e].get(key, 0) < val:
                    self.eng[e].wait_ge(self.esem[k], val)
                    self.waited[e][key] = val
            for i, (h, tot) in enumerate(self.dsems):
                key = ("D", i)
                if tot > 0 and self.waited[e].get(key, 0) < tot:
                    self.eng[e].wait_ge(h, tot)
                    self.waited[e][key] = tot


class Rot:
    def __init__(self, items):
        self.items = items
        self.i = 0

    def next(self):
        it = self.items[self.i % len(self.items)]
        self.i += 1
        return it


class Ctx:
    def __init__(self, nc, trk):
        self.nc = nc
        self.trk = trk
        self.es = ExitStack()
        self.bufs = []

    _uid = [0]

    def sb(self, name, shape, dt):
        Ctx._uid[0] += 1
        name = "s%d_%s" % (Ctx._uid[0], name)
        t = self.es.enter_context(self.nc.sbuf_tensor(name, list(shape), dt))
        b = Buf(name)
        self.bufs.append(b)
        return t, b

    def ps(self, name, shape, dt=F32):
        Ctx._uid[0] += 1
        name = "p%d_%s" % (Ctx._uid[0], name)
        t = self.es.enter_context(self.nc.psum_tensor(name, list(shape), dt))
        b = Buf(name)
        self.bufs.append(b)
        return t, b

    def rot(self, name, n, shape, dt, psum=False):
        f = self.ps if psum else self.sb
        return Rot([f("%s%d" % (name, i), shape, dt) for i in range(n)])

    def close(self):
        self.trk.barrier()
        self.trk.release_ds(self.bufs)
        self.es.close()


def build_program(S, layers, nsub_tok=4):
    assert S % 2048 == 0
    NT = S // 512
    NB = S // 128
    nc = bass.Bass("TRN2", target_bir_lowering=False)
    dt_in = lambda name, shape: nc.dram_tensor(name, list(shape), F32, kind="ExternalInput").ap()
    x_in = dt_in("x", [S, D])
    kbda = dt_in("kbda", [128, NB])
    kbdl = [dt_in("kbdl%d" % g, [128, NB]) for g in range(3)]
    rc_in = dt_in("ropec", [128, S])
    rs_in = dt_in("ropes", [128, S])
    ident_in = dt_in("ident", [128, 128])
    mask_in = dt_in("bmask", [128, 256])
    sel_in = dt_in("sel", [16, 16 * 128])
    da_wm = dt_in("da_wm", [2, D, 3072])
    da_wp = dt_in("da_wp", [2, D, 2048])
    da_wo = dt_in("da_wo", [2, D, D])
    da_lam = dt_in("da_lam", [2, 4, 64])
    da_g = dt_in("da_g", [2, 128])
    dl_wm = dt_in("dl_wm", [2, 3, D, 3072])
    dl_wp = dt_in("dl_wp", [2, 3, D, 2048])
    dl_wo = dt_in("dl_wo", [2, D, D])
    lnp = dt_in("lnp", [4, 4, D])
    w_r = dt_in("w_r", [4, D, 20])
    w_gate = dt_in("w_gate", [4, 16, D, 512])
    w_up = dt_in("w_up", [4, 16, D, 512])
    w_down = dt_in("w_down", [4, 16, 512, D])
    y_out = nc.dram_tensor("y", [S, D], F32, kind="ExternalOutput").ap()
    KDBG = int(os.environ.get("KDBG", "0"))
    scr = lambda name, shape, dt: nc.dram_tensor(name, list(shape), dt, kind=("ExternalOutput" if KDBG else "Internal")).ap()
    QT = scr("QT", [3, D, S], BF16)
    KT = scr("KT", [3, D, S], BF16)
    VV = scr("VV", [3, S, D], BF16)
    OT = scr("OT", [D, S], BF16)
    XR = [scr("XR0", [S, D], F32), scr("XR1", [S, D], F32)]
    bQT = [Buf("QT%d" % g) for g in range(3)]
    bKT = [Buf("KT%d" % g) for g in range(3)]
    bVV = [Buf("VV%d" % g) for g in range(3)]
    bOT = Buf("OT")
    bXR = [Buf("XR0"), Buf("XR1")]
    bIN = Buf("inputs")

    T = Trk(nc)
    E = T.eng
    pe, act, dve, pool = nc.tensor, nc.scalar, nc.vector, nc.gpsimd

    G = Ctx(nc, T)
    ident_bf, b_ident_bf = G.sb("ident_bf", [128, 128], BF16)
    ident_f, b_ident_f = G.sb("ident_f", [128, 128], F32)
    ones_bf, b_ones_bf = G.sb("ones_bf", [128, 128], BF16)
    ones_f, b_ones_f = G.sb("ones_f", [128, 128], F32)
    bmask, b_bmask = G.sb("bmask", [128, 256], BF16)
    sel, b_sel = G.sb("sel", [16, 16 * 128], F32)
    T.dma("pool", ident_bf[:], ident_in, reads=[bIN], writes=[b_ident_bf], sb=b_ident_bf)
    T.dma("sp", ident_f[:], ident_in, reads=[bIN], writes=[b_ident_f], sb=b_ident_f)
    T.dma("pool", bmask[:], mask_in, reads=[bIN], writes=[b_bmask], sb=b_bmask)
    T.dma("sp", sel[:], sel_in, reads=[bIN], writes=[b_sel], sb=b_sel)
    T.op("dve", lambda: dve.memset(ones_bf[:], 1.0), writes=[b_ones_bf])
    T.op("dve", lambda: dve.memset(ones_f[:], 1.0), writes=[b_ones_f])

    def proj_pass(x_src, b_xsrc, wm_ap, wp_ap, g):
        C = Ctx(nc, T)
        wm, b_wm = C.sb("wm", [128, 8, 3072], BF16)
        wp, b_wp = C.sb("wp", [128, 8, 2048], BF16)
        for kc in range(8):
            T.dma("pool", wm[:, kc, :], wm_ap[kc * 128:(kc + 1) * 128, :], reads=[bIN], writes=[b_wm], sb=b_wm)
            T.dma("pool", wp[:, kc, :], wp_ap[kc * 128:(kc + 1) * 128, :], reads=[bIN], writes=[b_wp], sb=b_wp)
        xt_r = C.rot("xt", 2, [128, 4, D], F32)
        xb_r = C.rot("xb", 1, [128, 4, D], BF16)
        xT_r = C.rot("xT", 2, [128, 8, 512], BF16)
        rc_r = C.rot("rc", 2, [128, 512], F32)
        rs_r = C.rot("rs", 2, [128, 512], F32)
        t1_r = C.rot("t1", 2, [128, 512], F32)
        t2_r = C.rot("t2", 2, [128, 512], F32)
        st_r = C.rot("st", 3, [128, 512], BF16)
        vst_r = C.rot("vst", 2, [128, 4, D], BF16)
        tp_r = C.rot("tp", 2, [128, 512], BF16, psum=True)
        pm_r = C.rot("pm", 2, [128, 512], F32, psum=True)
        pp_r = C.rot("pp", 2, [128, 512], F32, psum=True)
        pv_r = C.rot("pv", 2, [128, 512], F32, psum=True)

        def load(ti):
            xt, b_xt = xt_r.next()
            T.dma("sp", xt[:], x_src[ti * 512:(ti + 1) * 512, :].rearrange("(s p) d -> p s d", p=128),
                  reads=[b_xsrc], writes=[b_xt], sb=b_xt)
            rc, b_rc = rc_r.next()
            rs, b_rs = rs_r.next()
            T.dma("sp", rc[:], rc_in[:, ti * 512:(ti + 1) * 512], reads=[bIN], writes=[b_rc], sb=b_rc)
            T.dma("sp", rs[:], rs_in[:, ti * 512:(ti + 1) * 512], reads=[bIN], writes=[b_rs], sb=b_rs)
            return (xt, b_xt, rc, b_rc, rs, b_rs)

        nxt = load(0)
        for ti in range(NT):
            xt, b_xt, rc, b_rc, rs, b_rs = nxt
            if ti + 1 < NT:
                nxt = load(ti + 1)
            xb, b_xb = xb_r.next()
            for s in range(4):
                if s % 2 == 0:
                    T.op("act", lambda: act.copy(out=xb[:, s, :], in_=xt[:, s, :]), reads=[b_xt], writes=[b_xb])
                else:
                    T.op("dve", lambda: dve.tensor_copy(out=xb[:, s, :], in_=xt[:, s, :]), reads=[b_xt], writes=[b_xb])
            xT, b_xT = xT_r.next()
            for kc in range(8):
                tp, b_tp = tp_r.next()
                for s in range(4):
                    T.op("pe", lambda: pe.transpose(out=tp[:, s * 128:(s + 1) * 128], in_=xb[:, s, kc * 128:(kc + 1) * 128],
                                                    identity=ident_bf[:]),
                         reads=[b_xb, b_ident_bf], writes=[b_tp], sig=(s == 3))
                if kc % 2 == 0:
                    T.op("act", lambda: act.copy(out=xT[:, kc, :], in_=tp[:]), reads=[b_tp], writes=[b_xT])
                else:
                    T.op("dve", lambda: dve.tensor_copy(out=xT[:, kc, :], in_=tp[:]), reads=[b_tp], writes=[b_xT])
            for oc in range(16):
                pm, b_pm = pm_r.next()
                pp, b_pp = pp_r.next()
                for kc in range(8):
                    T.op("pe", lambda: pe.matmul(pm[:], lhsT=wm[:, kc, oc * 128:(oc + 1) * 128], rhs=xT[:, kc, :],
                                                 start=(kc == 0), stop=(kc == 7)),
                         reads=[b_wm, b_xT], writes=[b_pm], sig=(kc == 7))
                for kc in range(8):
                    T.op("pe", lambda: pe.matmul(pp[:], lhsT=wp[:, kc, oc * 128:(oc + 1) * 128], rhs=xT[:, kc, :],
                                                 start=(kc == 0), stop=(kc == 7)),
                         reads=[b_wp, b_xT], writes=[b_pp], sig=(kc == 7))
                t1, b_t1 = t1_r.next()
                t2, b_t2 = t2_r.next()
                st, b_st = st_r.next()
                T.op("dve", lambda: dve.tensor_tensor(out=t1[:], in0=pm[:], in1=rc[:], op=ALU.mult),
                     reads=[b_pm, b_rc], writes=[b_t1])
                T.op("dve", lambda: dve.tensor_tensor(out=t2[:], in0=pp[:], in1=rs[:], op=ALU.mult),
                     reads=[b_pp, b_rs], writes=[b_t2])
                T.op("pool", lambda: pool.tensor_tensor(out=st[:], in0=t1[:], in1=t2[:], op=ALU.add),
                     reads=[b_t1, b_t2], writes=[b_st])
                dst = QT if oc < 8 else KT
                bd = bQT[g] if oc < 8 else bKT[g]
                c8 = oc % 8
                T.dma("sp", dst[g, c8 * 128:(c8 + 1) * 128, ti * 512:(ti + 1) * 512], st[:], reads=[b_st], writes=[bd], sb=b_st)
            vst, b_vst = vst_r.next()
            for s in range(4):
                for hf in range(2):
                    pv, b_pv = pv_r.next()
                    for kc in range(8):
                        T.op("pe", lambda: pe.matmul(pv[:], lhsT=xT[:, kc, s * 128:(s + 1) * 128],
                                                     rhs=wm[:, kc, 2048 + hf * 512:2048 + (hf + 1) * 512],
                                                     start=(kc == 0), stop=(kc == 7)),
                             reads=[b_wm, b_xT], writes=[b_pv], sig=(kc == 7))
                    T.op("act", lambda: act.copy(out=vst[:, s, hf * 512:(hf + 1) * 512], in_=pv[:]), reads=[b_pv], writes=[b_vst])
            T.dma("sp", VV[g, ti * 512:(ti + 1) * 512, :].rearrange("(s p) d -> p s d", p=128), vst[:],
                  reads=[b_vst], writes=[bVV[g]], sb=b_vst)
        C.close()

    def da_attention(j, lambda_init):
        C = Ctx(nc, T)
        kb, b_kb = C.sb("kb", [128, NB], F32)
        T.dma("sp", kb[:], kbda, reads=[bIN], writes=[b_kb], sb=b_kb)
        lamv, b_lamv = C.sb("lamv", [1, 4, 64], F32)
        T.dma("sp", lamv[:], da_lam[j:j + 1, :, :], reads=[bIN], writes=[b_lamv], sb=b_lamv)
        lt, b_lt = C.sb("lt", [1, 2, 64], F32)
        ls, b_ls = C.sb("ls", [1, 4], F32)
        T.op("dve", lambda: dve.tensor_tensor(out=lt[:, 0, :], in0=lamv[:, 0, :], in1=lamv[:, 1, :], op=ALU.mult),
             reads=[b_lamv], writes=[b_lt])
        T.op("dve", lambda: dve.tensor_tensor(out=lt[:, 1, :], in0=lamv[:, 2, :], in1=lamv[:, 3, :], op=ALU.mult),
             reads=[b_lamv], writes=[b_lt])
        T.op("dve", lambda: dve.tensor_reduce(out=ls[:, 0:2], in_=lt[:], axis=AX.X, op=ALU.add), reads=[b_lt], writes=[b_ls])
        T.op("act", lambda: act.activation(out=ls[:, 2:4], in_=ls[:, 0:2], func=AF.Exp), reads=[b_ls], writes=[b_ls])
        T.op("dve", lambda: dve.tensor_tensor(out=ls[:, 0:1], in0=ls[:, 3:4], in1=ls[:, 2:3], op=ALU.subtract),
             reads=[b_ls], writes=[b_ls])
        T.op("dve", lambda: dve.tensor_scalar(out=ls[:, 1:2], in0=ls[:, 0:1], scalar1=-lambda_init, scalar2=None, op0=ALU.add),
             reads=[b_ls], writes=[b_ls])
        nlam, b_nlam = C.sb("nlam", [128, 1], F32)
        gsc, b_gsc = C.sb("gsc", [128, 1], F32)
        T.dma("sp", gsc[:], da_g[j, :].rearrange("(p o) -> p o", o=1), reads=[bIN], writes=[b_gsc], sb=b_gsc)
        T.op("dve", lambda: dve.tensor_scalar(out=gsc[:], in0=gsc[:], scalar1=(1.0 - lambda_init), scalar2=None, op0=ALU.mult),
             reads=[b_gsc], writes=[b_gsc])
        sc_r = C.rot("sc", 3, [128, 512], F32, psum=True)
        OA, b_OA = C.ps("OA", [128, 512])
        SA, b_SA = C.ps("SA", [128, 512])
        OB, b_OB = C.ps("OB", [128, 512])
        SB, b_SB = C.ps("SB", [128, 512])
        MS, b_MS = C.ps("MS", [128, 512])
        T.op("pe", lambda: pe.matmul(MS[:, 0:1], lhsT=ones_f[0:1, :], rhs=ls[0:1, 1:2], start=True, stop=True),
             reads=[b_ones_f, b_ls], writes=[b_MS])
        T.op("dve", lambda: dve.tensor_copy(out=nlam[:], in_=MS[:, 0:1]), reads=[b_MS], writes=[b_nlam])
        qt_r = C.rot("qT", 2, [128, S], BF16)
        kt_r = C.rot("kT", 2, [128, S], BF16)
        v_r = C.rot("vh", 2, [128, NB, 128], BF16)
        e_r = C.rot("e", 4, [128, 512], BF16)
        f_r = [C.rot("f%d" % i, 1, [128, 512], F32) for i in range(4)]
        ost_r = C.rot("ost", 2, [128, 512], BF16)

        def load(h):
            qT, b_qT = qt_r.next()
            kT, b_kT = kt_r.next()
            vh, b_vh = v_r.next()
            T.dma("sp", qT[:], QT[0, h * 128:(h + 1) * 128, :], reads=[bQT[0]], writes=[b_qT], sb=b_qT)
            T.dma("sp", kT[:], KT[0, h * 128:(h + 1) * 128, :], reads=[bKT[0]], writes=[b_kT], sb=b_kT)
            T.dma("sp", vh[:], VV[0, :, h * 128:(h + 1) * 128].rearrange("(c p) f -> p c f", p=128),
                  reads=[bVV[0]], writes=[b_vh], sb=b_vh)
            return qT, b_qT, kT, b_kT, vh, b_vh

        nxt = load(0)
        for h in range(8):
            qT, b_qT, kT, b_kT, vh, b_vh = nxt
            if h + 1 < 8:
                nxt = load(h + 1)
            for qt in range(NT):
                qs = slice(qt * 512, (qt + 1) * 512)
                for kc in (range(NB) if DA_LOOP else []):
                    ks = slice(kc * 128, (kc + 1) * 128)
                    first, last = kc == 0, kc == NB - 1
                    for m in range(2):
                        ps_ = slice(m * 64, (m + 1) * 64)
                        sc, b_sc = sc_r.next()
                        T.op("pe", lambda: pe.matmul(sc[:], lhsT=kT[ps_, ks], rhs=qT[ps_, qs], start=True, stop=True,
                                                     tile_position=(m * 64, 0)),
                             reads=[b_kT, b_qT], writes=[b_sc])
                        e, b_e = e_r.next()
                        T.op("act", lambda: act.activation(out=e[:], in_=sc[:], func=AF.Exp, bias=kb[:, kc:kc + 1], scale=0.125),
                             reads=[b_sc, b_kb], writes=[b_e])
                        O_, bO = (OA, b_OA) if m == 0 else (OB, b_OB)
                        S_, bS = (SA, b_SA) if m == 0 else (SB, b_SB)
                        T.op("pe", lambda: pe.matmul(O_[:], lhsT=vh[:, kc, :], rhs=e[:], start=first, stop=last),
                             reads=[b_vh, b_e], writes=[bO], sig=last)
                        T.op("pe", lambda: pe.matmul(S_[:], lhsT=ones_bf[:], rhs=e[:], start=first, stop=last),
                             reads=[b_ones_bf, b_e], writes=[bS], sig=True)
                if not DA_FIN:
                    continue
                (f0, b_f0), (f1, b_f1), (f2, b_f2), (f3, b_f3) = [r.next() for r in f_r]
                T.op("dve", lambda: dve.reciprocal(out=f0[:], in_=SA[:]), reads=[b_SA], writes=[b_f0])
                T.op("dve", lambda: dve.tensor_tensor(out=f1[:], in0=OA[:], in1=f0[:], op=ALU.mult), reads=[b_OA, b_f0], writes=[b_f1])
                T.op("dve", lambda: dve.reciprocal(out=f0[:], in_=SB[:]), reads=[b_SB], writes=[b_f0])
                T.op("dve", lambda: dve.tensor_tensor(out=f2[:], in0=OB[:], in1=f0[:], op=ALU.mult), reads=[b_OB, b_f0], writes=[b_f2])
                T.op("dve", lambda: dve.scalar_tensor_tensor(out=f3[:], in0=f2[:], scalar=nlam[:, 0:1], in1=f1[:],
                                                             op0=ALU.mult, op1=ALU.add),
                     reads=[b_f2, b_f1, b_nlam], writes=[b_f3])
                T.op("act", lambda: act.activation(out=f1[:], in_=f3[:], func=AF.Square), reads=[b_f3], writes=[b_f1])
                T.op("pe", lambda: pe.matmul(MS[:], lhsT=ones_f[:], rhs=f1[:], start=True, stop=True),
                     reads=[b_ones_f, b_f1], writes=[b_MS])
                T.op("dve", lambda: dve.tensor_scalar(out=f2[:], in0=MS[:], scalar1=1.0 / 128.0, scalar2=LN_EPS,
                                                      op0=ALU.mult, op1=ALU.add), reads=[b_MS], writes=[b_f2])
                T.op("act", lambda: act.activation(out=f2[:], in_=f2[:], func=AF.Ln), reads=[b_f2], writes=[b_f2])
                T.op("act", lambda: act.activation(out=f2[:], in_=f2[:], func=AF.Exp, scale=-0.5), reads=[b_f2], writes=[b_f2])
                T.op("dve", lambda: dve.tensor_tensor(out=f3[:], in0=f3[:], in1=f2[:], op=ALU.mult), reads=[b_f3, b_f2], writes=[b_f3])
                ost, b_ost = ost_r.next()
                DA_DBG = int(os.environ.get("DA_DBG", "0"))
                if DA_DBG == 2:
                    T.op("dve", lambda: dve.tensor_scalar(out=ost[:], in0=f3[:], scalar1=0.0, scalar2=nlam[:, 0:1], op0=ALU.mult, op1=ALU.add),
                         reads=[b_f3, b_nlam], writes=[b_ost])
                elif DA_DBG == 1:
                    T.op("dve", lambda: dve.reciprocal(out=f0[:], in_=SB[:]), reads=[b_SB], writes=[b_f0])
                    T.op("dve", lambda: dve.tensor_tensor(out=ost[:], in0=OB[:], in1=f0[:], op=ALU.mult), reads=[b_OB, b_f0], writes=[b_ost])
                else:
                    T.op("dve", lambda: dve.tensor_scalar(out=ost[:], in0=f3[:], scalar1=gsc[:, 0:1], scalar2=None, op0=ALU.mult),
                         reads=[b_f3, b_gsc], writes=[b_ost])
                T.dma("sp", OT[h * 128:(h + 1) * 128, qs], ost[:], reads=[b_ost], writes=[bOT], sb=b_ost)
        C.close()

    def dl_attention():
        C = Ctx(nc, T)
        kbs = []
        for g in range(3):
            t, b = C.sb("kbl%d" % g, [128, NB], F32)
            T.dma("sp", t[:], kbdl[g], reads=[bIN], writes=[b], sb=b)
            kbs.append((t, b))
        raw_r = C.rot("raw", 3, [128, 2048], BF16)
        Qs, b_Qs = C.sb("Qs", [128, S], BF16)
        Ks, b_Ks = C.sb("Ks", [128, S], BF16)
        vs_r = C.rot("Vs", 2, [128, NB, 128], BF16)
        acc, b_acc = C.sb("acc", [128, S], F32)
        acs, b_acs = C.sb("acs", [128, S], F32)
        oo, b_oo = C.sb("oo", [128, S], BF16)
        e_r = C.rot("e", 4, [128, 256], BF16)
        m_r = C.rot("m", 4, [128, 256], BF16)
        sc_r = C.rot("sc", 4, [128, 256], F32, psum=True)
        po_r = C.rot("po", 2, [128, 256], F32, psum=True)
        pq_r = C.rot("pq", 2, [128, 256], F32, psum=True)
        cnt = 0
        for hp in range(8):
            T.op("dve", lambda: dve.memset(acc[:], 0.0), writes=[b_acc])
            T.op("pool", lambda: pool.memset(acs[:], 0.0), writes=[b_acs])
            for g in range(3):
                d = DIL[g]
                M = S // d
                MC = M // 128
                for which, (dstT, b_dst, srcT, b_src) in enumerate(((Qs, b_Qs, QT, bQT[g]), (Ks, b_Ks, KT, bKT[g]))):
                    dview = dstT[:].rearrange("p (r m) -> p r m", r=d)
                    for pc in range(S // 2048):
                        raw, b_raw = raw_r.next()
                        T.dma("sp", raw[:], srcT[g, hp * 128:(hp + 1) * 128, pc * 2048:(pc + 1) * 2048],
                              reads=[b_src], writes=[b_raw], sb=b_raw)
                        ml = 2048 // d
                        en = "pool" if (cnt % 2 == 0) else "dve"
                        cnt += 1
                        eng = pool if en == "pool" else dve
                        T.op(en, lambda: eng.tensor_copy(out=dview[:, :, pc * ml:(pc + 1) * ml],
                                                         in_=raw[:].rearrange("p (m r) -> p r m", r=d)),
                             reads=[b_raw], writes=[b_dst])
                Vs, b_Vs = vs_r.next()
                vsv = Vs[:].rearrange("p (r c) f -> p r c f", r=d)
                vsrc = VV[g, :, hp * 128:(hp + 1) * 128].rearrange("(m r) f -> r m f", r=d)
                for r in range(d):
                    T.dma("sp", vsv[:, r, :, :], vsrc[r].rearrange("(c p) f -> p c f", p=128),
                          reads=[bVV[g]], writes=[b_Vs], sb=b_Vs)
                kbt, b_kbt = kbs[g]
                Qv = Qs[:].rearrange("p (r m) -> p r m", r=d)
                Kv = Ks[:].rearrange("p (r m) -> p r m", r=d)
                accv = acc[:].rearrange("p (m r) -> p r m", r=d)
                acsv = acs[:].rearrange("p (m r) -> p r m", r=d)
                for r in range(d):
                    for c in range(MC):
                        qlo = max(0, 128 * c - 64)
                        qhi = min(M, 128 * c + 192)
                        N = qhi - qlo
                        mo = qlo - (128 * c - 64)
                        ms_ = []
                        for m in range(2):
                            ps_ = slice(m * 64, (m + 1) * 64)
                            sc, b_sc = sc_r.next()
                            T.op("pe", lambda: pe.matmul(sc[:, 0:N], lhsT=Kv[ps_, r, c * 128:(c + 1) * 128], rhs=Qv[ps_, r, qlo:qhi],
                                                         start=True, stop=True, tile_position=(m * 64, 0)),
                                 reads=[b_Ks, b_Qs], writes=[b_sc])
                            e, b_e = e_r.next()
                            T.op("act", lambda: act.activation(out=e[:, 0:N], in_=sc[:, 0:N], func=AF.Exp,
                                                               bias=kbt[:, r * MC + c:r * MC + c + 1], scale=0.125),
                                 reads=[b_sc, b_kbt], writes=[b_e])
                            mm_, b_mm = m_r.next()
                            en = "pool" if m == 0 else "dve"
                            eng = pool if m == 0 else dve
                            T.op(en, lambda: eng.tensor_tensor(out=mm_[:, 0:N], in0=e[:, 0:N], in1=bmask[:, mo:mo + N], op=ALU.mult),
                                 reads=[b_e, b_bmask], writes=[b_mm])
                            ms_.append((mm_, b_mm))
                        po, b_po = po_r.next()
                        pq, b_pq = pq_r.next()
                        for m in range(2):
                            mm_, b_mm = ms_[m]
                            os_ = slice(m * 64, (m + 1) * 64)
                            T.op("pe", lambda: pe.matmul(po[os_, 0:N], lhsT=Vs[:, r * MC + c, m * 64:(m + 1) * 64], rhs=mm_[:, 0:N],
                                                         start=True, stop=True, tile_position=(0, m * 64)),
                                 reads=[b_Vs, b_mm], writes=[b_po], sig=(m == 1))
                        for m in range(2):
                            mm_, b_mm = ms_[m]
                            os_ = slice(m * 64, (m + 1) * 64)
                            T.op("pe", lambda: pe.matmul(pq[os_, 0:N], lhsT=ones_bf[:, 0:64], rhs=mm_[:, 0:N],
                                                         start=True, stop=True, tile_position=(0, m * 64)),
                                 reads=[b_ones_bf, b_mm], writes=[b_pq], sig=(m == 1))
                        T.op("dve", lambda: dve.tensor_tensor(out=accv[:, r, qlo:qhi], in0=accv[:, r, qlo:qhi], in1=po[:, 0:N], op=ALU.add),
                             reads=[b_po], writes=[b_acc])
                        T.op("dve", lambda: dve.tensor_tensor(out=acsv[:, r, qlo:qhi], in0=acsv[:, r, qlo:qhi], in1=pq[:, 0:N], op=ALU.add),
                             reads=[b_pq], writes=[b_acs])
            T.op("dve", lambda: dve.tensor_scalar(out=acs[:], in0=acs[:], scalar1=1e-30, scalar2=None, op0=ALU.max),
                 reads=[b_acs], writes=[b_acs])
            T.op("dve", lambda: dve.reciprocal(out=acs[:], in_=acs[:]), reads=[b_acs], writes=[b_acs])
            T.op("dve", lambda: dve.tensor_tensor(out=oo[:], in0=acc[:], in1=acs[:], op=ALU.mult), reads=[b_acc, b_acs], writes=[b_oo])
            T.dma("sp", OT[hp * 128:(hp + 1) * 128, :], oo[:], reads=[b_oo], writes=[bOT], sb=b_oo)
        C.close()

    def tok_phase(i, x_src, b_xsrc, wo_ap, x_dst, b_xdst):
        C = Ctx(nc, T)
        wo, b_wo = C.sb("wo", [128, 8, D], BF16)
        T.dma("pool", wo[:], wo_ap.rearrange("(c p) f -> p c f", p=128), reads=[bIN], writes=[b_wo], sb=b_wo)
        wr, b_wr = C.sb("wr", [128, 8, 20], BF16)
        T.dma("pool", wr[:], w_r[i].rearrange("(c p) f -> p c f", p=128), reads=[bIN], writes=[b_wr], sb=b_wr)
        lnb = []
        for k in range(4):
            t, b = C.sb("lnb%d" % k, [128, D], F32)
            T.dma("sp", t[:], lnp[i, k, :].partition_broadcast(128), reads=[bIN], writes=[b], sb=b)
            lnb.append((t, b))
        oT_r = C.rot("oT", 2, [128, 8, 512], BF16)
        xt_r = C.rot("xt", 1, [128, 4, D], F32)
        ya, b_ya = C.sb("ya", [128, 4, D], F32)
        tmp, b_tmp = C.sb("tmp", [128, D], F32)
        tmp2, b_tmp2 = C.sb("tmp2", [128, D], F32)
        x1b, b_x1b = C.sb("x1b", [128, 4, D], BF16)
        x1T, b_x1T = C.sb("x1T", [128, 8, 512], BF16)
        st_, b_st = C.sb("stt", [128, 8], F32)
        rt, b_rt = C.sb("rt", [128, 64], F32)
        cw, b_cw = C.sb("cw", [128, 4, 16], F32)
        cwT, b_cwT = C.sb("cwT", [16, 512], F32)
        cwb_r = C.rot("cwb", 2, [128, 512], F32)
        wg_r = C.rot("wg", 2, [128, 8, 512], BF16)
        wu_r = C.rot("wu", 2, [128, 8, 512], BF16)
        wd_r = C.rot("wd", 2, [128, 4, D], BF16)
        sg_r = C.rot("sg", 2, [128, 512], F32)
        tt_r = C.rot("tt", 2, [128, 512], F32)
        aT_r = C.rot("aT", 2, [128, 4, 512], BF16)
        xo_r = C.rot("xo", 2, [128, D], F32)
        ph_r = C.rot("ph", 2, [128, 512], F32, psum=True)
        tp_r = C.rot("tp", 1, [128, 512], BF16, psum=True)
        pg_r = C.rot("pg", 2, [128, 512], F32, psum=True)
        pu_r = C.rot("pu", 2, [128, 512], F32, psum=True)
        px, b_px = C.ps("px", [128, 512])

        def layer_norm(src_ap, b_src, gk, out_ap, b_out, extra_writes=()):
            g_, bg = lnb[gk]
            be_, bbe = lnb[gk + 1]
            T.op("dve", lambda: dve.tensor_reduce(out=st_[:, 0:1], in_=src_ap, axis=AX.X, op=ALU.add), reads=[b_src], writes=[b_st])
            T.op("pool", lambda: pool.tensor_tensor(out=tmp2[:], in0=src_ap, in1=src_ap, op=ALU.mult), reads=[b_src], writes=[b_tmp2])
            T.op("dve", lambda: dve.tensor_reduce(out=st_[:, 1:2], in_=tmp2[:], axis=AX.X, op=ALU.add), reads=[b_tmp2], writes=[b_st])
            T.op("dve", lambda: dve.tensor_scalar(out=st_[:, 2:4], in0=st_[:, 0:2], scalar1=1.0 / D, scalar2=None, op0=ALU.mult),
                 reads=[b_st], writes=[b_st])
            T.op("dve", lambda: dve.tensor_tensor(out=st_[:, 4:5], in0=st_[:, 2:3], in1=st_[:, 2:3], op=ALU.mult), reads=[b_st], writes=[b_st])
            T.op("dve", lambda: dve.tensor_tensor(out=st_[:, 5:6], in0=st_[:, 3:4], in1=st_[:, 4:5], op=ALU.subtract), reads=[b_st], writes=[b_st])
            T.op("dve", lambda: dve.tensor_scalar(out=st_[:, 6:7], in0=st_[:, 5:6], scalar1=LN_EPS, scalar2=None, op0=ALU.add),
                 reads=[b_st], writes=[b_st])
            T.op("act", lambda: act.activation(out=st_[:, 7:8], in_=st_[:, 6:7], func=AF.Ln), reads=[b_st], writes=[b_st])
            T.op("act", lambda: act.activation(out=st_[:, 6:7], in_=st_[:, 7:8], func=AF.Exp, scale=-0.5), reads=[b_st], writes=[b_st])
            T.op("dve", lambda: dve.tensor_scalar(out=tmp[:], in0=src_ap, scalar1=st_[:, 2:3], scalar2=st_[:, 6:7],
                                                  op0=ALU.subtract, op1=ALU.mult), reads=[b_src, b_st], writes=[b_tmp])
            T.op("pool", lambda: pool.tensor_tensor(out=tmp[:], in0=tmp[:], in1=g_[:], op=ALU.mult), reads=[b_tmp, bg], writes=[b_tmp])
            T.op("dve", lambda: dve.tensor_tensor(out=out_ap, in0=tmp[:], in1=be_[:], op=ALU.add), reads=[b_tmp, bbe],
                 writes=[b_out] + list(extra_writes))

        def load(ti):
            oT, b_oT = oT_r.next()
            T.dma("sp", oT[:], OT[:, ti * 512:(ti + 1) * 512].rearrange("(c p) t -> p c t", p=128), reads=[bOT], writes=[b_oT], sb=b_oT)
            return oT, b_oT

        def load_x(ti):
            xt, b_xt = xt_r.next()
            T.dma("sp", xt[:], x_src[ti * 512:(ti + 1) * 512, :].rearrange("(s p) d -> p s d", p=128),
                  reads=[b_xsrc], writes=[b_xt], sb=b_xt)
            return xt, b_xt

        def load_w(e):
            wg, b_wg = wg_r.next()
            wu, b_wu = wu_r.next()
            wd, b_wd = wd_r.next()
            T.dma("pool", wg[:], w_gate[i, e].rearrange("(c p) f -> p c f", p=128), reads=[bIN], writes=[b_wg], sb=b_wg)
            T.dma("pool", wu[:], w_up[i, e].rearrange("(c p) f -> p c f", p=128), reads=[bIN], writes=[b_wu], sb=b_wu)
            T.dma("pool", wd[:], w_down[i, e].rearrange("(c p) f -> p c f", p=128), reads=[bIN], writes=[b_wd], sb=b_wd)
            return wg, b_wg, wu, b_wu, wd, b_wd

        nxt = load(0)
        nw = load_w(0)
        for ti in range(NT):
            oT, b_oT = nxt
            xt, b_xt = load_x(ti)
            if ti + 1 < NT:
                nxt = load(ti + 1)
            for s in range(4):
                for hf in range(2):
                    ph, b_ph = ph_r.next()
                    for fc in range(8):
                        T.op("pe", lambda: pe.matmul(ph[:], lhsT=oT[:, fc, s * 128:(s + 1) * 128], rhs=wo[:, fc, hf * 512:(hf + 1) * 512],
                                                     start=(fc == 0), stop=(fc == 7)),
                             reads=[b_oT, b_wo], writes=[b_ph], sig=(fc == 7))
                    T.op("dve", lambda: dve.scalar_tensor_tensor(out=ya[:, s, hf * 512:(hf + 1) * 512], in0=xt[:, s, hf * 512:(hf + 1) * 512],
                                                                 scalar=ALPHA, in1=ph[:], op0=ALU.mult, op1=ALU.add),
                         reads=[b_xt, b_ph], writes=[b_ya])
                layer_norm(ya[:, s, :], b_ya, 0, ya[:, s, :], b_ya)
                T.op("act", lambda: act.copy(out=x1b[:, s, :], in_=ya[:, s, :]), reads=[b_ya], writes=[b_x1b])
            TOK_DBG = int(os.environ.get("TOK_DBG", "0"))
            if TOK_DBG == 1:
                for s in range(4):
                    T.dma("sp", x_dst[ti * 512 + s * 128:ti * 512 + (s + 1) * 128, :], ya[:, s, :], reads=[b_ya], writes=[b_xdst], sb=b_ya)
                continue
            for kc in range(8):
                tp, b_tp = tp_r.next()
                for s in range(4):
                    T.op("pe", lambda: pe.transpose(out=tp[:, s * 128:(s + 1) * 128], in_=x1b[:, s, kc * 128:(kc + 1) * 128],
                                                    identity=ident_bf[:]),
                         reads=[b_x1b, b_ident_bf], writes=[b_tp], sig=(s == 3))
                T.op("act", lambda: act.copy(out=x1T[:, kc, :], in_=tp[:]), reads=[b_tp], writes=[b_x1T])
            for s in range(4):
                for kc in range(8):
                    T.op("pe", lambda: pe.matmul(px[:, 0:20], lhsT=x1T[:, kc, s * 128:(s + 1) * 128], rhs=wr[:, kc, :],
                                                 start=(kc == 0), stop=(kc == 7)),
                         reads=[b_x1T, b_wr], writes=[b_px], sig=(kc == 7))
                R = lambda a, b: rt[:, a:b]
                V = lambda fn: T.op("dve", fn, reads=[b_rt], writes=[b_rt])
                T.op("dve", lambda: dve.tensor_copy(out=R(0, 20), in_=px[:, 0:20]), reads=[b_px], writes=[b_rt])
                V(lambda: dve.tensor_reduce(out=R(20, 21), in_=R(0, 4), axis=AX.X, op=ALU.max))
                V(lambda: dve.tensor_scalar(out=R(21, 25), in0=R(0, 4), scalar1=R(20, 21), scalar2=None, op0=ALU.is_equal))
                V(lambda: dve.tensor_scalar(out=R(25, 26), in0=R(20, 21), scalar1=-1.0, scalar2=None, op0=ALU.mult))
                T.op("act", lambda: act.activation(out=R(26, 30), in_=R(0, 4), func=AF.Exp, bias=R(25, 26), scale=1.0),
                     reads=[b_rt], writes=[b_rt])
                V(lambda: dve.tensor_reduce(out=R(30, 31), in_=R(26, 30), axis=AX.X, op=ALU.add))
                V(lambda: dve.reciprocal(out=R(31, 32), in_=R(30, 31)))
                V(lambda: dve.tensor_scalar(out=R(32, 36), in0=R(4, 8), scalar1=R(21, 22), scalar2=None, op0=ALU.mult))
                for gg in range(1, 4):
                    V(lambda: dve.scalar_tensor_tensor(out=R(32, 36), in0=R(4 + 4 * gg, 8 + 4 * gg), scalar=R(21 + gg, 22 + gg),
                                                       in1=R(32, 36), op0=ALU.mult, op1=ALU.add))
                V(lambda: dve.tensor_reduce(out=R(36, 37), in_=R(32, 36), axis=AX.X, op=ALU.max))
                V(lambda: dve.tensor_scalar(out=R(37, 41), in0=R(32, 36), scalar1=R(36, 37), scalar2=None, op0=ALU.is_equal))
                V(lambda: dve.scalar_tensor_tensor(out=R(41, 45), in0=R(37, 41), scalar=-1e30, in1=R(32, 36), op0=ALU.mult, op1=ALU.add))
                V(lambda: dve.tensor_reduce(out=R(45, 46), in_=R(41, 45), axis=AX.X, op=ALU.max))
                V(lambda: dve.tensor_scalar(out=R(46, 50), in0=R(41, 45), scalar1=R(45, 46), scalar2=None, op0=ALU.is_equal))
                V(lambda: dve.tensor_tensor(out=R(50, 51), in0=R(45, 46), in1=R(36, 37), op=ALU.subtract))
                T.op("act", lambda: act.activation(out=R(51, 52), in_=R(50, 51), func=AF.Exp), reads=[b_rt], writes=[b_rt])
                V(lambda: dve.tensor_scalar(out=R(52, 53), in0=R(51, 52), scalar1=1.0, scalar2=None, op0=ALU.add))
                V(lambda: dve.reciprocal(out=R(53, 54), in_=R(52, 53)))
                V(lambda: dve.tensor_tensor(out=R(54, 55), in0=R(53, 54), in1=R(31, 32), op=ALU.mult))
                V(lambda: dve.tensor_tensor(out=R(55, 56), in0=R(31, 32), in1=R(54, 55), op=ALU.subtract))
                V(lambda: dve.tensor_scalar(out=R(56, 60), in0=R(37, 41), scalar1=R(54, 55), scalar2=None, op0=ALU.mult))
                V(lambda: dve.scalar_tensor_tensor(out=R(56, 60), in0=R(46, 50), scalar=R(55, 56), in1=R(56, 60), op0=ALU.mult, op1=ALU.add))
                for gg in range(4):
                    T.op("dve", lambda: dve.tensor_scalar(out=cw[:, s, gg * 4:(gg + 1) * 4], in0=R(56, 60), scalar1=R(21 + gg, 22 + gg),
                                                          scalar2=None, op0=ALU.mult), reads=[b_rt], writes=[b_cw])
            for s in range(4):
                T.op("pe", lambda: pe.transpose(out=px[0:16, s * 128:(s + 1) * 128], in_=cw[:, s, :], identity=ident_f[:]),
                     reads=[b_cw, b_ident_f], writes=[b_px], sig=(s == 3))
            T.op("dve", lambda: dve.tensor_copy(out=cwT[:], in_=px[0:16, :]), reads=[b_px], writes=[b_cwT])
            T.op("pool", lambda: pool.tensor_scalar(out=ya[:], in0=ya[:], scalar1=ALPHA, scalar2=None, op0=ALU.mult),
                 reads=[b_ya], writes=[b_ya])
            for e in range(16):
                wg, b_wg, wu, b_wu, wd, b_wd = nw
                if not (ti == NT - 1 and e == 15):
                    nw = load_w((e + 1) % 16)
                T.op("pe", lambda: pe.matmul(px[:], lhsT=sel[0:16, e * 128:(e + 1) * 128], rhs=cwT[0:16, :], start=True, stop=True),
                     reads=[b_sel, b_cwT], writes=[b_px])
                cwb, b_cwb = cwb_r.next()
                T.op("act", lambda: act.copy(out=cwb[:], in_=px[:]), reads=[b_px], writes=[b_cwb])
                aT, b_aT = aT_r.next()
                for fcn in range(4):
                    pg, b_pg = pg_r.next()
                    pu, b_pu = pu_r.next()
                    for kc in range(8):
                        T.op("pe", lambda: pe.matmul(pg[:], lhsT=wg[:, kc, fcn * 128:(fcn + 1) * 128], rhs=x1T[:, kc, :],
                                                     start=(kc == 0), stop=(kc == 7)), reads=[b_wg, b_x1T], writes=[b_pg], sig=(kc == 7))
                    for kc in range(8):
                        T.op("pe", lambda: pe.matmul(pu[:], lhsT=wu[:, kc, fcn * 128:(fcn + 1) * 128], rhs=x1T[:, kc, :],
                                                     start=(kc == 0), stop=(kc == 7)), reads=[b_wu, b_x1T], writes=[b_pu], sig=(kc == 7))
                    sg, b_sg = sg_r.next()
                    tt, b_tt = tt_r.next()
                    T.op("act", lambda: act.activation(out=sg[:], in_=pg[:], func=AF.Silu), reads=[b_pg], writes=[b_sg])
                    T.op("dve", lambda: dve.tensor_tensor(out=tt[:], in0=pu[:], in1=cwb[:], op=ALU.mult), reads=[b_pu, b_cwb], writes=[b_tt])
                    T.op("pool", lambda: pool.tensor_tensor(out=aT[:, fcn, :], in0=sg[:], in1=tt[:], op=ALU.mult),
                         reads=[b_sg, b_tt], writes=[b_aT])
                for s in range(4):
                    for hf in range(2):
                        ph, b_ph = ph_r.next()
                        for fcn in range(4):
                            T.op("pe", lambda: pe.matmul(ph[:], lhsT=aT[:, fcn, s * 128:(s + 1) * 128], rhs=wd[:, fcn, hf * 512:(hf + 1) * 512],
                                                         start=(fcn == 0), stop=(fcn == 3)), reads=[b_aT, b_wd], writes=[b_ph], sig=(fcn == 3))
                        T.op("dve", lambda: dve.tensor_tensor(out=ya[:, s, hf * 512:(hf + 1) * 512], in0=ya[:, s, hf * 512:(hf + 1) * 512],
                                                              in1=ph[:], op=ALU.add), reads=[b_ph], writes=[b_ya])
            if TOK_DBG == 2:
                for s in range(4):
                    T.dma("sp", x_dst[ti * 512 + s * 128:ti * 512 + (s + 1) * 128, :], ya[:, s, :], reads=[b_ya], writes=[b_xdst], sb=b_ya)
                continue
            for s in range(4):
                xo, b_xo = xo_r.next()
                layer_norm(ya[:, s, :], b_ya, 2, xo[:], b_xo)
                T.dma("sp", x_dst[ti * 512 + s * 128:ti * 512 + (s + 1) * 128, :], xo[:], reads=[b_xo], writes=[b_xdst], sb=b_xo)
        C.close()

    cur, b_cur = x_in, bIN
    for i in layers:
        j = i // 2
        last = (i == layers[-1])
        dst, b_dst = (y_out, Buf("yout")) if last else (XR[i % 2], bXR[i % 2])
        if i % 2 == 0:
            lambda_init = 0.8 - 0.6 * math.exp(-0.3 * i)
            proj_pass(cur, b_cur, da_wm[j], da_wp[j], 0)
            if KSTOP >= 2:
                da_attention(j, lambda_init)
            if KSTOP >= 3:
                tok_phase(i, cur, b_cur, da_wo[j], dst, b_dst)
        else:
            for g in range(3):
                proj_pass(cur, b_cur, dl_wm[j, g], dl_wp[j, g], g)
            dl_attention()
            tok_phase(i, cur, b_cur, dl_wo[j], dst, b_dst)
        cur, b_cur = dst, b_dst
    T.barrier()
    G.es.close()
    return nc, T


def _rope_tables(S):
    half = 8
    inv = (ROPE_THETA ** (-np.arange(half, dtype=np.float32) * np.float32(2.0 / 16))).astype(np.float32)
    ang = np.arange(S, dtype=np.float32)[:, None] * inv[None, :]
    cos = np.cos(ang).astype(np.float32).T
    sin = np.sin(ang).astype(np.float32).T
    C = np.ones((128, S), np.float32)
    Sg = np.zeros((128, S), np.float32)
    for hh in range(2):
        b = hh * 64
        C[b:b + 8] = cos
        C[b + 8:b + 16] = cos
        Sg[b:b + 8] = -sin
        Sg[b + 8:b + 16] = sin
    return C, Sg


def _partner_perm(ncols):
    idx = np.arange(ncols)
    i = idx % 64
    p = idx.copy()
    p[i < 8] = idx[i < 8] + 8
    m = (i >= 8) & (i < 16)
    p[m] = idx[m] - 8
    return p


def prep_shared(inp, S):
    f = lambda a: np.ascontiguousarray(np.asarray(a, dtype=np.float32))
    da_w_in = f(inp["da_w_in"])
    hidx = np.arange(8)[:, None, None]
    midx = np.arange(2)[None, :, None]
    didx = np.arange(64)[None, None, :]
    qcols = (midx * 512 + hidx * 64 + didx).reshape(-1)
    kcols = qcols + 1024
    pp = _partner_perm(1024)
    da_wm = np.concatenate([da_w_in[:, :, qcols], da_w_in[:, :, kcols], da_w_in[:, :, 2048:]], axis=2)
    da_wp = np.concatenate([da_w_in[:, :, qcols[pp]], da_w_in[:, :, kcols[pp]]], axis=2)
    dl = f(inp["dl_w_in"]).reshape(2, D, 3, 3, 1024)
    dl_wm = np.ascontiguousarray(dl.transpose(0, 2, 1, 3, 4).reshape(2, 3, D, 3072))
    dlq = dl[:, :, :, 0, :][..., pp]
    dlk = dl[:, :, :, 1, :][..., pp]
    dl_wp = np.ascontiguousarray(np.concatenate([dlq, dlk], axis=-1).transpose(0, 2, 1, 3))
    C, Sg = _rope_tables(S)
    ii = np.arange(128)[:, None]
    jj = np.arange(256)[None, :]
    bmask = ((ii >= jj - 128) & (ii <= jj)).astype(np.float32)
    sel = np.zeros((16, 16, 128), np.float32)
    for e in range(16):
        sel[e, e, :] = 1.0
    sh = {
        "ropec": C, "ropes": Sg, "ident": np.eye(128, dtype=np.float32), "bmask": bmask,
        "sel": sel.reshape(16, 16 * 128),
        "da_wm": np.ascontiguousarray(da_wm), "da_wp": np.ascontiguousarray(da_wp), "da_wo": f(inp["da_w_out"]),
        "da_lam": np.ascontiguousarray(np.stack([f(inp["da_lambda_q1"]), f(inp["da_lambda_k1"]),
                                                 f(inp["da_lambda_q2"]), f(inp["da_lambda_k2"])], axis=1)),
        "da_g": f(inp["da_subln_g"]),
        "dl_wm": dl_wm, "dl_wp": dl_wp, "dl_wo": f(inp["dl_w_out"]),
        "lnp": np.ascontiguousarray(np.stack([f(inp["ln1_g"]), f(inp["ln1_b"]), f(inp["ln2_g"]), f(inp["ln2_b"])], axis=1)),
        "w_r": np.ascontiguousarray(np.concatenate([f(inp["moe_router_group"]),
                                                    f(inp["moe_router_expert"]).reshape(4, D, 16)], axis=2)),
        "w_gate": f(inp["moe_w_gate"]).reshape(4, 16, D, 512),
        "w_up": f(inp["moe_w_up"]).reshape(4, 16, D, 512),
        "w_down": f(inp["moe_w_down"]).reshape(4, 16, 512, D),
    }
    return sh


def prep_core(xseq, S):
    L = xseq.shape[0]
    xp = np.zeros((S, D), np.float32)
    xp[:L] = xseq
    kb = np.zeros((S,), np.float32)
    kb[L:] = NEG
    NB = S // 128
    m = {"x": xp, "kbda": np.ascontiguousarray(kb.reshape(NB, 128).T)}
    for g, d in enumerate(DIL):
        M = S // d
        MC = M // 128
        t = kb.reshape(MC, 128, d)
        m["kbdl%d" % g] = np.ascontiguousarray(t.transpose(1, 2, 0).reshape(128, d * MC))
    return m


_CACHE = {}


def kernel(**inputs):
    S = 8192
    xp = np.asarray(inputs["x_prompt"], dtype=np.float32)
    xs = np.asarray(inputs["x_sample"], dtype=np.float32)
    sh = prep_shared(inputs, S)
    seqs = [xp[b] for b in range(4)] + [xs[b] for b in range(4)]
    in_maps = []
    for c in range(8):
        m = dict(sh)
        m.update(prep_core(seqs[c], S))
        in_maps.append(m)
    if "nc" not in _CACHE:
        _CACHE["nc"] = build_program(S, [0, 1, 2, 3])[0]
    nc = _CACHE["nc"]
    res = run_bass_kernel_spmd(nc, in_maps, core_ids=list(range(8)))
    ys = [np.asarray(r["y"], dtype=np.float32) for r in res.results]
    y_prompt = np.stack([ys[b] for b in range(4)], axis=0)
    y_sample = np.stack([ys[4 + b][:xs.shape[1]] for b in range(4)], axis=0)
    return (y_prompt, y_sample)
```

```python
import math
from contextlib import ExitStack
import numpy as np
import concourse.bass as bass
import concourse.mybir as mybir
from concourse.bass_utils import run_bass_kernel_spmd

F32 = mybir.dt.float32
BF16 = mybir.dt.bfloat16
AF = mybir.ActivationFunctionType
ALU = mybir.AluOpType
AX = mybir.AxisListType

D = 1024
DEPTH = 4
LN_EPS = 1e-5
ALPHA = (2.0 * DEPTH) ** 0.25
ROPE_THETA = 500000.0
DIL = (1, 4, 16)
NEG = -30000.0
import os
KSTOP = int(os.environ.get('KSTOP', '3'))
DA_FIN = int(os.environ.get('DA_FIN', '1'))
DA_LOOP = int(os.environ.get('DA_LOOP', '1'))


class Buf:
    __slots__ = ("name", "w", "r", "ds")

    def __init__(self, name):
        self.name = name
        self.w = {}
        self.r = {}
        self.ds = None


class Trk:
    def __init__(self, nc):
        self.nc = nc
        self.eng = {"pe": nc.tensor, "act": nc.scalar, "dve": nc.vector, "pool": nc.gpsimd, "sp": nc.sync}
        self.esem = {k: nc.alloc_semaphore(name="e_" + k) for k in ("pe", "act", "dve", "pool")}
        self.ecnt = {k: 0 for k in self.esem}
        self.pending = {k: [] for k in self.esem}
        self.waited = {e: {} for e in self.eng}
        self.dsems = []
        self.free_ds = []
        self.ninstr = 0

    def get_ds(self):
        if self.free_ds:
            return self.free_ds.pop()
        h = self.nc.alloc_semaphore(name="d%d" % len(self.dsems))
        self.dsems.append([h, 0])
        return len(self.dsems) - 1

    def release_ds(self, bufs):
        for b in bufs:
            if b.ds is not None:
                self.free_ds.append(b.ds)
                b.ds = None

    def _semval(self, key, tok):
        if key[0] == "E":
            assert tok[1] is not None, "dependency on unsignaled instr of %s" % key[1]
            return self.esem[key[1]], tok[1]
        h, tot = self.dsems[key[1]]
        return h, tot

    def _waits(self, e, deps, skip_same, raw=None):
        for key, tok in deps.items():
            if skip_same and key == ("E", e):
                if e == "pe" or raw is None or key not in raw:
                    continue
                tok = raw[key]
            h, val = self._semval(key, tok)
            if self.waited[e].get(key, 0) >= val:
                continue
            self.eng[e].wait_ge(h, val)
            self.waited[e][key] = val
            self.ninstr += 1

    @staticmethod
    def _collect(reads, writes):
        deps = {}

        def add(k, t):
            o = deps.get(k)
            if o is None or t[1] is None or (o[1] is not None and t[1] > o[1]):
                deps[k] = t

        for b in reads:
            for k, t in b.w.items():
                add(k, t)
        for b in writes:
            for k, t in b.w.items():
                add(k, t)
            for k, t in b.r.items():
                add(k, t)
        return deps

    @staticmethod
    def _record(key, tok, reads, writes):
        for b in reads:
            o = b.r.get(key)
            if o is None or tok[1] is None or (o[1] is not None and tok[1] >= o[1]):
                b.r[key] = tok
        for b in writes:
            b.w = {key: tok}
            b.r = {}

    def op(self, e, fn, reads=(), writes=(), sig=True):
        deps = self._collect(reads, writes)
        raw = self._collect(reads, ())
        self._waits(e, deps, True, raw)
        ins = fn()
        self.ninstr += 1
        key = ("E", e)
        tok = [key, None]
        if sig:
            self.ecnt[e] += 1
            ins.then_inc(self.esem[e], 1)
            tok[1] = self.ecnt[e]
            for t in self.pending[e]:
                t[1] = self.ecnt[e]
            self.pending[e] = []
        else:
            self.pending[e].append(tok)
        self._record(key, tok, reads, writes)
        return ins

    def dma(self, q, out, in_, reads=(), writes=(), sb=None):
        deps = self._collect(reads, writes)
        self._waits(q, deps, False)
        if sb.ds is None:
            sb.ds = self.get_ds()
        ins = self.eng[q].dma_start(out=out, in_=in_)
        self.ninstr += 1
        d = self.dsems[sb.ds]
        d[1] += 16
        ins.then_inc(d[0], 16)
        key = ("D", sb.ds)
        tok = [key, d[1]]
        self._record(key, tok, reads, writes)
        return ins

    def barrier(self):
        for e in self.eng:
            assert not self.pending.get(e, [])
        for e in self.eng:
            for k in self.esem:
                if k == e:
                    continue
                key = ("E", k)
                val = self.ecnt[k]
                if val > 0 and self.waited[e].get(key, 0) < val:
                    self.eng[e].wait_ge(self.esem[k], val)
                    self.waited[e][key] = val
            for i, (h, tot) in enumerate(self.dsems):
                key = ("D", i)
                if tot > 0 and self.waited[e].get(key, 0) < tot:
                    self.eng[e].wait_ge(h, tot)
                    self.waited[e][key] = tot


class Rot:
    def __init__(self, items):
        self.items = items
        self.i = 0

    def next(self):
        it = self.items[self.i % len(self.items)]
        self.i += 1
        return it


class Ctx:
    def __init__(self, nc, trk):
        self.nc = nc
        self.trk = trk
        self.es = ExitStack()
        self.bufs = []

    _uid = [0]

    def sb(self, name, shape, dt):
        Ctx._uid[0] += 1
        name = "s%d_%s" % (Ctx._uid[0], name)
        t = self.es.enter_context(self.nc.sbuf_tensor(name, list(shape), dt))
        b = Buf(name)
        self.bufs.append(b)
        return t, b

    def ps(self, name, shape, dt=F32):
        Ctx._uid[0] += 1
        name = "p%d_%s" % (Ctx._uid[0], name)
        t = self.es.enter_context(self.nc.psum_tensor(name, list(shape), dt))
        b = Buf(name)
        self.bufs.append(b)
        return t, b

    def rot(self, name, n, shape, dt, psum=False):
        f = self.ps if psum else self.sb
        return Rot([f("%s%d" % (name, i), shape, dt) for i in range(n)])

    def close(self):
        self.trk.barrier()
        self.trk.release_ds(self.bufs)
        self.es.close()


def build_program(S, layers, nsub_tok=4):
    assert S % 2048 == 0
    NT = S // 512
    NB = S // 128
    nc = bass.Bass("TRN2", target_bir_lowering=False)
    dt_in = lambda name, shape: nc.dram_tensor(name, list(shape), F32, kind="ExternalInput").ap()
    x_in = dt_in("x", [S, D])
    kbda = dt_in("kbda", [128, NB])
    kbdl = [dt_in("kbdl%d" % g, [128, NB]) for g in range(3)]
    rc_in = dt_in("ropec", [128, S])
    rs_in = dt_in("ropes", [128, S])
    ident_in = dt_in("ident", [128, 128])
    mask_in = dt_in("bmask", [128, 256])
    sel_in = dt_in("sel", [16, 16 * 128])
    da_wm = dt_in("da_wm", [2, D, 3072])
    da_wp = dt_in("da_wp", [2, D, 2048])
    da_wo = dt_in("da_wo", [2, D, D])
    da_lam = dt_in("da_lam", [2, 4, 64])
    da_g = dt_in("da_g", [2, 128])
    dl_wm = dt_in("dl_wm", [2, 3, D, 3072])
    dl_wp = dt_in("dl_wp", [2, 3, D, 2048])
    dl_wo = dt_in("dl_wo", [2, D, D])
    lnp = dt_in("lnp", [4, 4, D])
    w_r = dt_in("w_r", [4, D, 20])
    w_gate = dt_in("w_gate", [4, 16, D, 512])
    w_up = dt_in("w_up", [4, 16, D, 512])
    w_down = dt_in("w_down", [4, 16, 512, D])
    y_out = nc.dram_tensor("y", [S, D], F32, kind="ExternalOutput").ap()
    KDBG = int(os.environ.get("KDBG", "0"))
    scr = lambda name, shape, dt: nc.dram_tensor(name, list(shape), dt, kind=("ExternalOutput" if KDBG else "Internal")).ap()
    QT = scr("QT", [3, D, S], BF16)
    KT = scr("KT", [3, D, S], BF16)
    VV = scr("VV", [3, S, D], BF16)
    OT = scr("OT", [D, S], BF16)
    XR = [scr("XR0", [S, D], F32), scr("XR1", [S, D], F32)]
    bQT = [Buf("QT%d" % g) for g in range(3)]
    bKT = [Buf("KT%d" % g) for g in range(3)]
    bVV = [Buf("VV%d" % g) for g in range(3)]
    bOT = Buf("OT")
    bXR = [Buf("XR0"), Buf("XR1")]
    bIN = Buf("inputs")

    T = Trk(nc)
    E = T.eng
    pe, act, dve, pool = nc.tensor, nc.scalar, nc.vector, nc.gpsimd

    G = Ctx(nc, T)
    ident_bf, b_ident_bf = G.sb("ident_bf", [128, 128], BF16)
    ident_f, b_ident_f = G.sb("ident_f", [128, 128], F32)
    ones_bf, b_ones_bf = G.sb("ones_bf", [128, 128], BF16)
    ones_f, b_ones_f = G.sb("ones_f", [128, 128], F32)
    bmask, b_bmask = G.sb("bmask", [128, 256], BF16)
    sel, b_sel = G.sb("sel", [16, 16 * 128], BF16)
    T.dma("pool", ident_bf[:], ident_in, reads=[bIN], writes=[b_ident_bf], sb=b_ident_bf)
    T.dma("sp", ident_f[:], ident_in, reads=[bIN], writes=[b_ident_f], sb=b_ident_f)
    T.dma("pool", bmask[:], mask_in, reads=[bIN], writes=[b_bmask], sb=b_bmask)
    T.dma("pool", sel[:], sel_in, reads=[bIN], writes=[b_sel], sb=b_sel)
    T.op("dve", lambda: dve.memset(ones_bf[:], 1.0), writes=[b_ones_bf])
    T.op("dve", lambda: dve.memset(ones_f[:], 1.0), writes=[b_ones_f])

    def proj_pass(x_src, b_xsrc, wm_ap, wp_ap, g):
        C = Ctx(nc, T)
        wm, b_wm = C.sb("wm", [128, 8, 3072], BF16)
        wp, b_wp = C.sb("wp", [128, 8, 2048], BF16)
        for kc in range(8):
            T.dma("pool", wm[:, kc, :], wm_ap[kc * 128:(kc + 1) * 128, :], reads=[bIN], writes=[b_wm], sb=b_wm)
            T.dma("pool", wp[:, kc, :], wp_ap[kc * 128:(kc + 1) * 128, :], reads=[bIN], writes=[b_wp], sb=b_wp)
        xt_r = C.rot("xt", 2, [128, 4, D], F32)
        xb_r = C.rot("xb", 1, [128, 4, D], BF16)
        xT_r = C.rot("xT", 2, [128, 8, 512], BF16)
        rc_r = C.rot("rc", 2, [128, 512], F32)
        rs_r = C.rot("rs", 2, [128, 512], F32)
        t1_r = C.rot("t1", 2, [128, 512], F32)
        t2_r = C.rot("t2", 2, [128, 512], F32)
        st_r = C.rot("st", 3, [128, 512], BF16)
        vst_r = C.rot("vst", 2, [128, 4, D], BF16)
        tp_r = C.rot("tp", 2, [128, 512], BF16, psum=True)
        pm_r = C.rot("pm", 2, [128, 512], F32, psum=True)
        pp_r = C.rot("pp", 2, [128, 512], F32, psum=True)
        pv_r = C.rot("pv", 2, [128, 512], F32, psum=True)

        def load(ti):
            xt, b_xt = xt_r.next()
            T.dma("sp", xt[:], x_src[ti * 512:(ti + 1) * 512, :].rearrange("(s p) d -> p s d", p=128),
                  reads=[b_xsrc], writes=[b_xt], sb=b_xt)
            rc, b_rc = rc_r.next()
            rs, b_rs = rs_r.next()
            T.dma("sp", rc[:], rc_in[:, ti * 512:(ti + 1) * 512], reads=[bIN], writes=[b_rc], sb=b_rc)
            T.dma("sp", rs[:], rs_in[:, ti * 512:(ti + 1) * 512], reads=[bIN], writes=[b_rs], sb=b_rs)
            return (xt, b_xt, rc, b_rc, rs, b_rs)

        nxt = load(0)
        for ti in range(NT):
            xt, b_xt, rc, b_rc, rs, b_rs = nxt
            if ti + 1 < NT:
                nxt = load(ti + 1)
            xb, b_xb = xb_r.next()
            for s in range(4):
                if s % 2 == 0:
                    T.op("act", lambda: act.copy(out=xb[:, s, :], in_=xt[:, s, :]), reads=[b_xt], writes=[b_xb])
                else:
                    T.op("dve", lambda: dve.tensor_copy(out=xb[:, s, :], in_=xt[:, s, :]), reads=[b_xt], writes=[b_xb])
            xT, b_xT = xT_r.next()
            for kc in range(8):
                tp, b_tp = tp_r.next()
                for s in range(4):
                    T.op("pe", lambda: pe.transpose(out=tp[:, s * 128:(s + 1) * 128], in_=xb[:, s, kc * 128:(kc + 1) * 128],
                                                    identity=ident_bf[:]),
                         reads=[b_xb, b_ident_bf], writes=[b_tp], sig=(s == 3))
                if kc % 2 == 0:
                    T.op("act", lambda: act.copy(out=xT[:, kc, :], in_=tp[:]), reads=[b_tp], writes=[b_xT])
                else:
                    T.op("dve", lambda: dve.tensor_copy(out=xT[:, kc, :], in_=tp[:]), reads=[b_tp], writes=[b_xT])
            for oc in range(16):
                pm, b_pm = pm_r.next()
                pp, b_pp = pp_r.next()
                for kc in range(8):
                    T.op("pe", lambda: pe.matmul(pm[:], lhsT=wm[:, kc, oc * 128:(oc + 1) * 128], rhs=xT[:, kc, :],
                                                 start=(kc == 0), stop=(kc == 7)),
                         reads=[b_wm, b_xT], writes=[b_pm], sig=(kc == 7))
                for kc in range(8):
                    T.op("pe", lambda: pe.matmul(pp[:], lhsT=wp[:, kc, oc * 128:(oc + 1) * 128], rhs=xT[:, kc, :],
                                                 start=(kc == 0), stop=(kc == 7)),
                         reads=[b_wp, b_xT], writes=[b_pp], sig=(kc == 7))
                t1, b_t1 = t1_r.next()
                t2, b_t2 = t2_r.next()
                st, b_st = st_r.next()
                T.op("dve", lambda: dve.tensor_tensor(out=t1[:], in0=pm[:], in1=rc[:], op=ALU.mult),
                     reads=[b_pm, b_rc], writes=[b_t1])
                T.op("dve", lambda: dve.tensor_tensor(out=t2[:], in0=pp[:], in1=rs[:], op=ALU.mult),
                     reads=[b_pp, b_rs], writes=[b_t2])
                T.op("pool", lambda: pool.tensor_tensor(out=st[:], in0=t1[:], in1=t2[:], op=ALU.add),
                     reads=[b_t1, b_t2], writes=[b_st])
                dst = QT if oc < 8 else KT
                bd = bQT[g] if oc < 8 else bKT[g]
                c8 = oc % 8
                T.dma("sp", dst[g, c8 * 128:(c8 + 1) * 128, ti * 512:(ti + 1) * 512], st[:], reads=[b_st], writes=[bd], sb=b_st)
            vst, b_vst = vst_r.next()
            for s in range(4):
                for hf in range(2):
                    pv, b_pv = pv_r.next()
                    for kc in range(8):
                        T.op("pe", lambda: pe.matmul(pv[:], lhsT=xT[:, kc, s * 128:(s + 1) * 128],
                                                     rhs=wm[:, kc, 2048 + hf * 512:2048 + (hf + 1) * 512],
                                                     start=(kc == 0), stop=(kc == 7)),
                             reads=[b_wm, b_xT], writes=[b_pv], sig=(kc == 7))
                    T.op("act", lambda: act.copy(out=vst[:, s, hf * 512:(hf + 1) * 512], in_=pv[:]), reads=[b_pv], writes=[b_vst])
            T.dma("sp", VV[g, ti * 512:(ti + 1) * 512, :].rearrange("(s p) d -> p s d", p=128), vst[:],
                  reads=[b_vst], writes=[bVV[g]], sb=b_vst)
        C.close()

    def da_attention(j, lambda_init):
        C = Ctx(nc, T)
        kb, b_kb = C.sb("kb", [128, NB], F32)
        T.dma("sp", kb[:], kbda, reads=[bIN], writes=[b_kb], sb=b_kb)
        lamv, b_lamv = C.sb("lamv", [1, 4, 64], F32)
        T.dma("sp", lamv[:], da_lam[j:j + 1, :, :], reads=[bIN], writes=[b_lamv], sb=b_lamv)
        lt, b_lt = C.sb("lt", [1, 2, 64], F32)
        ls, b_ls = C.sb("ls", [1, 4], F32)
        T.op("dve", lambda: dve.tensor_tensor(out=lt[:, 0, :], in0=lamv[:, 0, :], in1=lamv[:, 1, :], op=ALU.mult),
             reads=[b_lamv], writes=[b_lt])
        T.op("dve", lambda: dve.tensor_tensor(out=lt[:, 1, :], in0=lamv[:, 2, :], in1=lamv[:, 3, :], op=ALU.mult),
             reads=[b_lamv], writes=[b_lt])
        T.op("dve", lambda: dve.tensor_reduce(out=ls[:, 0:2], in_=lt[:], axis=AX.X, op=ALU.add), reads=[b_lt], writes=[b_ls])
        T.op("act", lambda: act.activation(out=ls[:, 2:4], in_=ls[:, 0:2], func=AF.Exp), reads=[b_ls], writes=[b_ls])
        T.op("dve", lambda: dve.tensor_tensor(out=ls[:, 0:1], in0=ls[:, 3:4], in1=ls[:, 2:3], op=ALU.subtract),
             reads=[b_ls], writes=[b_ls])
        T.op("dve", lambda: dve.tensor_scalar(out=ls[:, 1:2], in0=ls[:, 0:1], scalar1=-lambda_init, scalar2=None, op0=ALU.add),
             reads=[b_ls], writes=[b_ls])
        nlam, b_nlam = C.sb("nlam", [128, 1], F32)
        gsc, b_gsc = C.sb("gsc", [128, 1], F32)
        T.dma("sp", gsc[:], da_g[j, :].rearrange("(p o) -> p o", o=1), reads=[bIN], writes=[b_gsc], sb=b_gsc)
        T.op("dve", lambda: dve.tensor_scalar(out=gsc[:], in0=gsc[:], scalar1=(1.0 - lambda_init), scalar2=None, op0=ALU.mult),
             reads=[b_gsc], writes=[b_gsc])
        sc_r = C.rot("sc", 2, [128, 1024], F32, psum=True)
        OA, b_OA = C.ps("OA", [128, 512])
        OB, b_OB = C.ps("OB", [128, 512])
        SA, b_SA = C.ps("SA", [128, 512])
        SB, b_SB = C.ps("SB", [128, 512])
        MS, b_MS = SA, b_SA
        T.op("pe", lambda: pe.matmul(MS[:, 0:1], lhsT=ones_f[0:1, :], rhs=ls[0:1, 1:2], start=True, stop=True),
             reads=[b_ones_f, b_ls], writes=[b_MS])
        T.op("dve", lambda: dve.tensor_copy(out=nlam[:], in_=MS[:, 0:1]), reads=[b_MS], writes=[b_nlam])
        qt_r = C.rot("qT", 2, [128, S], BF16)
        kt_r = C.rot("kT", 2, [128, S], BF16)
        v_r = C.rot("vh", 2, [128, NB, 128], BF16)
        e_r = C.rot("e", 3, [128, 1024], BF16)
        sacc, b_sacc = C.sb("sacc", [128, 1024], F32)
        b_sacc2 = Buf("sacc2")
        f_r = [C.rot("f%d" % i, 1, [128, 512], F32) for i in range(4)]
        ost_r = C.rot("ost", 2, [128, 512], BF16)

        def load(h):
            qT, b_qT = qt_r.next()
            kT, b_kT = kt_r.next()
            vh, b_vh = v_r.next()
            T.dma("sp", qT[:], QT[0, h * 128:(h + 1) * 128, :], reads=[bQT[0]], writes=[b_qT], sb=b_qT)
            T.dma("sp", kT[:], KT[0, h * 128:(h + 1) * 128, :], reads=[bKT[0]], writes=[b_kT], sb=b_kT)
            T.dma("sp", vh[:], VV[0, :, h * 128:(h + 1) * 128].rearrange("(c p) f -> p c f", p=128),
                  reads=[bVV[0]], writes=[b_vh], sb=b_vh)
            return qT, b_qT, kT, b_kT, vh, b_vh

        nxt = load(0)
        for h in range(8):
            qT, b_qT, kT, b_kT, vh, b_vh = nxt
            if h + 1 < 8:
                nxt = load(h + 1)
            for qt in range(NT):
                qs = slice(qt * 512, (qt + 1) * 512)
                pend = None

                def stage2(kc, e, b_e):
                    first, last = kc == 0, kc == NB - 1
                    T.op("pe", lambda: pe.matmul(OA[:], lhsT=vh[:, kc, :], rhs=e[:, 0:512], start=first, stop=last),
                         reads=[b_vh, b_e], writes=[b_OA], sig=False)
                    T.op("pe", lambda: pe.matmul(OB[:], lhsT=vh[:, kc, :], rhs=e[:, 512:1024], start=first, stop=last),
                         reads=[b_vh, b_e], writes=[b_OB], sig=True)
                    if first:
                        T.op("dve", lambda: dve.tensor_copy(out=sacc[:, 0:512], in_=e[:, 0:512]), reads=[b_e], writes=[b_sacc])
                        T.op("pool", lambda: pool.tensor_copy(out=sacc[:, 512:1024], in_=e[:, 512:1024]), reads=[b_e], writes=[b_sacc2])
                    else:
                        T.op("dve", lambda: dve.tensor_tensor(out=sacc[:, 0:512], in0=sacc[:, 0:512], in1=e[:, 0:512], op=ALU.add),
                             reads=[b_e], writes=[b_sacc])
                        T.op("pool", lambda: pool.tensor_tensor(out=sacc[:, 512:1024], in0=sacc[:, 512:1024], in1=e[:, 512:1024], op=ALU.add),
                             reads=[b_e], writes=[b_sacc2])

                for kc in (range(NB) if DA_LOOP else []):
                    ks = slice(kc * 128, (kc + 1) * 128)
                    sc, b_sc = sc_r.next()
                    for m in range(2):
                        ps_ = slice(m * 64, (m + 1) * 64)
                        T.op("pe", lambda: pe.matmul(sc[:, m * 512:(m + 1) * 512], lhsT=kT[ps_, ks], rhs=qT[ps_, qs], start=True, stop=True,
                                                     tile_position=(m * 64, 0)),
                             reads=[b_kT, b_qT], writes=[b_sc], sig=(m == 1))
                    e, b_e = e_r.next()
                    T.op("act", lambda: act.activation(out=e[:], in_=sc[:], func=AF.Exp, bias=kb[:, kc:kc + 1], scale=0.125),
                         reads=[b_sc, b_kb], writes=[b_e])
                    if pend is not None:
                        stage2(*pend)
                    pend = (kc, e, b_e)
                if pend is not None:
                    stage2(*pend)
                if not DA_FIN:
                    continue
                (f0, b_f0), (f1, b_f1), (f2, b_f2), (f3, b_f3) = [r.next() for r in f_r]
                T.op("pe", lambda: pe.matmul(SA[:], lhsT=ones_f[:], rhs=sacc[:, 0:512], start=True, stop=True),
                     reads=[b_ones_f, b_sacc], writes=[b_SA])
                T.op("pe", lambda: pe.matmul(SB[:], lhsT=ones_f[:], rhs=sacc[:, 512:1024], start=True, stop=True),
                     reads=[b_ones_f, b_sacc2], writes=[b_SB])
                T.op("dve", lambda: dve.reciprocal(out=f0[:], in_=SA[:]), reads=[b_SA], writes=[b_f0])
                T.op("dve", lambda: dve.tensor_tensor(out=f1[:], in0=OA[:], in1=f0[:], op=ALU.mult), reads=[b_OA, b_f0], writes=[b_f1])
                T.op("dve", lambda: dve.reciprocal(out=f0[:], in_=SB[:]), reads=[b_SB], writes=[b_f0])
                T.op("dve", lambda: dve.tensor_tensor(out=f2[:], in0=OB[:], in1=f0[:], op=ALU.mult), reads=[b_OB, b_f0], writes=[b_f2])
                T.op("dve", lambda: dve.scalar_tensor_tensor(out=f3[:], in0=f2[:], scalar=nlam[:, 0:1], in1=f1[:],
                                                             op0=ALU.mult, op1=ALU.add),
                     reads=[b_f2, b_f1, b_nlam], writes=[b_f3])
                T.op("act", lambda: act.activation(out=f1[:], in_=f3[:], func=AF.Square), reads=[b_f3], writes=[b_f1])
                T.op("pe", lambda: pe.matmul(MS[:], lhsT=ones_f[:], rhs=f1[:], start=True, stop=True),
                     reads=[b_ones_f, b_f1], writes=[b_MS])
                T.op("dve", lambda: dve.tensor_scalar(out=f2[:], in0=MS[:], scalar1=1.0 / 128.0, scalar2=LN_EPS,
                                                      op0=ALU.mult, op1=ALU.add), reads=[b_MS], writes=[b_f2])
                T.op("act", lambda: act.activation(out=f2[:], in_=f2[:], func=AF.Ln), reads=[b_f2], writes=[b_f2])
                T.op("act", lambda: act.activation(out=f2[:], in_=f2[:], func=AF.Exp, scale=-0.5), reads=[b_f2], writes=[b_f2])
                T.op("dve", lambda: dve.tensor_tensor(out=f3[:], in0=f3[:], in1=f2[:], op=ALU.mult), reads=[b_f3, b_f2], writes=[b_f3])
                ost, b_ost = ost_r.next()
                T.op("dve", lambda: dve.tensor_scalar(out=ost[:], in0=f3[:], scalar1=gsc[:, 0:1], scalar2=None, op0=ALU.mult),
                     reads=[b_f3, b_gsc], writes=[b_ost])
                T.dma("sp", OT[h * 128:(h + 1) * 128, qs], ost[:], reads=[b_ost], writes=[bOT], sb=b_ost)
        C.close()

    def dl_attention():
        C = Ctx(nc, T)
        kbs = []
        for g in range(3):
            t, b = C.sb("kbl%d" % g, [128, NB], F32)
            T.dma("sp", t[:], kbdl[g], reads=[bIN], writes=[b], sb=b)
            kbs.append((t, b))
        raw_r = C.rot("raw", 3, [128, 2048], BF16)
        Qs, b_Qs = C.sb("Qs", [128, S], BF16)
        Ks, b_Ks = C.sb("Ks", [128, S], BF16)
        vs_r = C.rot("Vs", 2, [128, NB, 128], BF16)
        acc, b_acc = C.sb("acc", [128, S], F32)
        acs, b_acs = C.sb("acs", [128, S], F32)
        oo, b_oo = C.sb("oo", [128, S], BF16)
        e_r = C.rot("e", 4, [128, 256], BF16)
        m_r = C.rot("m", 4, [128, 256], BF16)
        sc_r = C.rot("sc", 4, [128, 256], F32, psum=True)
        po_r = C.rot("po", 2, [128, 256], F32, psum=True)
        pq_r = C.rot("pq", 2, [128, 256], F32, psum=True)
        cnt = 0
        for hp in range(8):
            T.op("dve", lambda: dve.memset(acc[:], 0.0), writes=[b_acc])
            T.op("pool", lambda: pool.memset(acs[:], 0.0), writes=[b_acs])
            for g in range(3):
                d = DIL[g]
                M = S // d
                MC = M // 128
                for which, (dstT, b_dst, srcT, b_src) in enumerate(((Qs, b_Qs, QT, bQT[g]), (Ks, b_Ks, KT, bKT[g]))):
                    dview = dstT[:].rearrange("p (r m) -> p r m", r=d)
                    for pc in range(S // 2048):
                        raw, b_raw = raw_r.next()
                        T.dma("sp", raw[:], srcT[g, hp * 128:(hp + 1) * 128, pc * 2048:(pc + 1) * 2048],
                              reads=[b_src], writes=[b_raw], sb=b_raw)
                        ml = 2048 // d
                        en = "pool" if (cnt % 2 == 0) else "dve"
                        cnt += 1
                        eng = pool if en == "pool" else dve
                        T.op(en, lambda: eng.tensor_copy(out=dview[:, :, pc * ml:(pc + 1) * ml],
                                                         in_=raw[:].rearrange("p (m r) -> p r m", r=d)),
                             reads=[b_raw], writes=[b_dst])
                Vs, b_Vs = vs_r.next()
                vsv = Vs[:].rearrange("p (r c) f -> p r c f", r=d)
                vsrc = VV[g, :, hp * 128:(hp + 1) * 128].rearrange("(m r) f -> r m f", r=d)
                for r in range(d):
                    T.dma("sp", vsv[:, r, :, :], vsrc[r].rearrange("(c p) f -> p c f", p=128),
                          reads=[bVV[g]], writes=[b_Vs], sb=b_Vs)
                kbt, b_kbt = kbs[g]
                Qv = Qs[:].rearrange("p (r m) -> p r m", r=d)
                Kv = Ks[:].rearrange("p (r m) -> p r m", r=d)
                accv = acc[:].rearrange("p (m r) -> p r m", r=d)
                acsv = acs[:].rearrange("p (m r) -> p r m", r=d)
                for r in range(d):
                    for c in range(MC):
                        qlo = max(0, 128 * c - 64)
                        qhi = min(M, 128 * c + 192)
                        N = qhi - qlo
                        mo = qlo - (128 * c - 64)
                        ms_ = []
                        for m in range(2):
                            ps_ = slice(m * 64, (m + 1) * 64)
                            sc, b_sc = sc_r.next()
                            T.op("pe", lambda: pe.matmul(sc[:, 0:N], lhsT=Kv[ps_, r, c * 128:(c + 1) * 128], rhs=Qv[ps_, r, qlo:qhi],
                                                         start=True, stop=True, tile_position=(m * 64, 0)),
                                 reads=[b_Ks, b_Qs], writes=[b_sc])
                            e, b_e = e_r.next()
                            T.op("act", lambda: act.activation(out=e[:, 0:N], in_=sc[:, 0:N], func=AF.Exp,
                                                               bias=kbt[:, r * MC + c:r * MC + c + 1], scale=0.125),
                                 reads=[b_sc, b_kbt], writes=[b_e])
                            mm_, b_mm = m_r.next()
                            en = "pool" if m == 0 else "dve"
                            eng = pool if m == 0 else dve
                            T.op(en, lambda: eng.tensor_tensor(out=mm_[:, 0:N], in0=e[:, 0:N], in1=bmask[:, mo:mo + N], op=ALU.mult),
                                 reads=[b_e, b_bmask], writes=[b_mm])
                            ms_.append((mm_, b_mm))
                        po, b_po = po_r.next()
                        pq, b_pq = pq_r.next()
                        for m in range(2):
                            mm_, b_mm = ms_[m]
                            os_ = slice(m * 64, (m + 1) * 64)
                            T.op("pe", lambda: pe.matmul(po[os_, 0:N], lhsT=Vs[:, r * MC + c, m * 64:(m + 1) * 64], rhs=mm_[:, 0:N],
                                                         start=True, stop=True, tile_position=(0, m * 64)),
                                 reads=[b_Vs, b_mm], writes=[b_po], sig=(m == 1))
                        for m in range(2):
                            mm_, b_mm = ms_[m]
                            os_ = slice(m * 64, (m + 1) * 64)
                            T.op("pe", lambda: pe.matmul(pq[os_, 0:N], lhsT=ones_bf[:, 0:64], rhs=mm_[:, 0:N],
                                                         start=True, stop=True, tile_position=(0, m * 64)),
                                 reads=[b_ones_bf, b_mm], writes=[b_pq], sig=(m == 1))
                        T.op("dve", lambda: dve.tensor_tensor(out=accv[:, r, qlo:qhi], in0=accv[:, r, qlo:qhi], in1=po[:, 0:N], op=ALU.add),
                             reads=[b_po, b_acc], writes=[b_acc])
                        T.op("dve", lambda: dve.tensor_tensor(out=acsv[:, r, qlo:qhi], in0=acsv[:, r, qlo:qhi], in1=pq[:, 0:N], op=ALU.add),
                             reads=[b_pq, b_acs], writes=[b_acs])
            T.op("dve", lambda: dve.tensor_scalar(out=acs[:], in0=acs[:], scalar1=1e-30, scalar2=None, op0=ALU.max),
                 reads=[b_acs], writes=[b_acs])
            T.op("dve", lambda: dve.reciprocal(out=acs[:], in_=acs[:]), reads=[b_acs], writes=[b_acs])
            T.op("dve", lambda: dve.tensor_tensor(out=oo[:], in0=acc[:], in1=acs[:], op=ALU.mult), reads=[b_acc, b_acs], writes=[b_oo])
            T.dma("sp", OT[hp * 128:(hp + 1) * 128, :], oo[:], reads=[b_oo], writes=[bOT], sb=b_oo)
        C.close()

    def tok_phase(i, x_src, b_xsrc, wo_ap, x_dst, b_xdst):
        C = Ctx(nc, T)
        wo, b_wo = C.sb("wo", [128, 8, D], BF16)
        T.dma("pool", wo[:], wo_ap.rearrange("(c p) f -> p c f", p=128), reads=[bIN], writes=[b_wo], sb=b_wo)
        wr, b_wr = C.sb("wr", [128, 8, 20], BF16)
        T.dma("pool", wr[:], w_r[i].rearrange("(c p) f -> p c f", p=128), reads=[bIN], writes=[b_wr], sb=b_wr)
        lnb = []
        for k in range(4):
            t, b = C.sb("lnb%d" % k, [128, D], F32)
            T.dma("sp", t[:], lnp[i, k, :].partition_broadcast(128), reads=[bIN], writes=[b], sb=b)
            lnb.append((t, b))
        NH = 2
        TM = NH * 512
        NS = NH * 4
        oT_r = C.rot("oT", 2, [128, 8, 512], BF16)
        xt_r = C.rot("xt", 1, [128, 4, D], F32)
        ya, b_ya = C.sb("ya", [128, NS, D], F32)
        tmp, b_tmp = C.sb("tmp", [128, D], F32)
        tmp2, b_tmp2 = C.sb("tmp2", [128, D], F32)
        xa, b_xa = C.sb("xa", [128, 4, TM], BF16)
        x1b, b_x1b = xa, b_xa
        x1T, b_x1T = C.sb("x1T", [128, 8, TM], BF16)
        st_, b_st = C.sb("stt", [128, 8], F32)
        rt, b_rt = C.sb("rt", [128, 64], F32)
        cw, b_cw = C.sb("cw", [128, NS, 16], F32)
        cwT, b_cwT = C.sb("cwT", [16, TM], BF16)
        cwb_r = C.rot("cwb", 2, [128, TM], F32)
        wg_r = C.rot("wg", 2, [128, 8, 512], BF16)
        wu_r = C.rot("wu", 2, [128, 8, 512], BF16)
        wd_r = C.rot("wd", 2, [128, 4, D], BF16)
        sg_r = C.rot("sg", 1, [128, 512], F32)
        tt_r = C.rot("tt", 1, [128, 512], F32)
        aT_r = Rot([(xa, b_xa)])
        xo_r = C.rot("xo", 1, [128, D], F32)
        ph_r = C.rot("ph", 2, [128, 512], F32, psum=True)
        tp_r = C.rot("tp", 1, [128, 512], BF16, psum=True)
        pg_r = C.rot("pg", 2, [128, 512], F32, psum=True)
        pu_r = C.rot("pu", 2, [128, 512], F32, psum=True)
        px, b_px = C.ps("px", [128, 512])

        def layer_norm(src_ap, b_src, gk, out_ap, b_out, extra_writes=()):
            g_, bg = lnb[gk]
            be_, bbe = lnb[gk + 1]
            T.op("dve", lambda: dve.tensor_reduce(out=st_[:, 0:1], in_=src_ap, axis=AX.X, op=ALU.add), reads=[b_src], writes=[b_st])
            T.op("pool", lambda: pool.tensor_tensor(out=tmp2[:], in0=src_ap, in1=src_ap, op=ALU.mult), reads=[b_src], writes=[b_tmp2])
            T.op("dve", lambda: dve.tensor_reduce(out=st_[:, 1:2], in_=tmp2[:], axis=AX.X, op=ALU.add), reads=[b_tmp2], writes=[b_st])
            T.op("dve", lambda: dve.tensor_scalar(out=st_[:, 2:4], in0=st_[:, 0:2], scalar1=1.0 / D, scalar2=None, op0=ALU.mult),
                 reads=[b_st], writes=[b_st])
            T.op("dve", lambda: dve.tensor_tensor(out=st_[:, 4:5], in0=st_[:, 2:3], in1=st_[:, 2:3], op=ALU.mult), reads=[b_st], writes=[b_st])
            T.op("dve", lambda: dve.tensor_tensor(out=st_[:, 5:6], in0=st_[:, 3:4], in1=st_[:, 4:5], op=ALU.subtract), reads=[b_st], writes=[b_st])
            T.op("dve", lambda: dve.tensor_scalar(out=st_[:, 6:7], in0=st_[:, 5:6], scalar1=LN_EPS, scalar2=None, op0=ALU.add),
                 reads=[b_st], writes=[b_st])
            T.op("act", lambda: act.activation(out=st_[:, 7:8], in_=st_[:, 6:7], func=AF.Ln), reads=[b_st], writes=[b_st])
            T.op("act", lambda: act.activation(out=st_[:, 6:7], in_=st_[:, 7:8], func=AF.Exp, scale=-0.5), reads=[b_st], writes=[b_st])
            T.op("dve", lambda: dve.tensor_scalar(out=tmp[:], in0=src_ap, scalar1=st_[:, 2:3], scalar2=st_[:, 6:7],
                                                  op0=ALU.subtract, op1=ALU.mult), reads=[b_src, b_st], writes=[b_tmp])
            T.op("pool", lambda: pool.tensor_tensor(out=tmp[:], in0=tmp[:], in1=g_[:], op=ALU.mult), reads=[b_tmp, bg], writes=[b_tmp])
            T.op("dve", lambda: dve.tensor_tensor(out=out_ap, in0=tmp[:], in1=be_[:], op=ALU.add), reads=[b_tmp, bbe],
                 writes=[b_out] + list(extra_writes))

        def load(ti):
            oT, b_oT = oT_r.next()
            T.dma("sp", oT[:], OT[:, ti * 512:(ti + 1) * 512].rearrange("(c p) t -> p c t", p=128), reads=[bOT], writes=[b_oT], sb=b_oT)
            return oT, b_oT

        def load_x(ti):
            xt, b_xt = xt_r.next()
            T.dma("sp", xt[:], x_src[ti * 512:(ti + 1) * 512, :].rearrange("(s p) d -> p s d", p=128),
                  reads=[b_xsrc], writes=[b_xt], sb=b_xt)
            return xt, b_xt

        def load_w(e):
            wg, b_wg = wg_r.next()
            wu, b_wu = wu_r.next()
            wd, b_wd = wd_r.next()
            T.dma("pool", wg[:], w_gate[i, e].rearrange("(c p) f -> p c f", p=128), reads=[bIN], writes=[b_wg], sb=b_wg)
            T.dma("pool", wu[:], w_up[i, e].rearrange("(c p) f -> p c f", p=128), reads=[bIN], writes=[b_wu], sb=b_wu)
            T.dma("pool", wd[:], w_down[i, e].rearrange("(c p) f -> p c f", p=128), reads=[bIN], writes=[b_wd], sb=b_wd)
            return wg, b_wg, wu, b_wu, wd, b_wd

        nxt = load(0)
        nw = load_w(0)
        TOK_DBG = int(os.environ.get("TOK_DBG", "0"))
        for sti in range(NT // NH):
            for hh in range(NH):
                ti = sti * NH + hh
                oT, b_oT = nxt
                xt, b_xt = load_x(ti)
                if ti + 1 < NT:
                    nxt = load(ti + 1)
                for s in range(4):
                    sg_ = hh * 4 + s
                    for hf in range(2):
                        ph, b_ph = ph_r.next()
                        for fc in range(8):
                            T.op("pe", lambda: pe.matmul(ph[:], lhsT=oT[:, fc, s * 128:(s + 1) * 128], rhs=wo[:, fc, hf * 512:(hf + 1) * 512],
                                                         start=(fc == 0), stop=(fc == 7)),
                                 reads=[b_oT, b_wo], writes=[b_ph], sig=(fc == 7))
                        T.op("dve", lambda: dve.scalar_tensor_tensor(out=ya[:, sg_, hf * 512:(hf + 1) * 512], in0=xt[:, s, hf * 512:(hf + 1) * 512],
                                                                     scalar=ALPHA, in1=ph[:], op0=ALU.mult, op1=ALU.add),
                             reads=[b_xt, b_ph], writes=[b_ya])
                    layer_norm(ya[:, sg_, :], b_ya, 0, ya[:, sg_, :], b_ya)
                    T.op("act", lambda: act.copy(out=x1b[:, s, :], in_=ya[:, sg_, :]), reads=[b_ya], writes=[b_x1b])
                for kc in range(8):
                    tp, b_tp = tp_r.next()
                    for s in range(4):
                        T.op("pe", lambda: pe.transpose(out=tp[:, s * 128:(s + 1) * 128], in_=x1b[:, s, kc * 128:(kc + 1) * 128],
                                                        identity=ident_bf[:]),
                             reads=[b_x1b, b_ident_bf], writes=[b_tp], sig=(s == 3))
                    T.op("act", lambda: act.copy(out=x1T[:, kc, hh * 512:(hh + 1) * 512], in_=tp[:]), reads=[b_tp], writes=[b_x1T])
                for s in range(4):
                    sg_ = hh * 4 + s
                    for kc in range(8):
                        T.op("pe", lambda: pe.matmul(px[:, 0:20], lhsT=x1T[:, kc, hh * 512 + s * 128:hh * 512 + (s + 1) * 128], rhs=wr[:, kc, :],
                                                     start=(kc == 0), stop=(kc == 7)),
                             reads=[b_x1T, b_wr], writes=[b_px], sig=(kc == 7))
                    R = lambda a, b: rt[:, a:b]
                    V = lambda fn: T.op("dve", fn, reads=[b_rt], writes=[b_rt])
                    T.op("dve", lambda: dve.tensor_copy(out=R(0, 20), in_=px[:, 0:20]), reads=[b_px], writes=[b_rt])
                    V(lambda: dve.tensor_reduce(out=R(20, 21), in_=R(0, 4), axis=AX.X, op=ALU.max))
                    V(lambda: dve.tensor_scalar(out=R(21, 25), in0=R(0, 4), scalar1=R(20, 21), scalar2=None, op0=ALU.is_equal))
                    V(lambda: dve.tensor_scalar(out=R(25, 26), in0=R(20, 21), scalar1=-1.0, scalar2=None, op0=ALU.mult))
                    T.op("act", lambda: act.activation(out=R(26, 30), in_=R(0, 4), func=AF.Exp, bias=R(25, 26), scale=1.0),
                         reads=[b_rt], writes=[b_rt])
                    V(lambda: dve.tensor_reduce(out=R(30, 31), in_=R(26, 30), axis=AX.X, op=ALU.add))
                    V(lambda: dve.reciprocal(out=R(31, 32), in_=R(30, 31)))
                    V(lambda: dve.tensor_scalar(out=R(32, 36), in0=R(4, 8), scalar1=R(21, 22), scalar2=None, op0=ALU.mult))
                    for gg in range(1, 4):
                        V(lambda: dve.scalar_tensor_tensor(out=R(32, 36), in0=R(4 + 4 * gg, 8 + 4 * gg), scalar=R(21 + gg, 22 + gg),
                                                           in1=R(32, 36), op0=ALU.mult, op1=ALU.add))
                    V(lambda: dve.tensor_reduce(out=R(36, 37), in_=R(32, 36), axis=AX.X, op=ALU.max))
                    V(lambda: dve.tensor_scalar(out=R(37, 41), in0=R(32, 36), scalar1=R(36, 37), scalar2=None, op0=ALU.is_equal))
                    V(lambda: dve.scalar_tensor_tensor(out=R(41, 45), in0=R(37, 41), scalar=-1e30, in1=R(32, 36), op0=ALU.mult, op1=ALU.add))
                    V(lambda: dve.tensor_reduce(out=R(45, 46), in_=R(41, 45), axis=AX.X, op=ALU.max))
                    V(lambda: dve.tensor_scalar(out=R(46, 50), in0=R(41, 45), scalar1=R(45, 46), scalar2=None, op0=ALU.is_equal))
                    V(lambda: dve.tensor_tensor(out=R(50, 51), in0=R(45, 46), in1=R(36, 37), op=ALU.subtract))
                    T.op("act", lambda: act.activation(out=R(51, 52), in_=R(50, 51), func=AF.Exp), reads=[b_rt], writes=[b_rt])
                    V(lambda: dve.tensor_scalar(out=R(52, 53), in0=R(51, 52), scalar1=1.0, scalar2=None, op0=ALU.add))
                    V(lambda: dve.reciprocal(out=R(53, 54), in_=R(52, 53)))
                    V(lambda: dve.tensor_tensor(out=R(54, 55), in0=R(53, 54), in1=R(31, 32), op=ALU.mult))
                    V(lambda: dve.tensor_tensor(out=R(55, 56), in0=R(31, 32), in1=R(54, 55), op=ALU.subtract))
                    V(lambda: dve.tensor_scalar(out=R(56, 60), in0=R(37, 41), scalar1=R(54, 55), scalar2=None, op0=ALU.mult))
                    V(lambda: dve.scalar_tensor_tensor(out=R(56, 60), in0=R(46, 50), scalar=R(55, 56), in1=R(56, 60), op0=ALU.mult, op1=ALU.add))
                    for gg in range(4):
                        T.op("dve", lambda: dve.tensor_scalar(out=cw[:, sg_, gg * 4:(gg + 1) * 4], in0=R(56, 60), scalar1=R(21 + gg, 22 + gg),
                                                              scalar2=None, op0=ALU.mult), reads=[b_rt], writes=[b_cw])
                for s in range(4):
                    T.op("pe", lambda: pe.transpose(out=px[0:16, s * 128:(s + 1) * 128], in_=cw[:, hh * 4 + s, :], identity=ident_f[:]),
                         reads=[b_cw, b_ident_f], writes=[b_px], sig=(s == 3))
                T.op("dve", lambda: dve.tensor_copy(out=cwT[:, hh * 512:(hh + 1) * 512], in_=px[0:16, :]), reads=[b_px], writes=[b_cwT])
            if TOK_DBG == 1:
                for s in range(NS):
                    T.dma("sp", x_dst[sti * TM + s * 128:sti * TM + (s + 1) * 128, :], ya[:, s, :], reads=[b_ya], writes=[b_xdst], sb=b_ya)
                continue
            T.op("pool", lambda: pool.tensor_scalar(out=ya[:], in0=ya[:], scalar1=ALPHA, scalar2=None, op0=ALU.mult),
                 reads=[b_ya], writes=[b_ya])
            for e in range(16):
                wg, b_wg, wu, b_wu, wd, b_wd = nw
                if not (sti == NT // NH - 1 and e == 15):
                    nw = load_w((e + 1) % 16)
                cwb, b_cwb = cwb_r.next()
                for hh in range(NH):
                    T.op("pe", lambda: pe.matmul(px[:], lhsT=sel[0:16, e * 128:(e + 1) * 128], rhs=cwT[0:16, hh * 512:(hh + 1) * 512],
                                                 start=True, stop=True), reads=[b_sel, b_cwT], writes=[b_px])
                    T.op("act", lambda: act.copy(out=cwb[:, hh * 512:(hh + 1) * 512], in_=px[:]), reads=[b_px], writes=[b_cwb])
                aT, b_aT = aT_r.next()
                for fcn in range(4):
                    for hh in range(NH):
                        hs = slice(hh * 512, (hh + 1) * 512)
                        pg, b_pg = pg_r.next()
                        pu, b_pu = pu_r.next()
                        for kc in range(8):
                            T.op("pe", lambda: pe.matmul(pg[:], lhsT=wg[:, kc, fcn * 128:(fcn + 1) * 128], rhs=x1T[:, kc, hs],
                                                         start=(kc == 0), stop=(kc == 7)), reads=[b_wg, b_x1T], writes=[b_pg], sig=(kc == 7))
                        for kc in range(8):
                            T.op("pe", lambda: pe.matmul(pu[:], lhsT=wu[:, kc, fcn * 128:(fcn + 1) * 128], rhs=x1T[:, kc, hs],
                                                         start=(kc == 0), stop=(kc == 7)), reads=[b_wu, b_x1T], writes=[b_pu], sig=(kc == 7))
                        sg, b_sg = sg_r.next()
                        tt, b_tt = tt_r.next()
                        T.op("act", lambda: act.activation(out=sg[:], in_=pg[:], func=AF.Silu), reads=[b_pg], writes=[b_sg])
                        T.op("dve", lambda: dve.tensor_tensor(out=tt[:], in0=pu[:], in1=cwb[:, hs], op=ALU.mult), reads=[b_pu, b_cwb], writes=[b_tt])
                        T.op("pool", lambda: pool.tensor_tensor(out=aT[:, fcn, hs], in0=sg[:], in1=tt[:], op=ALU.mult),
                             reads=[b_sg, b_tt], writes=[b_aT])
                for s in range(NS):
                    for hf in range(2):
                        ph, b_ph = ph_r.next()
                        for fcn in range(4):
                            T.op("pe", lambda: pe.matmul(ph[:], lhsT=aT[:, fcn, s * 128:(s + 1) * 128], rhs=wd[:, fcn, hf * 512:(hf + 1) * 512],
                                                         start=(fcn == 0), stop=(fcn == 3)), reads=[b_aT, b_wd], writes=[b_ph], sig=(fcn == 3))
                        T.op("dve", lambda: dve.tensor_tensor(out=ya[:, s, hf * 512:(hf + 1) * 512], in0=ya[:, s, hf * 512:(hf + 1) * 512],
                                                              in1=ph[:], op=ALU.add), reads=[b_ph, b_ya], writes=[b_ya])
            if TOK_DBG == 2:
                for s in range(NS):
                    T.dma("sp", x_dst[sti * TM + s * 128:sti * TM + (s + 1) * 128, :], ya[:, s, :], reads=[b_ya], writes=[b_xdst], sb=b_ya)
                continue
            for s in range(NS):
                xo, b_xo = xo_r.next()
                layer_norm(ya[:, s, :], b_ya, 2, xo[:], b_xo)
                T.dma("sp", x_dst[sti * TM + s * 128:sti * TM + (s + 1) * 128, :], xo[:], reads=[b_xo], writes=[b_xdst], sb=b_xo)
        C.close()

    cur, b_cur = x_in, bIN
    for i in layers:
        j = i // 2
        last = (i == layers[-1])
        dst, b_dst = (y_out, Buf("yout")) if last else (XR[i % 2], bXR[i % 2])
        if i % 2 == 0:
            lambda_init = 0.8 - 0.6 * math.exp(-0.3 * i)
            proj_pass(cur, b_cur, da_wm[j], da_wp[j], 0)
            if KSTOP >= 2:
                da_attention(j, lambda_init)
            if KSTOP >= 3:
                tok_phase(i, cur, b_cur, da_wo[j], dst, b_dst)
        else:
            for g in range(3):
                proj_pass(cur, b_cur, dl_wm[j, g], dl_wp[j, g], g)
            dl_attention()
            tok_phase(i, cur, b_cur, dl_wo[j], dst, b_dst)
        cur, b_cur = dst, b_dst
    T.barrier()
    G.es.close()
    return nc, T


def _rope_tables(S):
    half = 8
    inv = (ROPE_THETA ** (-np.arange(half, dtype=np.float32) * np.float32(2.0 / 16))).astype(np.float32)
    ang = np.arange(S, dtype=np.float32)[:, None] * inv[None, :]
    cos = np.cos(ang).astype(np.float32).T
    sin = np.sin(ang).astype(np.float32).T
    C = np.ones((128, S), np.float32)
    Sg = np.zeros((128, S), np.float32)
    for hh in range(2):
        b = hh * 64
        C[b:b + 8] = cos
        C[b + 8:b + 16] = cos
        Sg[b:b + 8] = -sin
        Sg[b + 8:b + 16] = sin
    return C, Sg


def _partner_perm(ncols):
    idx = np.arange(ncols)
    i = idx % 64
    p = idx.copy()
    p[i < 8] = idx[i < 8] + 8
    m = (i >= 8) & (i < 16)
    p[m] = idx[m] - 8
    return p


def prep_shared(inp, S):
    f = lambda a: np.ascontiguousarray(np.asarray(a, dtype=np.float32))
    da_w_in = f(inp["da_w_in"])
    hidx = np.arange(8)[:, None, None]
    midx = np.arange(2)[None, :, None]
    didx = np.arange(64)[None, None, :]
    qcols = (midx * 512 + hidx * 64 + didx).reshape(-1)
    kcols = qcols + 1024
    pp = _partner_perm(1024)
    da_wm = np.concatenate([da_w_in[:, :, qcols], da_w_in[:, :, kcols], da_w_in[:, :, 2048:]], axis=2)
    da_wp = np.concatenate([da_w_in[:, :, qcols[pp]], da_w_in[:, :, kcols[pp]]], axis=2)
    dl = f(inp["dl_w_in"]).reshape(2, D, 3, 3, 1024)
    dl_wm = np.ascontiguousarray(dl.transpose(0, 2, 1, 3, 4).reshape(2, 3, D, 3072))
    dlq = dl[:, :, :, 0, :][..., pp]
    dlk = dl[:, :, :, 1, :][..., pp]
    dl_wp = np.ascontiguousarray(np.concatenate([dlq, dlk], axis=-1).transpose(0, 2, 1, 3))
    C, Sg = _rope_tables(S)
    ii = np.arange(128)[:, None]
    jj = np.arange(256)[None, :]
    bmask = ((ii >= jj - 128) & (ii <= jj)).astype(np.float32)
    sel = np.zeros((16, 16, 128), np.float32)
    for e in range(16):
        sel[e, e, :] = 1.0
    sh = {
        "ropec": C, "ropes": Sg, "ident": np.eye(128, dtype=np.float32), "bmask": bmask,
        "sel": sel.reshape(16, 16 * 128),
        "da_wm": np.ascontiguousarray(da_wm), "da_wp": np.ascontiguousarray(da_wp), "da_wo": f(inp["da_w_out"]),
        "da_lam": np.ascontiguousarray(np.stack([f(inp["da_lambda_q1"]), f(inp["da_lambda_k1"]),
                                                 f(inp["da_lambda_q2"]), f(inp["da_lambda_k2"])], axis=1)),
        "da_g": f(inp["da_subln_g"]),
        "dl_wm": dl_wm, "dl_wp": dl_wp, "dl_wo": f(inp["dl_w_out"]),
        "lnp": np.ascontiguousarray(np.stack([f(inp["ln1_g"]), f(inp["ln1_b"]), f(inp["ln2_g"]), f(inp["ln2_b"])], axis=1)),
        "w_r": np.ascontiguousarray(np.concatenate([f(inp["moe_router_group"]),
                                                    f(inp["moe_router_expert"]).reshape(4, D, 16)], axis=2)),
        "w_gate": f(inp["moe_w_gate"]).reshape(4, 16, D, 512),
        "w_up": f(inp["moe_w_up"]).reshape(4, 16, D, 512),
        "w_down": f(inp["moe_w_down"]).reshape(4, 16, 512, D),
    }
    return sh


def prep_core(xseq, S):
    L = xseq.shape[0]
    xp = np.zeros((S, D), np.float32)
    xp[:L] = xseq
    kb = np.zeros((S,), np.float32)
    kb[L:] = NEG
    NB = S // 128
    m = {"x": xp, "kbda": np.ascontiguousarray(kb.reshape(NB, 128).T)}
    for g, d in enumerate(DIL):
        M = S // d
        MC = M // 128
        t = kb.reshape(MC, 128, d)
        m["kbdl%d" % g] = np.ascontiguousarray(t.transpose(1, 2, 0).reshape(128, d * MC))
    return m


_CACHE = {}


def kernel(**inputs):
    S = 8192
    xp = np.asarray(inputs["x_prompt"], dtype=np.float32)
    xs = np.asarray(inputs["x_sample"], dtype=np.float32)
    sh = prep_shared(inputs, S)
    seqs = [xp[b] for b in range(4)] + [xs[b] for b in range(4)]
    in_maps = []
    for c in range(8):
        m = dict(sh)
        m.update(prep_core(seqs[c], S))
        in_maps.append(m)
    if "nc" not in _CACHE:
        _CACHE["nc"] = build_program(S, [0, 1, 2, 3])[0]
    nc = _CACHE["nc"]
    res = run_bass_kernel_spmd(nc, in_maps, core_ids=list(range(8)))
    ys = [np.asarray(r["y"], dtype=np.float32) for r in res.results]
    y_prompt = np.stack([ys[b] for b in range(4)], axis=0)
    y_sample = np.stack([ys[4 + b][:xs.shape[1]] for b in range(4)], axis=0)
    return (y_prompt, y_sample)
```

```python
import math
from contextlib import ExitStack
import numpy as np
import concourse.bass as bass
import concourse.mybir as mybir
from concourse.bass_utils import run_bass_kernel_spmd

F32 = mybir.dt.float32
BF16 = mybir.dt.bfloat16
AF = mybir.ActivationFunctionType
ALU = mybir.AluOpType
AX = mybir.AxisListType

D = 1024
DEPTH = 4
LN_EPS = 1e-5
ALPHA = (2.0 * DEPTH) ** 0.25
ROPE_THETA = 500000.0
DIL = (1, 4, 16)
NEG = -30000.0
import os
KSTOP = int(os.environ.get('KSTOP', '3'))
DA_FIN = int(os.environ.get('DA_FIN', '1'))
DA_LOOP = int(os.environ.get('DA_LOOP', '1'))


class Buf:
    __slots__ = ("name", "w", "r", "ds", "dq")

    def __init__(self, name):
        self.name = name
        self.w = {}
        self.r = {}
        self.ds = None
        self.dq = None


class Trk:
    def __init__(self, nc):
        self.nc = nc
        self.eng = {"pe": nc.tensor, "act": nc.scalar, "dve": nc.vector, "pool": nc.gpsimd, "sp": nc.sync}
        self.esem = {k: nc.alloc_semaphore(name="e_" + k) for k in ("pe", "act", "dve", "pool")}
        self.ecnt = {k: 0 for k in self.esem}
        self.pending = {k: [] for k in self.esem}
        self.waited = {e: {} for e in self.eng}
        self.dsems = []
        self.free_ds = {"sp": [], "pool": []}
        self.ninstr = 0

    def get_ds(self, q):
        if self.free_ds[q]:
            return self.free_ds[q].pop()
        h = self.nc.alloc_semaphore(name="d%s%d" % (q, len(self.dsems)))
        self.dsems.append([h, 0])
        return len(self.dsems) - 1

    def release_ds(self, bufs):
        for b in bufs:
            if b.ds is not None:
                self.free_ds[b.dq].append(b.ds)
                b.ds = None
                b.dq = None

    def _semval(self, key, tok):
        if key[0] == "E":
            assert tok[1] is not None, "dependency on unsignaled instr of %s" % key[1]
            return self.esem[key[1]], tok[1]
        h, tot = self.dsems[key[1]]
        return h, tot

    def _waits(self, e, deps, skip_same, raw=None):
        for key, tok in deps.items():
            if skip_same and key == ("E", e):
                if e == "pe" or raw is None or key not in raw:
                    continue
                tok = raw[key]
            h, val = self._semval(key, tok)
            if self.waited[e].get(key, 0) >= val:
                continue
            self.eng[e].wait_ge(h, val)
            self.waited[e][key] = val
            self.ninstr += 1

    @staticmethod
    def _collect(reads, writes):
        deps = {}

        def add(k, t):
            o = deps.get(k)
            if o is None or t[1] is None or (o[1] is not None and t[1] > o[1]):
                deps[k] = t

        for b in reads:
            for k, t in b.w.items():
                add(k, t)
        for b in writes:
            for k, t in b.w.items():
                add(k, t)
            for k, t in b.r.items():
                add(k, t)
        return deps

    @staticmethod
    def _record(key, tok, reads, writes):
        for b in reads:
            o = b.r.get(key)
            if o is None or tok[1] is None or (o[1] is not None and tok[1] >= o[1]):
                b.r[key] = tok
        for b in writes:
            b.w = {key: tok}
            b.r = {}

    def op(self, e, fn, reads=(), writes=(), sig=True):
        deps = self._collect(reads, writes)
        raw = self._collect(reads, ())
        self._waits(e, deps, True, raw)
        ins = fn()
        self.ninstr += 1
        key = ("E", e)
        tok = [key, None]
        if sig:
            self.ecnt[e] += 1
            ins.then_inc(self.esem[e], 1)
            tok[1] = self.ecnt[e]
            for t in self.pending[e]:
                t[1] = self.ecnt[e]
            self.pending[e] = []
        else:
            self.pending[e].append(tok)
        self._record(key, tok, reads, writes)
        return ins

    def dma(self, q, out, in_, reads=(), writes=(), sb=None):
        deps = self._collect(reads, writes)
        self._waits(q, deps, False)
        if sb.ds is None:
            sb.ds = self.get_ds(q)
            sb.dq = q
        assert sb.dq == q, "buffer %s used with two DMA queues" % sb.name
        ins = self.eng[q].dma_start(out=out, in_=in_)
        self.ninstr += 1
        d = self.dsems[sb.ds]
        d[1] += 16
        ins.then_inc(d[0], 16)
        key = ("D", sb.ds)
        tok = [key, d[1]]
        self._record(key, tok, reads, writes)
        return ins

    def barrier(self):
        for e in self.eng:
            assert not self.pending.get(e, [])
        for e in self.eng:
            for k in self.esem:
                if k == e:
                    continue
                key = ("E", k)
                val = self.ecnt[k]
                if val > 0 and self.waited[e].get(key, 0) < val:
                    self.eng[e].wait_ge(self.esem[k], val)
                    self.waited[e][key] = val
            for i, (h, tot) in enumerate(self.dsems):
                key = ("D", i)
                if tot > 0 and self.waited[e].get(key, 0) < tot:
                    self.eng[e].wait_ge(h, tot)
                    self.waited[e][key] = tot


class Rot:
    def __init__(self, items):
        self.items = items
        self.i = 0

    def next(self):
        it = self.items[self.i % len(self.items)]
        self.i += 1
        return it


class Ctx:
    def __init__(self, nc, trk):
        self.nc = nc
        self.trk = trk
        self.es = ExitStack()
        self.bufs = []

    _uid = [0]

    def sb(self, name, shape, dt):
        Ctx._uid[0] += 1
        name = "s%d_%s" % (Ctx._uid[0], name)
        t = self.es.enter_context(self.nc.sbuf_tensor(name, list(shape), dt))
        b = Buf(name)
        self.bufs.append(b)
        return t, b

    def ps(self, name, shape, dt=F32):
        Ctx._uid[0] += 1
        name = "p%d_%s" % (Ctx._uid[0], name)
        t = self.es.enter_context(self.nc.psum_tensor(name, list(shape), dt))
        b = Buf(name)
        self.bufs.append(b)
        return t, b

    def rot(self, name, n, shape, dt, psum=False):
        f = self.ps if psum else self.sb
        return Rot([f("%s%d" % (name, i), shape, dt) for i in range(n)])

    def close(self):
        self.trk.barrier()
        self.trk.release_ds(self.bufs)
        self.es.close()


def build_program(S, layers, nsub_tok=4):
    assert S % 2048 == 0
    NT = S // 512
    NB = S // 128
    nc = bass.Bass("TRN2", target_bir_lowering=False)
    dt_in = lambda name, shape: nc.dram_tensor(name, list(shape), F32, kind="ExternalInput").ap()
    x_in = dt_in("x", [S, D])
    kbda = dt_in("kbda", [128, NB])
    kbdl = [dt_in("kbdl%d" % g, [128, NB]) for g in range(3)]
    rc_in = dt_in("ropec", [128, S])
    rs_in = dt_in("ropes", [128, S])
    ident_in = dt_in("ident", [128, 128])
    mask_in = dt_in("bmask", [128, 256])
    sel_in = dt_in("sel", [16, 16 * 128])
    da_wm = dt_in("da_wm", [2, D, 3072])
    da_wp = dt_in("da_wp", [2, D, 2048])
    da_wo = dt_in("da_wo", [2, D, D])
    da_lam = dt_in("da_lam", [2, 4, 64])
    da_g = dt_in("da_g", [2, 128])
    dl_wm = dt_in("dl_wm", [2, 3, D, 3072])
    dl_wp = dt_in("dl_wp", [2, 3, D, 2048])
    dl_wo = dt_in("dl_wo", [2, D, D])
    lnp = dt_in("lnp", [4, 4, D])
    w_r = dt_in("w_r", [4, D, 20])
    w_gate = dt_in("w_gate", [4, 16, D, 512])
    w_up = dt_in("w_up", [4, 16, D, 512])
    w_down = dt_in("w_down", [4, 16, 512, D])
    y_out = nc.dram_tensor("y", [S, D], F32, kind="ExternalOutput").ap()
    KDBG = int(os.environ.get("KDBG", "0"))
    scr = lambda name, shape, dt: nc.dram_tensor(name, list(shape), dt, kind=("ExternalOutput" if KDBG else "Internal")).ap()
    QT = scr("QT", [3, D, S], BF16)
    KT = scr("KT", [3, D, S], BF16)
    VV = scr("VV", [3, S, D], BF16)
    OT = scr("OT", [D, S], BF16)
    XR = [scr("XR0", [S, D], F32), scr("XR1", [S, D], F32)]
    WB = [(scr("WBg%d" % p_, [16, D, 512], BF16), scr("WBu%d" % p_, [16, D, 512], BF16), scr("WBd%d" % p_, [16, 512, D], BF16))
          for p_ in range(1)]
    WB = [WB[0], WB[0]]
    _b = Buf("WB0")
    bWB = [_b, _b]
    b_wconv = Buf("wconv")
    bQT = [Buf("QT%d" % g) for g in range(3)]
    bKT = [Buf("KT%d" % g) for g in range(3)]
    bVV = [Buf("VV%d" % g) for g in range(3)]
    bOT = Buf("OT")
    bXR = [Buf("XR0"), Buf("XR1")]
    bIN = Buf("inputs")

    T = Trk(nc)
    E = T.eng
    pe, act, dve, pool = nc.tensor, nc.scalar, nc.vector, nc.gpsimd

    G = Ctx(nc, T)
    ident_bf, b_ident_bf = G.sb("ident_bf", [128, 128], BF16)
    ident_f, b_ident_f = G.sb("ident_f", [128, 128], F32)
    ones_bf, b_ones_bf = G.sb("ones_bf", [128, 128], BF16)
    ones_f, b_ones_f = G.sb("ones_f", [128, 128], F32)
    bmask, b_bmask = G.sb("bmask", [128, 256], BF16)
    sel, b_sel = G.sb("sel", [16, 16 * 128], BF16)
    T.dma("pool", ident_bf[:], ident_in, reads=[bIN], writes=[b_ident_bf], sb=b_ident_bf)
    T.dma("sp", ident_f[:], ident_in, reads=[bIN], writes=[b_ident_f], sb=b_ident_f)
    T.dma("pool", bmask[:], mask_in, reads=[bIN], writes=[b_bmask], sb=b_bmask)
    T.dma("pool", sel[:], sel_in, reads=[bIN], writes=[b_sel], sb=b_sel)
    T.op("dve", lambda: dve.memset(ones_bf[:], 1.0), writes=[b_ones_bf])
    T.op("dve", lambda: dve.memset(ones_f[:], 1.0), writes=[b_ones_f])

    def convert_weights(i):
        p_ = i % 2
        for e in range(16):
            T.dma("pool", WB[p_][0][e], w_gate[i, e], reads=[bIN], writes=[bWB[p_]], sb=b_wconv)
            T.dma("pool", WB[p_][1][e], w_up[i, e], reads=[bIN], writes=[bWB[p_]], sb=b_wconv)
            T.dma("pool", WB[p_][2][e], w_down[i, e], reads=[bIN], writes=[bWB[p_]], sb=b_wconv)

    def proj_pass(x_src, b_xsrc, wm_ap, wp_ap, g):
        C = Ctx(nc, T)
        wm, b_wm = C.sb("wm", [128, 8, 3072], BF16)
        wp, b_wp = C.sb("wp", [128, 8, 2048], BF16)
        for kc in range(8):
            T.dma("pool", wm[:, kc, :], wm_ap[kc * 128:(kc + 1) * 128, :], reads=[bIN], writes=[b_wm], sb=b_wm)
            T.dma("pool", wp[:, kc, :], wp_ap[kc * 128:(kc + 1) * 128, :], reads=[bIN], writes=[b_wp], sb=b_wp)
        xt_r = C.rot("xt", 2, [128, 4, D], F32)
        xb_r = C.rot("xb", 1, [128, 4, D], BF16)
        xT_r = C.rot("xT", 2, [128, 8, 512], BF16)
        rc_r = C.rot("rc", 2, [128, 512], F32)
        rs_r = C.rot("rs", 2, [128, 512], F32)
        t1_r = C.rot("t1", 2, [128, 512], F32)
        t2_r = C.rot("t2", 2, [128, 512], F32)
        st_r = C.rot("st", 3, [128, 512], BF16)
        vst_r = C.rot("vst", 2, [128, 4, D], BF16)
        tp_r = C.rot("tp", 2, [128, 512], BF16, psum=True)
        pm_r = C.rot("pm", 2, [128, 512], F32, psum=True)
        pp_r = C.rot("pp", 2, [128, 512], F32, psum=True)
        pv_r = C.rot("pv", 2, [128, 512], F32, psum=True)

        def load(ti):
            xt, b_xt = xt_r.next()
            T.dma("sp", xt[:], x_src[ti * 512:(ti + 1) * 512, :].rearrange("(s p) d -> p s d", p=128),
                  reads=[b_xsrc], writes=[b_xt], sb=b_xt)
            rc, b_rc = rc_r.next()
            rs, b_rs = rs_r.next()
            T.dma("sp", rc[:], rc_in[:, ti * 512:(ti + 1) * 512], reads=[bIN], writes=[b_rc], sb=b_rc)
            T.dma("sp", rs[:], rs_in[:, ti * 512:(ti + 1) * 512], reads=[bIN], writes=[b_rs], sb=b_rs)
            return (xt, b_xt, rc, b_rc, rs, b_rs)

        nxt = load(0)
        for ti in range(NT):
            xt, b_xt, rc, b_rc, rs, b_rs = nxt
            if ti + 1 < NT:
                nxt = load(ti + 1)
            xb, b_xb = xb_r.next()
            for s in range(4):
                if s % 2 == 0:
                    T.op("act", lambda: act.copy(out=xb[:, s, :], in_=xt[:, s, :]), reads=[b_xt], writes=[b_xb])
                else:
                    T.op("dve", lambda: dve.tensor_copy(out=xb[:, s, :], in_=xt[:, s, :]), reads=[b_xt], writes=[b_xb])
            xT, b_xT = xT_r.next()
            for kc in range(8):
                tp, b_tp = tp_r.next()
                for s in range(4):
                    T.op("pe", lambda: pe.transpose(out=tp[:, s * 128:(s + 1) * 128], in_=xb[:, s, kc * 128:(kc + 1) * 128],
                                                    identity=ident_bf[:]),
                         reads=[b_xb, b_ident_bf], writes=[b_tp], sig=(s == 3))
                if kc % 2 == 0:
                    T.op("act", lambda: act.copy(out=xT[:, kc, :], in_=tp[:]), reads=[b_tp], writes=[b_xT])
                else:
                    T.op("dve", lambda: dve.tensor_copy(out=xT[:, kc, :], in_=tp[:]), reads=[b_tp], writes=[b_xT])
            for oc in range(16):
                pm, b_pm = pm_r.next()
                pp, b_pp = pp_r.next()
                for kc in range(8):
                    T.op("pe", lambda: pe.matmul(pm[:], lhsT=wm[:, kc, oc * 128:(oc + 1) * 128], rhs=xT[:, kc, :],
                                                 start=(kc == 0), stop=(kc == 7)),
                         reads=[b_wm, b_xT], writes=[b_pm], sig=(kc == 7))
                for kc in range(8):
                    T.op("pe", lambda: pe.matmul(pp[:], lhsT=wp[:, kc, oc * 128:(oc + 1) * 128], rhs=xT[:, kc, :],
                                                 start=(kc == 0), stop=(kc == 7)),
                         reads=[b_wp, b_xT], writes=[b_pp], sig=(kc == 7))
                t1, b_t1 = t1_r.next()
                t2, b_t2 = t2_r.next()
                st, b_st = st_r.next()
                T.op("dve", lambda: dve.tensor_tensor(out=t1[:], in0=pm[:], in1=rc[:], op=ALU.mult),
                     reads=[b_pm, b_rc], writes=[b_t1])
                T.op("dve", lambda: dve.tensor_tensor(out=t2[:], in0=pp[:], in1=rs[:], op=ALU.mult),
                     reads=[b_pp, b_rs], writes=[b_t2])
                T.op("pool", lambda: pool.tensor_tensor(out=st[:], in0=t1[:], in1=t2[:], op=ALU.add),
                     reads=[b_t1, b_t2], writes=[b_st])
                dst = QT if oc < 8 else KT
                bd = bQT[g] if oc < 8 else bKT[g]
                c8 = oc % 8
                T.dma("sp", dst[g, c8 * 128:(c8 + 1) * 128, ti * 512:(ti + 1) * 512], st[:], reads=[b_st], writes=[bd], sb=b_st)
            vst, b_vst = vst_r.next()
            for s in range(4):
                for hf in range(2):
                    pv, b_pv = pv_r.next()
                    for kc in range(8):
                        T.op("pe", lambda: pe.matmul(pv[:], lhsT=xT[:, kc, s * 128:(s + 1) * 128],
                                                     rhs=wm[:, kc, 2048 + hf * 512:2048 + (hf + 1) * 512],
                                                     start=(kc == 0), stop=(kc == 7)),
                             reads=[b_wm, b_xT], writes=[b_pv], sig=(kc == 7))
                    T.op("act", lambda: act.copy(out=vst[:, s, hf * 512:(hf + 1) * 512], in_=pv[:]), reads=[b_pv], writes=[b_vst])
            T.dma("sp", VV[g, ti * 512:(ti + 1) * 512, :].rearrange("(s p) d -> p s d", p=128), vst[:],
                  reads=[b_vst], writes=[bVV[g]], sb=b_vst)
        C.close()

    def da_attention(j, lambda_init):
        C = Ctx(nc, T)
        kb, b_kb = C.sb("kb", [128, NB], F32)
        T.dma("sp", kb[:], kbda, reads=[bIN], writes=[b_kb], sb=b_kb)
        lamv, b_lamv = C.sb("lamv", [1, 4, 64], F32)
        T.dma("sp", lamv[:], da_lam[j:j + 1, :, :], reads=[bIN], writes=[b_lamv], sb=b_lamv)
        lt, b_lt = C.sb("lt", [1, 2, 64], F32)
        ls, b_ls = C.sb("ls", [1, 4], F32)
        T.op("dve", lambda: dve.tensor_tensor(out=lt[:, 0, :], in0=lamv[:, 0, :], in1=lamv[:, 1, :], op=ALU.mult),
             reads=[b_lamv], writes=[b_lt])
        T.op("dve", lambda: dve.tensor_tensor(out=lt[:, 1, :], in0=lamv[:, 2, :], in1=lamv[:, 3, :], op=ALU.mult),
             reads=[b_lamv], writes=[b_lt])
        T.op("dve", lambda: dve.tensor_reduce(out=ls[:, 0:2], in_=lt[:], axis=AX.X, op=ALU.add), reads=[b_lt], writes=[b_ls])
        T.op("act", lambda: act.activation(out=ls[:, 2:4], in_=ls[:, 0:2], func=AF.Exp), reads=[b_ls], writes=[b_ls])
        T.op("dve", lambda: dve.tensor_tensor(out=ls[:, 0:1], in0=ls[:, 3:4], in1=ls[:, 2:3], op=ALU.subtract),
             reads=[b_ls], writes=[b_ls])
        T.op("dve", lambda: dve.tensor_scalar(out=ls[:, 1:2], in0=ls[:, 0:1], scalar1=-lambda_init, scalar2=None, op0=ALU.add),
             reads=[b_ls], writes=[b_ls])
        nlam, b_nlam = C.sb("nlam", [128, 1], F32)
        gsc, b_gsc = C.sb("gsc", [128, 1], F32)
        T.dma("sp", gsc[:], da_g[j, :].rearrange("(p o) -> p o", o=1), reads=[bIN], writes=[b_gsc], sb=b_gsc)
        T.op("dve", lambda: dve.tensor_scalar(out=gsc[:], in0=gsc[:], scalar1=(1.0 - lambda_init), scalar2=None, op0=ALU.mult),
             reads=[b_gsc], writes=[b_gsc])
        sc_r = C.rot("sc", 2, [128, 1024], F32, psum=True)
        OA, b_OA = C.ps("OA", [128, 512])
        OB, b_OB = C.ps("OB", [128, 512])
        SA, b_SA = C.ps("SA", [128, 512])
        SB, b_SB = C.ps("SB", [128, 512])
        MS, b_MS = SA, b_SA
        T.op("pe", lambda: pe.matmul(MS[:, 0:1], lhsT=ones_f[0:1, :], rhs=ls[0:1, 1:2], start=True, stop=True),
             reads=[b_ones_f, b_ls], writes=[b_MS])
        T.op("dve", lambda: dve.tensor_copy(out=nlam[:], in_=MS[:, 0:1]), reads=[b_MS], writes=[b_nlam])
        qt_r = C.rot("qT", 2, [128, S], BF16)
        kt_r = C.rot("kT", 2, [128, S], BF16)
        v_r = C.rot("vh", 2, [128, NB, 128], BF16)
        e_r = C.rot("e", 3, [128, 1024], BF16)
        sacc, b_sacc = C.sb("sacc", [128, 1024], F32)
        b_sacc2 = Buf("sacc2")
        SPL = 768
        f_r = [C.rot("f%d" % i, 1, [128, 512], F32) for i in range(4)]
        ost_r = C.rot("ost", 2, [128, 512], BF16)

        def load(h):
            qT, b_qT = qt_r.next()
            kT, b_kT = kt_r.next()
            vh, b_vh = v_r.next()
            T.dma("sp", qT[:], QT[0, h * 128:(h + 1) * 128, :], reads=[bQT[0]], writes=[b_qT], sb=b_qT)
            T.dma("sp", kT[:], KT[0, h * 128:(h + 1) * 128, :], reads=[bKT[0]], writes=[b_kT], sb=b_kT)
            T.dma("sp", vh[:], VV[0, :, h * 128:(h + 1) * 128].rearrange("(c p) f -> p c f", p=128),
                  reads=[bVV[0]], writes=[b_vh], sb=b_vh)
            return qT, b_qT, kT, b_kT, vh, b_vh

        nxt = load(0)
        for h in range(8):
            qT, b_qT, kT, b_kT, vh, b_vh = nxt
            if h + 1 < 8:
                nxt = load(h + 1)
            for qt in range(NT):
                qs = slice(qt * 512, (qt + 1) * 512)
                pend = None

                def stage2(kc, e, b_e):
                    first, last = kc == 0, kc == NB - 1
                    T.op("pe", lambda: pe.matmul(OA[:], lhsT=vh[:, kc, :], rhs=e[:, 0:512], start=first, stop=last),
                         reads=[b_vh, b_e], writes=[b_OA], sig=False)
                    T.op("pe", lambda: pe.matmul(OB[:], lhsT=vh[:, kc, :], rhs=e[:, 512:1024], start=first, stop=last),
                         reads=[b_vh, b_e], writes=[b_OB], sig=True)
                    if first:
                        T.op("dve", lambda: dve.tensor_copy(out=sacc[:, 0:SPL], in_=e[:, 0:SPL]), reads=[b_e], writes=[b_sacc])
                        T.op("pool", lambda: pool.tensor_copy(out=sacc[:, SPL:1024], in_=e[:, SPL:1024]), reads=[b_e], writes=[b_sacc2])
                    else:
                        T.op("dve", lambda: dve.tensor_tensor(out=sacc[:, 0:SPL], in0=sacc[:, 0:SPL], in1=e[:, 0:SPL], op=ALU.add),
                             reads=[b_e], writes=[b_sacc])
                        T.op("pool", lambda: pool.tensor_tensor(out=sacc[:, SPL:1024], in0=sacc[:, SPL:1024], in1=e[:, SPL:1024], op=ALU.add),
                             reads=[b_e], writes=[b_sacc2])

                for kc in (range(NB) if DA_LOOP else []):
                    ks = slice(kc * 128, (kc + 1) * 128)
                    sc, b_sc = sc_r.next()
                    for m in range(2):
                        ps_ = slice(m * 64, (m + 1) * 64)
                        T.op("pe", lambda: pe.matmul(sc[:, m * 512:(m + 1) * 512], lhsT=kT[ps_, ks], rhs=qT[ps_, qs], start=True, stop=True,
                                                     tile_position=(m * 64, 0)),
                             reads=[b_kT, b_qT], writes=[b_sc], sig=(m == 1))
                    e, b_e = e_r.next()
                    T.op("act", lambda: act.activation(out=e[:], in_=sc[:], func=AF.Exp, bias=kb[:, kc:kc + 1], scale=0.125),
                         reads=[b_sc, b_kb], writes=[b_e])
                    if pend is not None:
                        stage2(*pend)
                    pend = (kc, e, b_e)
                if pend is not None:
                    stage2(*pend)
                if not DA_FIN:
                    continue
                (f0, b_f0), (f1, b_f1), (f2, b_f2), (f3, b_f3) = [r.next() for r in f_r]
                T.op("pe", lambda: pe.matmul(SA[:], lhsT=ones_f[:], rhs=sacc[:, 0:512], start=True, stop=True),
                     reads=[b_ones_f, b_sacc], writes=[b_SA])
                T.op("pe", lambda: pe.matmul(SB[:], lhsT=ones_f[:], rhs=sacc[:, 512:1024], start=True, stop=True),
                     reads=[b_ones_f, b_sacc, b_sacc2], writes=[b_SB])
                T.op("dve", lambda: dve.reciprocal(out=f0[:], in_=SA[:]), reads=[b_SA], writes=[b_f0])
                T.op("dve", lambda: dve.tensor_tensor(out=f1[:], in0=OA[:], in1=f0[:], op=ALU.mult), reads=[b_OA, b_f0], writes=[b_f1])
                T.op("dve", lambda: dve.reciprocal(out=f0[:], in_=SB[:]), reads=[b_SB], writes=[b_f0])
                T.op("dve", lambda: dve.tensor_tensor(out=f2[:], in0=OB[:], in1=f0[:], op=ALU.mult), reads=[b_OB, b_f0], writes=[b_f2])
                T.op("dve", lambda: dve.scalar_tensor_tensor(out=f3[:], in0=f2[:], scalar=nlam[:, 0:1], in1=f1[:],
                                                             op0=ALU.mult, op1=ALU.add),
                     reads=[b_f2, b_f1, b_nlam], writes=[b_f3])
                T.op("act", lambda: act.activation(out=f1[:], in_=f3[:], func=AF.Square), reads=[b_f3], writes=[b_f1])
                T.op("pe", lambda: pe.matmul(MS[:], lhsT=ones_f[:], rhs=f1[:], start=True, stop=True),
                     reads=[b_ones_f, b_f1], writes=[b_MS])
                T.op("dve", lambda: dve.tensor_scalar(out=f2[:], in0=MS[:], scalar1=1.0 / 128.0, scalar2=LN_EPS,
                                                      op0=ALU.mult, op1=ALU.add), reads=[b_MS], writes=[b_f2])
                T.op("act", lambda: act.activation(out=f2[:], in_=f2[:], func=AF.Ln), reads=[b_f2], writes=[b_f2])
                T.op("act", lambda: act.activation(out=f2[:], in_=f2[:], func=AF.Exp, scale=-0.5), reads=[b_f2], writes=[b_f2])
                T.op("dve", lambda: dve.tensor_tensor(out=f3[:], in0=f3[:], in1=f2[:], op=ALU.mult), reads=[b_f3, b_f2], writes=[b_f3])
                ost, b_ost = ost_r.next()
                T.op("dve", lambda: dve.tensor_scalar(out=ost[:], in0=f3[:], scalar1=gsc[:, 0:1], scalar2=None, op0=ALU.mult),
                     reads=[b_f3, b_gsc], writes=[b_ost])
                T.dma("sp", OT[h * 128:(h + 1) * 128, qs], ost[:], reads=[b_ost], writes=[bOT], sb=b_ost)
        C.close()

    def dl_attention():
        C = Ctx(nc, T)
        kbs = []
        for g in range(3):
            t, b = C.sb("kbl%d" % g, [128, NB], F32)
            T.dma("sp", t[:], kbdl[g], reads=[bIN], writes=[b], sb=b)
            kbs.append((t, b))
        raw_r = C.rot("raw", 3, [128, 2048], BF16)
        Qs, b_Qs = C.sb("Qs", [128, S], BF16)
        Ks, b_Ks = C.sb("Ks", [128, S], BF16)
        vs_r = C.rot("Vs", 2, [128, NB, 128], BF16)
        acc, b_acc = C.sb("acc", [128, S], F32)
        acs, b_acs = C.sb("acs", [128, S], F32)
        oo, b_oo = C.sb("oo", [128, S], BF16)
        e_r = C.rot("e", 4, [128, 256], BF16)
        m_r = C.rot("m", 4, [128, 256], BF16)
        sc_r = C.rot("sc", 4, [128, 256], F32, psum=True)
        po_r = C.rot("po", 2, [128, 256], F32, psum=True)
        pq_r = C.rot("pq", 2, [128, 256], F32, psum=True)
        cnt = 0
        for hp in range(8):
            T.op("dve", lambda: dve.memset(acc[:], 0.0), writes=[b_acc])
            T.op("pool", lambda: pool.memset(acs[:], 0.0), writes=[b_acs])
            for g in range(3):
                d = DIL[g]
                M = S // d
                MC = M // 128
                for which, (dstT, b_dst, srcT, b_src) in enumerate(((Qs, b_Qs, QT, bQT[g]), (Ks, b_Ks, KT, bKT[g]))):
                    dview = dstT[:].rearrange("p (r m) -> p r m", r=d)
                    for pc in range(S // 2048):
                        raw, b_raw = raw_r.next()
                        T.dma("sp", raw[:], srcT[g, hp * 128:(hp + 1) * 128, pc * 2048:(pc + 1) * 2048],
                              reads=[b_src], writes=[b_raw], sb=b_raw)
                        ml = 2048 // d
                        en = "pool" if (cnt % 2 == 0) else "dve"
                        cnt += 1
                        eng = pool if en == "pool" else dve
                        T.op(en, lambda: eng.tensor_copy(out=dview[:, :, pc * ml:(pc + 1) * ml],
                                                         in_=raw[:].rearrange("p (m r) -> p r m", r=d)),
                             reads=[b_raw], writes=[b_dst])
                Vs, b_Vs = vs_r.next()
                vsv = Vs[:].rearrange("p (r c) f -> p r c f", r=d)
                vsrc = VV[g, :, hp * 128:(hp + 1) * 128].rearrange("(m r) f -> r m f", r=d)
                for r in range(d):
                    T.dma("sp", vsv[:, r, :, :], vsrc[r].rearrange("(c p) f -> p c f", p=128),
                          reads=[bVV[g]], writes=[b_Vs], sb=b_Vs)
                kbt, b_kbt = kbs[g]
                Qv = Qs[:].rearrange("p (r m) -> p r m", r=d)
                Kv = Ks[:].rearrange("p (r m) -> p r m", r=d)
                accv = acc[:].rearrange("p (m r) -> p r m", r=d)
                acsv = acs[:].rearrange("p (m r) -> p r m", r=d)
                pend = None

                def stage_b(r, c, qlo, qhi, N, ms_):
                    po, b_po = po_r.next()
                    pq, b_pq = pq_r.next()
                    for m in range(2):
                        mm_, b_mm = ms_[m]
                        os_ = slice(m * 64, (m + 1) * 64)
                        T.op("pe", lambda: pe.matmul(po[os_, 0:N], lhsT=Vs[:, r * MC + c, m * 64:(m + 1) * 64], rhs=mm_[:, 0:N],
                                                     start=True, stop=True, tile_position=(0, m * 64)),
                             reads=[b_Vs, b_mm], writes=[b_po], sig=(m == 1))
                    for m in range(2):
                        mm_, b_mm = ms_[m]
                        os_ = slice(m * 64, (m + 1) * 64)
                        T.op("pe", lambda: pe.matmul(pq[os_, 0:N], lhsT=ones_bf[:, 0:64], rhs=mm_[:, 0:N],
                                                     start=True, stop=True, tile_position=(0, m * 64)),
                             reads=[b_ones_bf, b_mm], writes=[b_pq], sig=(m == 1))
                    T.op("dve", lambda: dve.tensor_tensor(out=accv[:, r, qlo:qhi], in0=accv[:, r, qlo:qhi], in1=po[:, 0:N], op=ALU.add),
                         reads=[b_po, b_acc], writes=[b_acc])
                    T.op("dve", lambda: dve.tensor_tensor(out=acsv[:, r, qlo:qhi], in0=acsv[:, r, qlo:qhi], in1=pq[:, 0:N], op=ALU.add),
                         reads=[b_pq, b_acs], writes=[b_acs])

                for r in range(d):
                    for c in range(MC):
                        qlo = max(0, 128 * c - 64)
                        qhi = min(M, 128 * c + 192)
                        N = qhi - qlo
                        mo = qlo - (128 * c - 64)
                        ms_ = []
                        for m in range(2):
                            ps_ = slice(m * 64, (m + 1) * 64)
                            sc, b_sc = sc_r.next()
                            T.op("pe", lambda: pe.matmul(sc[:, 0:N], lhsT=Kv[ps_, r, c * 128:(c + 1) * 128], rhs=Qv[ps_, r, qlo:qhi],
                                                         start=True, stop=True, tile_position=(m * 64, 0)),
                                 reads=[b_Ks, b_Qs], writes=[b_sc])
                            e, b_e = e_r.next()
                            T.op("act", lambda: act.activation(out=e[:, 0:N], in_=sc[:, 0:N], func=AF.Exp,
                                                               bias=kbt[:, r * MC + c:r * MC + c + 1], scale=0.125),
                                 reads=[b_sc, b_kbt], writes=[b_e])
                            mm_, b_mm = m_r.next()
                            en = "pool" if m == 0 else "dve"
                            eng = pool if m == 0 else dve
                            T.op(en, lambda: eng.tensor_tensor(out=mm_[:, 0:N], in0=e[:, 0:N], in1=bmask[:, mo:mo + N], op=ALU.mult),
                                 reads=[b_e, b_bmask], writes=[b_mm])
                            ms_.append((mm_, b_mm))
                        if pend is not None:
                            stage_b(*pend)
                        pend = (r, c, qlo, qhi, N, ms_)
                if pend is not None:
                    stage_b(*pend)
            T.op("dve", lambda: dve.tensor_scalar(out=acs[:], in0=acs[:], scalar1=1e-30, scalar2=None, op0=ALU.max),
                 reads=[b_acs], writes=[b_acs])
            T.op("dve", lambda: dve.reciprocal(out=acs[:], in_=acs[:]), reads=[b_acs], writes=[b_acs])
            T.op("dve", lambda: dve.tensor_tensor(out=oo[:], in0=acc[:], in1=acs[:], op=ALU.mult), reads=[b_acc, b_acs], writes=[b_oo])
            T.dma("sp", OT[hp * 128:(hp + 1) * 128, :], oo[:], reads=[b_oo], writes=[bOT], sb=b_oo)
        C.close()

    def tok_phase(i, x_src, b_xsrc, wo_ap, x_dst, b_xdst):
        C = Ctx(nc, T)
        wo, b_wo = C.sb("wo", [128, 8, D], BF16)
        T.dma("pool", wo[:], wo_ap.rearrange("(c p) f -> p c f", p=128), reads=[bIN], writes=[b_wo], sb=b_wo)
        wr, b_wr = C.sb("wr", [128, 8, 20], BF16)
        T.dma("pool", wr[:], w_r[i].rearrange("(c p) f -> p c f", p=128), reads=[bIN], writes=[b_wr], sb=b_wr)
        lnb = []
        for k in range(4):
            t, b = C.sb("lnb%d" % k, [128, D], F32)
            T.dma("sp", t[:], lnp[i, k, :].partition_broadcast(128), reads=[bIN], writes=[b], sb=b)
            lnb.append((t, b))
        NH = 2
        TM = NH * 512
        NS = NH * 4
        oT_r = C.rot("oT", 2, [128, 8, 512], BF16)
        xt_r = C.rot("xt", 1, [128, 4, D], F32)
        ya, b_ya = C.sb("ya", [128, NS, D], F32)
        tmp, b_tmp = C.sb("tmp", [128, D], F32)
        tmp2, b_tmp2 = C.sb("tmp2", [128, D], F32)
        xa, b_xa = C.sb("xa", [128, 4, TM], BF16)
        x1b, b_x1b = xa, b_xa
        x1T, b_x1T = C.sb("x1T", [128, 8, TM], BF16)
        st_, b_st = C.sb("stt", [128, 8], F32)
        rt, b_rt = C.sb("rt", [128, 64], F32)
        cw, b_cw = C.sb("cw", [128, NS, 16], F32)
        cwT, b_cwT = C.sb("cwT", [16, TM], BF16)
        cwb_r = C.rot("cwb", 2, [128, TM], F32)
        wg_r = C.rot("wg", 2, [128, 8, 512], BF16)
        wu_r = C.rot("wu", 2, [128, 8, 512], BF16)
        wd_r = C.rot("wd", 2, [128, 4, D], BF16)
        sg_r = C.rot("sg", 1, [128, 512], F32)
        tt_r = C.rot("tt", 1, [128, 512], F32)
        aT_r = Rot([(xa, b_xa)])
        xo_r = C.rot("xo", 1, [128, D], F32)
        ph_r = C.rot("ph", 2, [128, 512], F32, psum=True)
        tp_r = C.rot("tp", 1, [128, 512], BF16, psum=True)
        pg_r = C.rot("pg", 2, [128, 512], F32, psum=True)
        pu_r = C.rot("pu", 2, [128, 512], F32, psum=True)
        px, b_px = C.ps("px", [128, 512])

        def layer_norm(src_ap, b_src, gk, out_ap, b_out, extra_writes=()):
            g_, bg = lnb[gk]
            be_, bbe = lnb[gk + 1]
            T.op("dve", lambda: dve.tensor_reduce(out=st_[:, 0:1], in_=src_ap, axis=AX.X, op=ALU.add), reads=[b_src], writes=[b_st])
            T.op("pool", lambda: pool.tensor_tensor(out=tmp2[:], in0=src_ap, in1=src_ap, op=ALU.mult), reads=[b_src], writes=[b_tmp2])
            T.op("dve", lambda: dve.tensor_reduce(out=st_[:, 1:2], in_=tmp2[:], axis=AX.X, op=ALU.add), reads=[b_tmp2], writes=[b_st])
            T.op("dve", lambda: dve.tensor_scalar(out=st_[:, 2:4], in0=st_[:, 0:2], scalar1=1.0 / D, scalar2=None, op0=ALU.mult),
                 reads=[b_st], writes=[b_st])
            T.op("dve", lambda: dve.tensor_tensor(out=st_[:, 4:5], in0=st_[:, 2:3], in1=st_[:, 2:3], op=ALU.mult), reads=[b_st], writes=[b_st])
            T.op("dve", lambda: dve.tensor_tensor(out=st_[:, 5:6], in0=st_[:, 3:4], in1=st_[:, 4:5], op=ALU.subtract), reads=[b_st], writes=[b_st])
            T.op("dve", lambda: dve.tensor_scalar(out=st_[:, 6:7], in0=st_[:, 5:6], scalar1=LN_EPS, scalar2=None, op0=ALU.add),
                 reads=[b_st], writes=[b_st])
            T.op("act", lambda: act.activation(out=st_[:, 7:8], in_=st_[:, 6:7], func=AF.Ln), reads=[b_st], writes=[b_st])
            T.op("act", lambda: act.activation(out=st_[:, 6:7], in_=st_[:, 7:8], func=AF.Exp, scale=-0.5), reads=[b_st], writes=[b_st])
            T.op("dve", lambda: dve.tensor_scalar(out=tmp[:], in0=src_ap, scalar1=st_[:, 2:3], scalar2=st_[:, 6:7],
                                                  op0=ALU.subtract, op1=ALU.mult), reads=[b_src, b_st], writes=[b_tmp])
            T.op("pool", lambda: pool.tensor_tensor(out=tmp[:], in0=tmp[:], in1=g_[:], op=ALU.mult), reads=[b_tmp, bg], writes=[b_tmp])
            T.op("dve", lambda: dve.tensor_tensor(out=out_ap, in0=tmp[:], in1=be_[:], op=ALU.add), reads=[b_tmp, bbe],
                 writes=[b_out] + list(extra_writes))

        def load(ti):
            oT, b_oT = oT_r.next()
            T.dma("sp", oT[:], OT[:, ti * 512:(ti + 1) * 512].rearrange("(c p) t -> p c t", p=128), reads=[bOT], writes=[b_oT], sb=b_oT)
            return oT, b_oT

        def load_x(ti):
            xt, b_xt = xt_r.next()
            T.dma("sp", xt[:], x_src[ti * 512:(ti + 1) * 512, :].rearrange("(s p) d -> p s d", p=128),
                  reads=[b_xsrc], writes=[b_xt], sb=b_xt)
            return xt, b_xt

        def load_w(e):
            wg, b_wg = wg_r.next()
            wu, b_wu = wu_r.next()
            wd, b_wd = wd_r.next()
            wb_ = WB[i % 2]
            T.dma("sp", wg[:], wb_[0][e].rearrange("(c p) f -> p c f", p=128), reads=[bWB[i % 2]], writes=[b_wg], sb=b_wg)
            T.dma("sp", wu[:], wb_[1][e].rearrange("(c p) f -> p c f", p=128), reads=[bWB[i % 2]], writes=[b_wu], sb=b_wu)
            T.dma("sp", wd[:], wb_[2][e].rearrange("(c p) f -> p c f", p=128), reads=[bWB[i % 2]], writes=[b_wd], sb=b_wd)
            return wg, b_wg, wu, b_wu, wd, b_wd

        nxt = load(0)
        nw = load_w(0)
        TOK_DBG = int(os.environ.get("TOK_DBG", "0"))
        for sti in range(NT // NH):
            for hh in range(NH):
                ti = sti * NH + hh
                oT, b_oT = nxt
                xt, b_xt = load_x(ti)
                if ti + 1 < NT:
                    nxt = load(ti + 1)
                for s in range(4):
                    sg_ = hh * 4 + s
                    for hf in range(2):
                        ph, b_ph = ph_r.next()
                        for fc in range(8):
                            T.op("pe", lambda: pe.matmul(ph[:], lhsT=oT[:, fc, s * 128:(s + 1) * 128], rhs=wo[:, fc, hf * 512:(hf + 1) * 512],
                                                         start=(fc == 0), stop=(fc == 7)),
                                 reads=[b_oT, b_wo], writes=[b_ph], sig=(fc == 7))
                        T.op("dve", lambda: dve.scalar_tensor_tensor(out=ya[:, sg_, hf * 512:(hf + 1) * 512], in0=xt[:, s, hf * 512:(hf + 1) * 512],
                                                                     scalar=ALPHA, in1=ph[:], op0=ALU.mult, op1=ALU.add),
                             reads=[b_xt, b_ph], writes=[b_ya])
                    layer_norm(ya[:, sg_, :], b_ya, 0, ya[:, sg_, :], b_ya)
                    T.op("act", lambda: act.copy(out=x1b[:, s, :], in_=ya[:, sg_, :]), reads=[b_ya], writes=[b_x1b])
                for kc in range(8):
                    tp, b_tp = tp_r.next()
                    for s in range(4):
                        T.op("pe", lambda: pe.transpose(out=tp[:, s * 128:(s + 1) * 128], in_=x1b[:, s, kc * 128:(kc + 1) * 128],
                                                        identity=ident_bf[:]),
                             reads=[b_x1b, b_ident_bf], writes=[b_tp], sig=(s == 3))
                    T.op("act", lambda: act.copy(out=x1T[:, kc, hh * 512:(hh + 1) * 512], in_=tp[:]), reads=[b_tp], writes=[b_x1T])
                for s in range(4):
                    sg_ = hh * 4 + s
                    for kc in range(8):
                        T.op("pe", lambda: pe.matmul(px[:, 0:20], lhsT=x1T[:, kc, hh * 512 + s * 128:hh * 512 + (s + 1) * 128], rhs=wr[:, kc, :],
                                                     start=(kc == 0), stop=(kc == 7)),
                             reads=[b_x1T, b_wr], writes=[b_px], sig=(kc == 7))
                    R = lambda a, b: rt[:, a:b]
                    V = lambda fn: T.op("dve", fn, reads=[b_rt], writes=[b_rt])
                    T.op("dve", lambda: dve.tensor_copy(out=R(0, 20), in_=px[:, 0:20]), reads=[b_px], writes=[b_rt])
                    V(lambda: dve.tensor_reduce(out=R(20, 21), in_=R(0, 4), axis=AX.X, op=ALU.max))
                    V(lambda: dve.tensor_scalar(out=R(21, 25), in0=R(0, 4), scalar1=R(20, 21), scalar2=None, op0=ALU.is_equal))
                    V(lambda: dve.tensor_scalar(out=R(25, 26), in0=R(20, 21), scalar1=-1.0, scalar2=None, op0=ALU.mult))
                    T.op("act", lambda: act.activation(out=R(26, 30), in_=R(0, 4), func=AF.Exp, bias=R(25, 26), scale=1.0),
                         reads=[b_rt], writes=[b_rt])
                    V(lambda: dve.tensor_reduce(out=R(30, 31), in_=R(26, 30), axis=AX.X, op=ALU.add))
                    V(lambda: dve.reciprocal(out=R(31, 32), in_=R(30, 31)))
                    V(lambda: dve.tensor_scalar(out=R(32, 36), in0=R(4, 8), scalar1=R(21, 22), scalar2=None, op0=ALU.mult))
                    for gg in range(1, 4):
                        V(lambda: dve.scalar_tensor_tensor(out=R(32, 36), in0=R(4 + 4 * gg, 8 + 4 * gg), scalar=R(21 + gg, 22 + gg),
                                                           in1=R(32, 36), op0=ALU.mult, op1=ALU.add))
                    V(lambda: dve.tensor_reduce(out=R(36, 37), in_=R(32, 36), axis=AX.X, op=ALU.max))
                    V(lambda: dve.tensor_scalar(out=R(37, 41), in0=R(32, 36), scalar1=R(36, 37), scalar2=None, op0=ALU.is_equal))
                    V(lambda: dve.scalar_tensor_tensor(out=R(41, 45), in0=R(37, 41), scalar=-1e30, in1=R(32, 36), op0=ALU.mult, op1=ALU.add))
                    V(lambda: dve.tensor_reduce(out=R(45, 46), in_=R(41, 45), axis=AX.X, op=ALU.max))
                    V(lambda: dve.tensor_scalar(out=R(46, 50), in0=R(41, 45), scalar1=R(45, 46), scalar2=None, op0=ALU.is_equal))
                    V(lambda: dve.tensor_tensor(out=R(50, 51), in0=R(45, 46), in1=R(36, 37), op=ALU.subtract))
                    T.op("act", lambda: act.activation(out=R(51, 52), in_=R(50, 51), func=AF.Exp), reads=[b_rt], writes=[b_rt])
                    V(lambda: dve.tensor_scalar(out=R(52, 53), in0=R(51, 52), scalar1=1.0, scalar2=None, op0=ALU.add))
                    V(lambda: dve.reciprocal(out=R(53, 54), in_=R(52, 53)))
                    V(lambda: dve.tensor_tensor(out=R(54, 55), in0=R(53, 54), in1=R(31, 32), op=ALU.mult))
                    V(lambda: dve.tensor_tensor(out=R(55, 56), in0=R(31, 32), in1=R(54, 55), op=ALU.subtract))
                    V(lambda: dve.tensor_scalar(out=R(56, 60), in0=R(37, 41), scalar1=R(54, 55), scalar2=None, op0=ALU.mult))
                    V(lambda: dve.scalar_tensor_tensor(out=R(56, 60), in0=R(46, 50), scalar=R(55, 56), in1=R(56, 60), op0=ALU.mult, op1=ALU.add))
                    for gg in range(4):
                        T.op("dve", lambda: dve.tensor_scalar(out=cw[:, sg_, gg * 4:(gg + 1) * 4], in0=R(56, 60), scalar1=R(21 + gg, 22 + gg),
                                                              scalar2=None, op0=ALU.mult), reads=[b_rt], writes=[b_cw])
                for s in range(4):
                    T.op("pe", lambda: pe.transpose(out=px[0:16, s * 128:(s + 1) * 128], in_=cw[:, hh * 4 + s, :], identity=ident_f[:]),
                         reads=[b_cw, b_ident_f], writes=[b_px], sig=(s == 3))
                T.op("dve", lambda: dve.tensor_copy(out=cwT[:, hh * 512:(hh + 1) * 512], in_=px[0:16, :]), reads=[b_px], writes=[b_cwT])
            if TOK_DBG == 1:
                for s in range(NS):
                    T.dma("sp", x_dst[sti * TM + s * 128:sti * TM + (s + 1) * 128, :], ya[:, s, :], reads=[b_ya], writes=[b_xdst], sb=b_ya)
                continue
            T.op("pool", lambda: pool.tensor_scalar(out=ya[:], in0=ya[:], scalar1=ALPHA, scalar2=None, op0=ALU.mult),
                 reads=[b_ya], writes=[b_ya])
            for e in range(16):
                wg, b_wg, wu, b_wu, wd, b_wd = nw
                if not (sti == NT // NH - 1 and e == 15):
                    nw = load_w((e + 1) % 16)
                cwb, b_cwb = cwb_r.next()
                for hh in range(NH):
                    T.op("pe", lambda: pe.matmul(px[:], lhsT=sel[0:16, e * 128:(e + 1) * 128], rhs=cwT[0:16, hh * 512:(hh + 1) * 512],
                                                 start=True, stop=True), reads=[b_sel, b_cwT], writes=[b_px])
                    T.op("act", lambda: act.copy(out=cwb[:, hh * 512:(hh + 1) * 512], in_=px[:]), reads=[b_px], writes=[b_cwb])
                aT, b_aT = aT_r.next()
                for fcn in range(4):
                    for hh in range(NH):
                        hs = slice(hh * 512, (hh + 1) * 512)
                        pg, b_pg = pg_r.next()
                        pu, b_pu = pu_r.next()
                        for kc in range(8):
                            T.op("pe", lambda: pe.matmul(pg[:], lhsT=wg[:, kc, fcn * 128:(fcn + 1) * 128], rhs=x1T[:, kc, hs],
                                                         start=(kc == 0), stop=(kc == 7)), reads=[b_wg, b_x1T], writes=[b_pg], sig=(kc == 7))
                        for kc in range(8):
                            T.op("pe", lambda: pe.matmul(pu[:], lhsT=wu[:, kc, fcn * 128:(fcn + 1) * 128], rhs=x1T[:, kc, hs],
                                                         start=(kc == 0), stop=(kc == 7)), reads=[b_wu, b_x1T], writes=[b_pu], sig=(kc == 7))
                        sg, b_sg = sg_r.next()
                        tt, b_tt = tt_r.next()
                        T.op("act", lambda: act.activation(out=sg[:], in_=pg[:], func=AF.Silu), reads=[b_pg], writes=[b_sg])
                        T.op("dve", lambda: dve.tensor_tensor(out=tt[:], in0=pu[:], in1=cwb[:, hs], op=ALU.mult), reads=[b_pu, b_cwb], writes=[b_tt])
                        T.op("pool", lambda: pool.tensor_tensor(out=aT[:, fcn, hs], in0=sg[:], in1=tt[:], op=ALU.mult),
                             reads=[b_sg, b_tt], writes=[b_aT])
                for s in range(NS):
                    for hf in range(2):
                        ph, b_ph = ph_r.next()
                        for fcn in range(4):
                            T.op("pe", lambda: pe.matmul(ph[:], lhsT=aT[:, fcn, s * 128:(s + 1) * 128], rhs=wd[:, fcn, hf * 512:(hf + 1) * 512],
                                                         start=(fcn == 0), stop=(fcn == 3)), reads=[b_aT, b_wd], writes=[b_ph], sig=(fcn == 3))
                        T.op("dve", lambda: dve.tensor_tensor(out=ya[:, s, hf * 512:(hf + 1) * 512], in0=ya[:, s, hf * 512:(hf + 1) * 512],
                                                              in1=ph[:], op=ALU.add), reads=[b_ph, b_ya], writes=[b_ya])
            if TOK_DBG == 2:
                for s in range(NS):
                    T.dma("sp", x_dst[sti * TM + s * 128:sti * TM + (s + 1) * 128, :], ya[:, s, :], reads=[b_ya], writes=[b_xdst], sb=b_ya)
                continue
            for s in range(NS):
                xo, b_xo = xo_r.next()
                layer_norm(ya[:, s, :], b_ya, 2, xo[:], b_xo)
                T.dma("sp", x_dst[sti * TM + s * 128:sti * TM + (s + 1) * 128, :], xo[:], reads=[b_xo], writes=[b_xdst], sb=b_xo)
        C.close()

    cur, b_cur = x_in, bIN
    for i in layers:
        j = i // 2
        last = (i == layers[-1])
        dst, b_dst = (y_out, Buf("yout")) if last else (XR[i % 2], bXR[i % 2])
        if i % 2 == 0:
            lambda_init = 0.8 - 0.6 * math.exp(-0.3 * i)
            proj_pass(cur, b_cur, da_wm[j], da_wp[j], 0)
            convert_weights(i)
            if KSTOP >= 2:
                da_attention(j, lambda_init)
            if KSTOP >= 3:
                tok_phase(i, cur, b_cur, da_wo[j], dst, b_dst)
        else:
            for g in range(3):
                proj_pass(cur, b_cur, dl_wm[j, g], dl_wp[j, g], g)
            convert_weights(i)
            dl_attention()
            tok_phase(i, cur, b_cur, dl_wo[j], dst, b_dst)
        cur, b_cur = dst, b_dst
    T.barrier()
    G.es.close()
    return nc, T


def _rope_tables(S):
    half = 8
    inv = (ROPE_THETA ** (-np.arange(half, dtype=np.float32) * np.float32(2.0 / 16))).astype(np.float32)
    ang = np.arange(S, dtype=np.float32)[:, None] * inv[None, :]
    cos = np.cos(ang).astype(np.float32).T
    sin = np.sin(ang).astype(np.float32).T
    C = np.ones((128, S), np.float32)
    Sg = np.zeros((128, S), np.float32)
    for hh in range(2):
        b = hh * 64
        C[b:b + 8] = cos
        C[b + 8:b + 16] = cos
        Sg[b:b + 8] = -sin
        Sg[b + 8:b + 16] = sin
    return C, Sg


def _partner_perm(ncols):
    idx = np.arange(ncols)
    i = idx % 64
    p = idx.copy()
    p[i < 8] = idx[i < 8] + 8
    m = (i >= 8) & (i < 16)
    p[m] = idx[m] - 8
    return p


def prep_shared(inp, S):
    f = lambda a: np.ascontiguousarray(np.asarray(a, dtype=np.float32))
    da_w_in = f(inp["da_w_in"])
    hidx = np.arange(8)[:, None, None]
    midx = np.arange(2)[None, :, None]
    didx = np.arange(64)[None, None, :]
    qcols = (midx * 512 + hidx * 64 + didx).reshape(-1)
    kcols = qcols + 1024
    pp = _partner_perm(1024)
    da_wm = np.concatenate([da_w_in[:, :, qcols], da_w_in[:, :, kcols], da_w_in[:, :, 2048:]], axis=2)
    da_wp = np.concatenate([da_w_in[:, :, qcols[pp]], da_w_in[:, :, kcols[pp]]], axis=2)
    dl = f(inp["dl_w_in"]).reshape(2, D, 3, 3, 1024)
    dl_wm = np.ascontiguousarray(dl.transpose(0, 2, 1, 3, 4).reshape(2, 3, D, 3072))
    dlq = dl[:, :, :, 0, :][..., pp]
    dlk = dl[:, :, :, 1, :][..., pp]
    dl_wp = np.ascontiguousarray(np.concatenate([dlq, dlk], axis=-1).transpose(0, 2, 1, 3))
    C, Sg = _rope_tables(S)
    ii = np.arange(128)[:, None]
    jj = np.arange(256)[None, :]
    bmask = ((ii >= jj - 128) & (ii <= jj)).astype(np.float32)
    sel = np.zeros((16, 16, 128), np.float32)
    for e in range(16):
        sel[e, e, :] = 1.0
    sh = {
        "ropec": C, "ropes": Sg, "ident": np.eye(128, dtype=np.float32), "bmask": bmask,
        "sel": sel.reshape(16, 16 * 128),
        "da_wm": np.ascontiguousarray(da_wm), "da_wp": np.ascontiguousarray(da_wp), "da_wo": f(inp["da_w_out"]),
        "da_lam": np.ascontiguousarray(np.stack([f(inp["da_lambda_q1"]), f(inp["da_lambda_k1"]),
                                                 f(inp["da_lambda_q2"]), f(inp["da_lambda_k2"])], axis=1)),
        "da_g": f(inp["da_subln_g"]),
        "dl_wm": dl_wm, "dl_wp": dl_wp, "dl_wo": f(inp["dl_w_out"]),
        "lnp": np.ascontiguousarray(np.stack([f(inp["ln1_g"]), f(inp["ln1_b"]), f(inp["ln2_g"]), f(inp["ln2_b"])], axis=1)),
        "w_r": np.ascontiguousarray(np.concatenate([f(inp["moe_router_group"]),
                                                    f(inp["moe_router_expert"]).reshape(4, D, 16)], axis=2)),
        "w_gate": f(inp["moe_w_gate"]).reshape(4, 16, D, 512),
        "w_up": f(inp["moe_w_up"]).reshape(4, 16, D, 512),
        "w_down": f(inp["moe_w_down"]).reshape(4, 16, 512, D),
    }
    return sh


def prep_core(xseq, S):
    L = xseq.shape[0]
    xp = np.zeros((S, D), np.float32)
    xp[:L] = xseq
    kb = np.zeros((S,), np.float32)
    kb[L:] = NEG
    NB = S // 128
    m = {"x": xp, "kbda": np.ascontiguousarray(kb.reshape(NB, 128).T)}
    for g, d in enumerate(DIL):
        M = S // d
        MC = M // 128
        t = kb.reshape(MC, 128, d)
        m["kbdl%d" % g] = np.ascontiguousarray(t.transpose(1, 2, 0).reshape(128, d * MC))
    return m


_CACHE = {}


def kernel(**inputs):
    S = 8192
    xp = np.asarray(inputs["x_prompt"], dtype=np.float32)
    xs = np.asarray(inputs["x_sample"], dtype=np.float32)
    sh = prep_shared(inputs, S)
    seqs = [xp[b] for b in range(4)] + [xs[b] for b in range(4)]
    in_maps = []
    for c in range(8):
        m = dict(sh)
        m.update(prep_core(seqs[c], S))
        in_maps.append(m)
    if "nc" not in _CACHE:
        _CACHE["nc"] = build_program(S, [0, 1, 2, 3])[0]
    nc = _CACHE["nc"]
    res = run_bass_kernel_spmd(nc, in_maps, core_ids=list(range(8)))
    ys = [np.asarray(r["y"], dtype=np.float32) for r in res.results]
    y_prompt = np.stack([ys[b] for b in range(4)], axis=0)
    y_sample = np.stack([ys[4 + b][:xs.shape[1]] for b in range(4)], axis=0)
    return (y_prompt, y_sample)
```

```python
import math
from contextlib import ExitStack
import numpy as np
import concourse.bass as bass
import concourse.mybir as mybir
from concourse.bass_utils import run_bass_kernel_spmd

F32 = mybir.dt.float32
BF16 = mybir.dt.bfloat16
AF = mybir.ActivationFunctionType
ALU = mybir.AluOpType
AX = mybir.AxisListType

D = 1024
DEPTH = 4
LN_EPS = 1e-5
ALPHA = (2.0 * DEPTH) ** 0.25
ROPE_THETA = 500000.0
DIL = (1, 4, 16)
NEG = -30000.0
import os
KSTOP = int(os.environ.get('KSTOP', '3'))
DA_FIN = int(os.environ.get('DA_FIN', '1'))
DA_LOOP = int(os.environ.get('DA_LOOP', '1'))


class Buf:
    __slots__ = ("name", "w", "r", "ds", "dq")

    def __init__(self, name):
        self.name = name
        self.w = {}
        self.r = {}
        self.ds = None
        self.dq = None


class Trk:
    def __init__(self, nc):
        self.nc = nc
        self.eng = {"pe": nc.tensor, "act": nc.scalar, "dve": nc.vector, "pool": nc.gpsimd, "sp": nc.sync}
        self.esem = {k: nc.alloc_semaphore(name="e_" + k) for k in ("pe", "act", "dve", "pool")}
        self.ecnt = {k: 0 for k in self.esem}
        self.pending = {k: [] for k in self.esem}
        self.waited = {e: {} for e in self.eng}
        self.dsems = []
        self.free_ds = {"sp": [], "pool": []}
        self.ninstr = 0

    def get_ds(self, q):
        if self.free_ds[q]:
            return self.free_ds[q].pop()
        h = self.nc.alloc_semaphore(name="d%s%d" % (q, len(self.dsems)))
        self.dsems.append([h, 0])
        return len(self.dsems) - 1

    def release_ds(self, bufs):
        for b in bufs:
            if b.ds is not None:
                self.free_ds[b.dq].append(b.ds)
                b.ds = None
                b.dq = None

    def _semval(self, key, tok):
        if key[0] == "E":
            assert tok[1] is not None, "dependency on unsignaled instr of %s" % key[1]
            return self.esem[key[1]], tok[1]
        h, tot = self.dsems[key[1]]
        return h, tot

    def _waits(self, e, deps, skip_same, raw=None):
        for key, tok in deps.items():
            if skip_same and key == ("E", e):
                if e == "pe" or raw is None or key not in raw:
                    continue
                tok = raw[key]
            h, val = self._semval(key, tok)
            if self.waited[e].get(key, 0) >= val:
                continue
            self.eng[e].wait_ge(h, val)
            self.waited[e][key] = val
            self.ninstr += 1

    @staticmethod
    def _collect(reads, writes):
        deps = {}

        def add(k, t):
            o = deps.get(k)
            if o is None or t[1] is None or (o[1] is not None and t[1] > o[1]):
                deps[k] = t

        for b in reads:
            for k, t in b.w.items():
                add(k, t)
        for b in writes:
            for k, t in b.w.items():
                add(k, t)
            for k, t in b.r.items():
                add(k, t)
        return deps

    @staticmethod
    def _record(key, tok, reads, writes):
        for b in reads:
            o = b.r.get(key)
            if o is None or tok[1] is None or (o[1] is not None and tok[1] >= o[1]):
                b.r[key] = tok
        for b in writes:
            b.w = {key: tok}
            b.r = {}

    def op(self, e, fn, reads=(), writes=(), sig=True):
        deps = self._collect(reads, writes)
        raw = self._collect(reads, ())
        self._waits(e, deps, True, raw)
        ins = fn()
        self.ninstr += 1
        key = ("E", e)
        tok = [key, None]
        if sig:
            self.ecnt[e] += 1
            ins.then_inc(self.esem[e], 1)
            tok[1] = self.ecnt[e]
            for t in self.pending[e]:
                t[1] = self.ecnt[e]
            self.pending[e] = []
        else:
            self.pending[e].append(tok)
        self._record(key, tok, reads, writes)
        return ins

    def dma(self, q, out, in_, reads=(), writes=(), sb=None):
        deps = self._collect(reads, writes)
        self._waits(q, deps, False)
        if sb.ds is None:
            sb.ds = self.get_ds(q)
            sb.dq = q
        assert sb.dq == q, "buffer %s used with two DMA queues" % sb.name
        ins = self.eng[q].dma_start(out=out, in_=in_)
        self.ninstr += 1
        d = self.dsems[sb.ds]
        d[1] += 16
        ins.then_inc(d[0], 16)
        key = ("D", sb.ds)
        tok = [key, d[1]]
        self._record(key, tok, reads, writes)
        return ins

    def barrier(self):
        for e in self.eng:
            assert not self.pending.get(e, [])
        for e in self.eng:
            for k in self.esem:
                if k == e:
                    continue
                key = ("E", k)
                val = self.ecnt[k]
                if val > 0 and self.waited[e].get(key, 0) < val:
                    self.eng[e].wait_ge(self.esem[k], val)
                    self.waited[e][key] = val
            for i, (h, tot) in enumerate(self.dsems):
                key = ("D", i)
                if tot > 0 and self.waited[e].get(key, 0) < tot:
                    self.eng[e].wait_ge(h, tot)
                    self.waited[e][key] = tot


class Rot:
    def __init__(self, items):
        self.items = items
        self.i = 0

    def next(self):
        it = self.items[self.i % len(self.items)]
        self.i += 1
        return it


class Ctx:
    def __init__(self, nc, trk):
        self.nc = nc
        self.trk = trk
        self.es = ExitStack()
        self.bufs = []

    _uid = [0]

    def sb(self, name, shape, dt):
        Ctx._uid[0] += 1
        name = "s%d_%s" % (Ctx._uid[0], name)
        t = self.es.enter_context(self.nc.sbuf_tensor(name, list(shape), dt))
        b = Buf(name)
        self.bufs.append(b)
        return t, b

    def ps(self, name, shape, dt=F32):
        Ctx._uid[0] += 1
        name = "p%d_%s" % (Ctx._uid[0], name)
        t = self.es.enter_context(self.nc.psum_tensor(name, list(shape), dt))
        b = Buf(name)
        self.bufs.append(b)
        return t, b

    def rot(self, name, n, shape, dt, psum=False):
        f = self.ps if psum else self.sb
        return Rot([f("%s%d" % (name, i), shape, dt) for i in range(n)])

    def close(self):
        self.trk.barrier()
        self.trk.release_ds(self.bufs)
        self.es.close()


def build_program(S, layers, nsub_tok=4):
    assert S % 2048 == 0
    NT = S // 512
    NB = S // 128
    nc = bass.Bass("TRN2", target_bir_lowering=False)
    dt_in = lambda name, shape: nc.dram_tensor(name, list(shape), F32, kind="ExternalInput").ap()
    x_in = dt_in("x", [S, D])
    kbda = dt_in("kbda", [128, NB])
    kbdl = [dt_in("kbdl%d" % g, [128, NB]) for g in range(3)]
    rc_in = dt_in("ropec", [128, S])
    rs_in = dt_in("ropes", [128, S])
    ident_in = dt_in("ident", [128, 128])
    mask_in = dt_in("bmask", [128, 256])
    sel_in = dt_in("sel", [16, 16 * 128])
    da_wm = dt_in("da_wm", [2, D, 3072])
    da_wp = dt_in("da_wp", [2, D, 2048])
    da_wo = dt_in("da_wo", [2, D, D])
    da_lam = dt_in("da_lam", [2, 4, 64])
    da_g = dt_in("da_g", [2, 128])
    dl_wm = dt_in("dl_wm", [2, 3, D, 3072])
    dl_wp = dt_in("dl_wp", [2, 3, D, 2048])
    dl_wo = dt_in("dl_wo", [2, D, D])
    lnp = dt_in("lnp", [4, 4, D])
    w_r = dt_in("w_r", [4, D, 20])
    w_gate = dt_in("w_gate", [4, 16, D, 512])
    w_up = dt_in("w_up", [4, 16, D, 512])
    w_down = dt_in("w_down", [4, 16, 512, D])
    y_out = nc.dram_tensor("y", [S, D], F32, kind="ExternalOutput").ap()
    KDBG = int(os.environ.get("KDBG", "0"))
    scr = lambda name, shape, dt: nc.dram_tensor(name, list(shape), dt, kind=("ExternalOutput" if KDBG else "Internal")).ap()
    QT = scr("QT", [3, D, S], BF16)
    KT = scr("KT", [3, D, S], BF16)
    VV = scr("VV", [3, S, D], BF16)
    OT = scr("OT", [D, S], BF16)
    XR = [scr("XR0", [S, D], F32), scr("XR1", [S, D], F32)]
    WB = [(scr("WBg%d" % p_, [16, D, 512], BF16), scr("WBu%d" % p_, [16, D, 512], BF16), scr("WBd%d" % p_, [16, 512, D], BF16))
          for p_ in range(1)]
    WB = [WB[0], WB[0]]
    _b = Buf("WB0")
    bWB = [_b, _b]
    b_wconv = Buf("wconv")
    bQT = [Buf("QT%d" % g) for g in range(3)]
    bKT = [Buf("KT%d" % g) for g in range(3)]
    bVV = [Buf("VV%d" % g) for g in range(3)]
    bOT = Buf("OT")
    bXR = [Buf("XR0"), Buf("XR1")]
    bIN = Buf("inputs")

    T = Trk(nc)
    E = T.eng
    pe, act, dve, pool = nc.tensor, nc.scalar, nc.vector, nc.gpsimd

    G = Ctx(nc, T)
    ident_bf, b_ident_bf = G.sb("ident_bf", [128, 128], BF16)
    ident_f, b_ident_f = G.sb("ident_f", [128, 128], F32)
    ones_bf, b_ones_bf = G.sb("ones_bf", [128, 128], BF16)
    ones_f, b_ones_f = G.sb("ones_f", [128, 128], F32)
    bmask, b_bmask = G.sb("bmask", [128, 256], BF16)
    sel, b_sel = G.sb("sel", [16, 16 * 128], BF16)
    T.dma("pool", ident_bf[:], ident_in, reads=[bIN], writes=[b_ident_bf], sb=b_ident_bf)
    T.dma("sp", ident_f[:], ident_in, reads=[bIN], writes=[b_ident_f], sb=b_ident_f)
    T.dma("pool", bmask[:], mask_in, reads=[bIN], writes=[b_bmask], sb=b_bmask)
    T.dma("pool", sel[:], sel_in, reads=[bIN], writes=[b_sel], sb=b_sel)
    T.op("dve", lambda: dve.memset(ones_bf[:], 1.0), writes=[b_ones_bf])
    T.op("dve", lambda: dve.memset(ones_f[:], 1.0), writes=[b_ones_f])

    def convert_weights(i):
        p_ = i % 2
        for e in range(16):
            T.dma("pool", WB[p_][0][e], w_gate[i, e], reads=[bIN], writes=[bWB[p_]], sb=b_wconv)
            T.dma("pool", WB[p_][1][e], w_up[i, e], reads=[bIN], writes=[bWB[p_]], sb=b_wconv)
            T.dma("pool", WB[p_][2][e], w_down[i, e], reads=[bIN], writes=[bWB[p_]], sb=b_wconv)

    def proj_pass(x_src, b_xsrc, wm_ap, wp_ap, g):
        C = Ctx(nc, T)
        wm, b_wm = C.sb("wm", [128, 8, 3072], BF16)
        wp, b_wp = C.sb("wp", [128, 8, 2048], BF16)
        for kc in range(8):
            T.dma("pool", wm[:, kc, :], wm_ap[kc * 128:(kc + 1) * 128, :], reads=[bIN], writes=[b_wm], sb=b_wm)
            T.dma("pool", wp[:, kc, :], wp_ap[kc * 128:(kc + 1) * 128, :], reads=[bIN], writes=[b_wp], sb=b_wp)
        xt_r = C.rot("xt", 2, [128, 4, D], F32)
        xb_r = C.rot("xb", 1, [128, 4, D], BF16)
        xT_r = C.rot("xT", 2, [128, 8, 512], BF16)
        rc_r = C.rot("rc", 2, [128, 512], F32)
        rs_r = C.rot("rs", 2, [128, 512], F32)
        t1_r = C.rot("t1", 2, [128, 512], F32)
        t2_r = C.rot("t2", 2, [128, 512], F32)
        st_r = C.rot("st", 3, [128, 512], BF16)
        vst_r = C.rot("vst", 2, [128, 4, D], BF16)
        tp_r = C.rot("tp", 2, [128, 512], BF16, psum=True)
        pm_r = C.rot("pm", 2, [128, 512], F32, psum=True)
        pp_r = C.rot("pp", 2, [128, 512], F32, psum=True)
        pv_r = C.rot("pv", 2, [128, 512], F32, psum=True)

        def load(ti):
            xt, b_xt = xt_r.next()
            T.dma("sp", xt[:], x_src[ti * 512:(ti + 1) * 512, :].rearrange("(s p) d -> p s d", p=128),
                  reads=[b_xsrc], writes=[b_xt], sb=b_xt)
            rc, b_rc = rc_r.next()
            rs, b_rs = rs_r.next()
            T.dma("sp", rc[:], rc_in[:, ti * 512:(ti + 1) * 512], reads=[bIN], writes=[b_rc], sb=b_rc)
            T.dma("sp", rs[:], rs_in[:, ti * 512:(ti + 1) * 512], reads=[bIN], writes=[b_rs], sb=b_rs)
            return (xt, b_xt, rc, b_rc, rs, b_rs)

        nxt = load(0)
        for ti in range(NT):
            xt, b_xt, rc, b_rc, rs, b_rs = nxt
            if ti + 1 < NT:
                nxt = load(ti + 1)
            xb, b_xb = xb_r.next()
            for s in range(4):
                if s % 2 == 0:
                    T.op("act", lambda: act.copy(out=xb[:, s, :], in_=xt[:, s, :]), reads=[b_xt], writes=[b_xb])
                else:
                    T.op("dve", lambda: dve.tensor_copy(out=xb[:, s, :], in_=xt[:, s, :]), reads=[b_xt], writes=[b_xb])
            xT, b_xT = xT_r.next()
            for kc in range(8):
                tp, b_tp = tp_r.next()
                for s in range(4):
                    T.op("pe", lambda: pe.transpose(out=tp[:, s * 128:(s + 1) * 128], in_=xb[:, s, kc * 128:(kc + 1) * 128],
                                                    identity=ident_bf[:]),
                         reads=[b_xb, b_ident_bf], writes=[b_tp], sig=(s == 3))
                if kc % 2 == 0:
                    T.op("act", lambda: act.copy(out=xT[:, kc, :], in_=tp[:]), reads=[b_tp], writes=[b_xT])
                else:
                    T.op("dve", lambda: dve.tensor_copy(out=xT[:, kc, :], in_=tp[:]), reads=[b_tp], writes=[b_xT])
            for oc in range(16):
                pm, b_pm = pm_r.next()
                pp, b_pp = pp_r.next()
                for kc in range(8):
                    T.op("pe", lambda: pe.matmul(pm[:], lhsT=wm[:, kc, oc * 128:(oc + 1) * 128], rhs=xT[:, kc, :],
                                                 start=(kc == 0), stop=(kc == 7)),
                         reads=[b_wm, b_xT], writes=[b_pm], sig=(kc == 7))
                for kc in range(8):
                    T.op("pe", lambda: pe.matmul(pp[:], lhsT=wp[:, kc, oc * 128:(oc + 1) * 128], rhs=xT[:, kc, :],
                                                 start=(kc == 0), stop=(kc == 7)),
                         reads=[b_wp, b_xT], writes=[b_pp], sig=(kc == 7))
                t1, b_t1 = t1_r.next()
                t2, b_t2 = t2_r.next()
                st, b_st = st_r.next()
                T.op("dve", lambda: dve.tensor_tensor(out=t1[:], in0=pm[:], in1=rc[:], op=ALU.mult),
                     reads=[b_pm, b_rc], writes=[b_t1])
                T.op("dve", lambda: dve.tensor_tensor(out=t2[:], in0=pp[:], in1=rs[:], op=ALU.mult),
                     reads=[b_pp, b_rs], writes=[b_t2])
                T.op("pool", lambda: pool.tensor_tensor(out=st[:], in0=t1[:], in1=t2[:], op=ALU.add),
                     reads=[b_t1, b_t2], writes=[b_st])
                dst = QT if oc < 8 else KT
                bd = bQT[g] if oc < 8 else bKT[g]
                c8 = oc % 8
                T.dma("sp", dst[g, c8 * 128:(c8 + 1) * 128, ti * 512:(ti + 1) * 512], st[:], reads=[b_st], writes=[bd], sb=b_st)
            vst, b_vst = vst_r.next()
            for s in range(4):
                for hf in range(2):
                    pv, b_pv = pv_r.next()
                    for kc in range(8):
                        T.op("pe", lambda: pe.matmul(pv[:], lhsT=xT[:, kc, s * 128:(s + 1) * 128],
                                                     rhs=wm[:, kc, 2048 + hf * 512:2048 + (hf + 1) * 512],
                                                     start=(kc == 0), stop=(kc == 7)),
                             reads=[b_wm, b_xT], writes=[b_pv], sig=(kc == 7))
                    T.op("act", lambda: act.copy(out=vst[:, s, hf * 512:(hf + 1) * 512], in_=pv[:]), reads=[b_pv], writes=[b_vst])
            T.dma("sp", VV[g, ti * 512:(ti + 1) * 512, :].rearrange("(s p) d -> p s d", p=128), vst[:],
                  reads=[b_vst], writes=[bVV[g]], sb=b_vst)
        C.close()

    def da_attention(j, lambda_init):
        C = Ctx(nc, T)
        kb, b_kb = C.sb("kb", [128, NB], F32)
        T.dma("sp", kb[:], kbda, reads=[bIN], writes=[b_kb], sb=b_kb)
        lamv, b_lamv = C.sb("lamv", [1, 4, 64], F32)
        T.dma("sp", lamv[:], da_lam[j:j + 1, :, :], reads=[bIN], writes=[b_lamv], sb=b_lamv)
        lt, b_lt = C.sb("lt", [1, 2, 64], F32)
        ls, b_ls = C.sb("ls", [1, 4], F32)
        T.op("dve", lambda: dve.tensor_tensor(out=lt[:, 0, :], in0=lamv[:, 0, :], in1=lamv[:, 1, :], op=ALU.mult),
             reads=[b_lamv], writes=[b_lt])
        T.op("dve", lambda: dve.tensor_tensor(out=lt[:, 1, :], in0=lamv[:, 2, :], in1=lamv[:, 3, :], op=ALU.mult),
             reads=[b_lamv], writes=[b_lt])
        T.op("dve", lambda: dve.tensor_reduce(out=ls[:, 0:2], in_=lt[:], axis=AX.X, op=ALU.add), reads=[b_lt], writes=[b_ls])
        T.op("act", lambda: act.activation(out=ls[:, 2:4], in_=ls[:, 0:2], func=AF.Exp), reads=[b_ls], writes=[b_ls])
        T.op("dve", lambda: dve.tensor_tensor(out=ls[:, 0:1], in0=ls[:, 3:4], in1=ls[:, 2:3], op=ALU.subtract),
             reads=[b_ls], writes=[b_ls])
        T.op("dve", lambda: dve.tensor_scalar(out=ls[:, 1:2], in0=ls[:, 0:1], scalar1=-lambda_init, scalar2=None, op0=ALU.add),
             reads=[b_ls], writes=[b_ls])
        nlam, b_nlam = C.sb("nlam", [128, 1], F32)
        gsc, b_gsc = C.sb("gsc", [128, 1], F32)
        T.dma("sp", gsc[:], da_g[j, :].rearrange("(p o) -> p o", o=1), reads=[bIN], writes=[b_gsc], sb=b_gsc)
        T.op("dve", lambda: dve.tensor_scalar(out=gsc[:], in0=gsc[:], scalar1=(1.0 - lambda_init), scalar2=None, op0=ALU.mult),
             reads=[b_gsc], writes=[b_gsc])
        sc_r = C.rot("sc", 2, [128, 1024], F32, psum=True)
        OA, b_OA = C.ps("OA", [128, 512])
        OB, b_OB = C.ps("OB", [128, 512])
        SA, b_SA = C.ps("SA", [128, 512])
        SB, b_SB = C.ps("SB", [128, 512])
        MS, b_MS = SA, b_SA
        T.op("pe", lambda: pe.matmul(MS[:, 0:1], lhsT=ones_f[0:1, :], rhs=ls[0:1, 1:2], start=True, stop=True),
             reads=[b_ones_f, b_ls], writes=[b_MS])
        T.op("dve", lambda: dve.tensor_copy(out=nlam[:], in_=MS[:, 0:1]), reads=[b_MS], writes=[b_nlam])
        qt_r = C.rot("qT", 2, [128, S], BF16)
        kt_r = C.rot("kT", 2, [128, S], BF16)
        v_r = C.rot("vh", 2, [128, NB, 128], BF16)
        e_r = C.rot("e", 3, [128, 1024], BF16)
        sacc, b_sacc = C.sb("sacc", [128, 1024], F32)
        b_sacc2 = Buf("sacc2")
        SPL = 768
        f_r = [C.rot("f%d" % i, 1, [128, 512], F32) for i in range(4)]
        ost_r = C.rot("ost", 2, [128, 512], BF16)

        def load(h):
            qT, b_qT = qt_r.next()
            kT, b_kT = kt_r.next()
            vh, b_vh = v_r.next()
            T.dma("sp", qT[:], QT[0, h * 128:(h + 1) * 128, :], reads=[bQT[0]], writes=[b_qT], sb=b_qT)
            T.dma("sp", kT[:], KT[0, h * 128:(h + 1) * 128, :], reads=[bKT[0]], writes=[b_kT], sb=b_kT)
            T.dma("sp", vh[:], VV[0, :, h * 128:(h + 1) * 128].rearrange("(c p) f -> p c f", p=128),
                  reads=[bVV[0]], writes=[b_vh], sb=b_vh)
            return qT, b_qT, kT, b_kT, vh, b_vh

        nxt = load(0)
        for h in range(8):
            qT, b_qT, kT, b_kT, vh, b_vh = nxt
            if h + 1 < 8:
                nxt = load(h + 1)
            for qt in range(NT):
                qs = slice(qt * 512, (qt + 1) * 512)
                pend = None

                def stage2(kc, e, b_e):
                    first, last = kc == 0, kc == NB - 1
                    T.op("pe", lambda: pe.matmul(OA[:], lhsT=vh[:, kc, :], rhs=e[:, 0:512], start=first, stop=last),
                         reads=[b_vh, b_e], writes=[b_OA], sig=False)
                    T.op("pe", lambda: pe.matmul(OB[:], lhsT=vh[:, kc, :], rhs=e[:, 512:1024], start=first, stop=last),
                         reads=[b_vh, b_e], writes=[b_OB], sig=True)
                    if first:
                        T.op("dve", lambda: dve.tensor_copy(out=sacc[:, 0:SPL], in_=e[:, 0:SPL]), reads=[b_e], writes=[b_sacc])
                        T.op("pool", lambda: pool.tensor_copy(out=sacc[:, SPL:1024], in_=e[:, SPL:1024]), reads=[b_e], writes=[b_sacc2])
                    else:
                        T.op("dve", lambda: dve.tensor_tensor(out=sacc[:, 0:SPL], in0=sacc[:, 0:SPL], in1=e[:, 0:SPL], op=ALU.add),
                             reads=[b_e], writes=[b_sacc])
                        T.op("pool", lambda: pool.tensor_tensor(out=sacc[:, SPL:1024], in0=sacc[:, SPL:1024], in1=e[:, SPL:1024], op=ALU.add),
                             reads=[b_e], writes=[b_sacc2])

                for kc in (range(NB) if DA_LOOP else []):
                    ks = slice(kc * 128, (kc + 1) * 128)
                    sc, b_sc = sc_r.next()
                    for m in range(2):
                        ps_ = slice(m * 64, (m + 1) * 64)
                        T.op("pe", lambda: pe.matmul(sc[:, m * 512:(m + 1) * 512], lhsT=kT[ps_, ks], rhs=qT[ps_, qs], start=True, stop=True,
                                                     tile_position=(m * 64, 0)),
                             reads=[b_kT, b_qT], writes=[b_sc], sig=(m == 1))
                    e, b_e = e_r.next()
                    T.op("act", lambda: act.activation(out=e[:], in_=sc[:], func=AF.Exp, bias=kb[:, kc:kc + 1], scale=0.125),
                         reads=[b_sc, b_kb], writes=[b_e])
                    if pend is not None:
                        stage2(*pend)
                    pend = (kc, e, b_e)
                if pend is not None:
                    stage2(*pend)
                if not DA_FIN:
                    continue
                (f0, b_f0), (f1, b_f1), (f2, b_f2), (f3, b_f3) = [r.next() for r in f_r]
                T.op("pe", lambda: pe.matmul(SA[:], lhsT=ones_f[:], rhs=sacc[:, 0:512], start=True, stop=True),
                     reads=[b_ones_f, b_sacc], writes=[b_SA])
                T.op("pe", lambda: pe.matmul(SB[:], lhsT=ones_f[:], rhs=sacc[:, 512:1024], start=True, stop=True),
                     reads=[b_ones_f, b_sacc, b_sacc2], writes=[b_SB])
                T.op("dve", lambda: dve.reciprocal(out=f0[:], in_=SA[:]), reads=[b_SA], writes=[b_f0])
                T.op("dve", lambda: dve.tensor_tensor(out=f1[:], in0=OA[:], in1=f0[:], op=ALU.mult), reads=[b_OA, b_f0], writes=[b_f1])
                T.op("dve", lambda: dve.reciprocal(out=f0[:], in_=SB[:]), reads=[b_SB], writes=[b_f0])
                T.op("dve", lambda: dve.tensor_tensor(out=f2[:], in0=OB[:], in1=f0[:], op=ALU.mult), reads=[b_OB, b_f0], writes=[b_f2])
                T.op("dve", lambda: dve.scalar_tensor_tensor(out=f3[:], in0=f2[:], scalar=nlam[:, 0:1], in1=f1[:],
                                                             op0=ALU.mult, op1=ALU.add),
                     reads=[b_f2, b_f1, b_nlam], writes=[b_f3])
                T.op("act", lambda: act.activation(out=f1[:], in_=f3[:], func=AF.Square), reads=[b_f3], writes=[b_f1])
                T.op("pe", lambda: pe.matmul(MS[:], lhsT=ones_f[:], rhs=f1[:], start=True, stop=True),
                     reads=[b_ones_f, b_f1], writes=[b_MS])
                T.op("dve", lambda: dve.tensor_scalar(out=f2[:], in0=MS[:], scalar1=1.0 / 128.0, scalar2=LN_EPS,
                                                      op0=ALU.mult, op1=ALU.add), reads=[b_MS], writes=[b_f2])
                T.op("act", lambda: act.activation(out=f2[:], in_=f2[:], func=AF.Ln), reads=[b_f2], writes=[b_f2])
                T.op("act", lambda: act.activation(out=f2[:], in_=f2[:], func=AF.Exp, scale=-0.5), reads=[b_f2], writes=[b_f2])
                T.op("dve", lambda: dve.tensor_tensor(out=f3[:], in0=f3[:], in1=f2[:], op=ALU.mult), reads=[b_f3, b_f2], writes=[b_f3])
                ost, b_ost = ost_r.next()
                T.op("dve", lambda: dve.tensor_scalar(out=ost[:], in0=f3[:], scalar1=gsc[:, 0:1], scalar2=None, op0=ALU.mult),
                     reads=[b_f3, b_gsc], writes=[b_ost])
                T.dma("sp", OT[h * 128:(h + 1) * 128, qs], ost[:], reads=[b_ost], writes=[bOT], sb=b_ost)
        C.close()

    def dl_attention():
        C = Ctx(nc, T)
        kbs = []
        for g in range(3):
            t, b = C.sb("kbl%d" % g, [128, NB], F32)
            T.dma("sp", t[:], kbdl[g], reads=[bIN], writes=[b], sb=b)
            kbs.append((t, b))
        raw_r = C.rot("raw", 3, [128, 2048], BF16)
        Qs, b_Qs = C.sb("Qs", [128, S], BF16)
        Ks, b_Ks = C.sb("Ks", [128, S], BF16)
        vs_r = C.rot("Vs", 2, [128, NB, 128], BF16)
        acc, b_acc = C.sb("acc", [128, S], F32)
        acs, b_acs = C.sb("acs", [128, S], F32)
        oo, b_oo = C.sb("oo", [128, S], BF16)
        e_r = C.rot("e", 4, [128, 256], BF16)
        m_r = C.rot("m", 4, [128, 256], BF16)
        sc_r = C.rot("sc", 4, [128, 256], F32, psum=True)
        po_r = C.rot("po", 2, [128, 256], F32, psum=True)
        pq_r = C.rot("pq", 2, [128, 256], F32, psum=True)
        cnt = 0
        for hp in range(8):
            T.op("dve", lambda: dve.memset(acc[:], 0.0), writes=[b_acc])
            T.op("pool", lambda: pool.memset(acs[:], 0.0), writes=[b_acs])
            for g in range(3):
                d = DIL[g]
                M = S // d
                MC = M // 128
                for which, (dstT, b_dst, srcT, b_src) in enumerate(((Qs, b_Qs, QT, bQT[g]), (Ks, b_Ks, KT, bKT[g]))):
                    dview = dstT[:].rearrange("p (r m) -> p r m", r=d)
                    for pc in range(S // 2048):
                        raw, b_raw = raw_r.next()
                        T.dma("sp", raw[:], srcT[g, hp * 128:(hp + 1) * 128, pc * 2048:(pc + 1) * 2048],
                              reads=[b_src], writes=[b_raw], sb=b_raw)
                        ml = 2048 // d
                        en = "pool" if (cnt % 2 == 0) else "dve"
                        cnt += 1
                        eng = pool if en == "pool" else dve
                        T.op(en, lambda: eng.tensor_copy(out=dview[:, :, pc * ml:(pc + 1) * ml],
                                                         in_=raw[:].rearrange("p (m r) -> p r m", r=d)),
                             reads=[b_raw], writes=[b_dst])
                Vs, b_Vs = vs_r.next()
                vsv = Vs[:].rearrange("p (r c) f -> p r c f", r=d)
                vsrc = VV[g, :, hp * 128:(hp + 1) * 128].rearrange("(m r) f -> r m f", r=d)
                for r in range(d):
                    T.dma("sp", vsv[:, r, :, :], vsrc[r].rearrange("(c p) f -> p c f", p=128),
                          reads=[bVV[g]], writes=[b_Vs], sb=b_Vs)
                kbt, b_kbt = kbs[g]
                Qv = Qs[:].rearrange("p (r m) -> p r m", r=d)
                Kv = Ks[:].rearrange("p (r m) -> p r m", r=d)
                accv = acc[:].rearrange("p (m r) -> p r m", r=d)
                acsv = acs[:].rearrange("p (m r) -> p r m", r=d)
                pend = None

                def stage_b(r, c, qlo, qhi, N, ms_):
                    po, b_po = po_r.next()
                    pq, b_pq = pq_r.next()
                    for m in range(2):
                        mm_, b_mm = ms_[m]
                        os_ = slice(m * 64, (m + 1) * 64)
                        T.op("pe", lambda: pe.matmul(po[os_, 0:N], lhsT=Vs[:, r * MC + c, m * 64:(m + 1) * 64], rhs=mm_[:, 0:N],
                                                     start=True, stop=True, tile_position=(0, m * 64)),
                             reads=[b_Vs, b_mm], writes=[b_po], sig=(m == 1))
                    for m in range(2):
                        mm_, b_mm = ms_[m]
                        os_ = slice(m * 64, (m + 1) * 64)
                        T.op("pe", lambda: pe.matmul(pq[os_, 0:N], lhsT=ones_bf[:, 0:64], rhs=mm_[:, 0:N],
                                                     start=True, stop=True, tile_position=(0, m * 64)),
                             reads=[b_ones_bf, b_mm], writes=[b_pq], sig=(m == 1))
                    T.op("dve", lambda: dve.tensor_tensor(out=accv[:, r, qlo:qhi], in0=accv[:, r, qlo:qhi], in1=po[:, 0:N], op=ALU.add),
                         reads=[b_po, b_acc], writes=[b_acc])
                    T.op("dve", lambda: dve.tensor_tensor(out=acsv[:, r, qlo:qhi], in0=acsv[:, r, qlo:qhi], in1=pq[:, 0:N], op=ALU.add),
                         reads=[b_pq, b_acs], writes=[b_acs])

                for r in range(d):
                    for c in range(MC):
                        qlo = max(0, 128 * c - 64)
                        qhi = min(M, 128 * c + 192)
                        N = qhi - qlo
                        mo = qlo - (128 * c - 64)
                        ms_ = []
                        for m in range(2):
                            ps_ = slice(m * 64, (m + 1) * 64)
                            sc, b_sc = sc_r.next()
                            T.op("pe", lambda: pe.matmul(sc[:, 0:N], lhsT=Kv[ps_, r, c * 128:(c + 1) * 128], rhs=Qv[ps_, r, qlo:qhi],
                                                         start=True, stop=True, tile_position=(m * 64, 0)),
                                 reads=[b_Ks, b_Qs], writes=[b_sc])
                            e, b_e = e_r.next()
                            T.op("act", lambda: act.activation(out=e[:, 0:N], in_=sc[:, 0:N], func=AF.Exp,
                                                               bias=kbt[:, r * MC + c:r * MC + c + 1], scale=0.125),
                                 reads=[b_sc, b_kbt], writes=[b_e])
                            mm_, b_mm = m_r.next()
                            en = "pool" if m == 0 else "dve"
                            eng = pool if m == 0 else dve
                            T.op(en, lambda: eng.tensor_tensor(out=mm_[:, 0:N], in0=e[:, 0:N], in1=bmask[:, mo:mo + N], op=ALU.mult),
                                 reads=[b_e, b_bmask], writes=[b_mm])
                            ms_.append((mm_, b_mm))
                        if pend is not None:
                            stage_b(*pend)
                        pend = (r, c, qlo, qhi, N, ms_)
                if pend is not None:
                    stage_b(*pend)
            T.op("dve", lambda: dve.tensor_scalar(out=acs[:], in0=acs[:], scalar1=1e-30, scalar2=None, op0=ALU.max),
                 reads=[b_acs], writes=[b_acs])
            T.op("dve", lambda: dve.reciprocal(out=acs[:], in_=acs[:]), reads=[b_acs], writes=[b_acs])
            T.op("dve", lambda: dve.tensor_tensor(out=oo[:], in0=acc[:], in1=acs[:], op=ALU.mult), reads=[b_acc, b_acs], writes=[b_oo])
            T.dma("sp", OT[hp * 128:(hp + 1) * 128, :], oo[:], reads=[b_oo], writes=[bOT], sb=b_oo)
        C.close()

    def tok_phase(i, x_src, b_xsrc, wo_ap, x_dst, b_xdst):
        C = Ctx(nc, T)
        wo, b_wo = C.sb("wo", [128, 8, D], BF16)
        T.dma("pool", wo[:], wo_ap.rearrange("(c p) f -> p c f", p=128), reads=[bIN], writes=[b_wo], sb=b_wo)
        wr, b_wr = C.sb("wr", [128, 8, 20], BF16)
        T.dma("pool", wr[:], w_r[i].rearrange("(c p) f -> p c f", p=128), reads=[bIN], writes=[b_wr], sb=b_wr)
        lnb = []
        for k in range(4):
            t, b = C.sb("lnb%d" % k, [128, D], F32)
            T.dma("sp", t[:], lnp[i, k, :].partition_broadcast(128), reads=[bIN], writes=[b], sb=b)
            lnb.append((t, b))
        NH = 2
        TM = NH * 512
        NS = NH * 4
        oT_r = C.rot("oT", 2, [128, 8, 512], BF16)
        xt_r = C.rot("xt", 1, [128, 4, D], F32)
        ya, b_ya = C.sb("ya", [128, NS, D], F32)
        tmp, b_tmp = C.sb("tmp", [128, D], F32)
        tmp2, b_tmp2 = C.sb("tmp2", [128, D], F32)
        xa, b_xa = C.sb("xa", [128, 4, TM], BF16)
        x1b, b_x1b = xa, b_xa
        x1T, b_x1T = C.sb("x1T", [128, 8, TM], BF16)
        st_, b_st = C.sb("stt", [128, 8], F32)
        rt, b_rt = C.sb("rt", [128, 64], F32)
        cw, b_cw = C.sb("cw", [128, NS, 16], F32)
        cwT, b_cwT = C.sb("cwT", [16, TM], BF16)
        cwb_r = C.rot("cwb", 2, [128, TM], F32)
        wg_r = C.rot("wg", 2, [128, 8, 512], BF16)
        wu_r = C.rot("wu", 2, [128, 8, 512], BF16)
        wd_r = C.rot("wd", 2, [128, 4, D], BF16)
        sg_r = C.rot("sg", 1, [128, 512], F32)
        tt_r = C.rot("tt", 1, [128, 512], F32)
        aT_r = Rot([(xa, b_xa)])
        xo_r = C.rot("xo", 1, [128, D], F32)
        ph_r = C.rot("ph", 2, [128, 512], F32, psum=True)
        tp_r = C.rot("tp", 1, [128, 512], BF16, psum=True)
        pg_r = C.rot("pg", 2, [128, 512], F32, psum=True)
        pu_r = C.rot("pu", 2, [128, 512], F32, psum=True)
        px, b_px = C.ps("px", [128, 512])

        def layer_norm(src_ap, b_src, gk, out_ap, b_out, extra_writes=()):
            g_, bg = lnb[gk]
            be_, bbe = lnb[gk + 1]
            T.op("dve", lambda: dve.tensor_reduce(out=st_[:, 0:1], in_=src_ap, axis=AX.X, op=ALU.add), reads=[b_src], writes=[b_st])
            T.op("dve", lambda: dve.tensor_tensor(out=tmp2[:], in0=src_ap, in1=src_ap, op=ALU.mult), reads=[b_src], writes=[b_tmp2])
            T.op("dve", lambda: dve.tensor_reduce(out=st_[:, 1:2], in_=tmp2[:], axis=AX.X, op=ALU.add), reads=[b_tmp2], writes=[b_st])
            T.op("dve", lambda: dve.tensor_scalar(out=st_[:, 2:4], in0=st_[:, 0:2], scalar1=1.0 / D, scalar2=None, op0=ALU.mult),
                 reads=[b_st], writes=[b_st])
            T.op("dve", lambda: dve.tensor_tensor(out=st_[:, 4:5], in0=st_[:, 2:3], in1=st_[:, 2:3], op=ALU.mult), reads=[b_st], writes=[b_st])
            T.op("dve", lambda: dve.tensor_tensor(out=st_[:, 5:6], in0=st_[:, 3:4], in1=st_[:, 4:5], op=ALU.subtract), reads=[b_st], writes=[b_st])
            T.op("dve", lambda: dve.tensor_scalar(out=st_[:, 6:7], in0=st_[:, 5:6], scalar1=LN_EPS, scalar2=None, op0=ALU.add),
                 reads=[b_st], writes=[b_st])
            T.op("act", lambda: act.activation(out=st_[:, 7:8], in_=st_[:, 6:7], func=AF.Ln), reads=[b_st], writes=[b_st])
            T.op("act", lambda: act.activation(out=st_[:, 6:7], in_=st_[:, 7:8], func=AF.Exp, scale=-0.5), reads=[b_st], writes=[b_st])
            T.op("dve", lambda: dve.tensor_scalar(out=tmp[:], in0=src_ap, scalar1=st_[:, 2:3], scalar2=st_[:, 6:7],
                                                  op0=ALU.subtract, op1=ALU.mult), reads=[b_src, b_st], writes=[b_tmp])
            T.op("dve", lambda: dve.tensor_tensor(out=tmp[:], in0=tmp[:], in1=g_[:], op=ALU.mult), reads=[b_tmp, bg], writes=[b_tmp])
            T.op("dve", lambda: dve.tensor_tensor(out=out_ap, in0=tmp[:], in1=be_[:], op=ALU.add), reads=[b_tmp, bbe],
                 writes=[b_out] + list(extra_writes))

        def load(ti):
            oT, b_oT = oT_r.next()
            T.dma("sp", oT[:], OT[:, ti * 512:(ti + 1) * 512].rearrange("(c p) t -> p c t", p=128), reads=[bOT], writes=[b_oT], sb=b_oT)
            return oT, b_oT

        def load_x(ti):
            xt, b_xt = xt_r.next()
            T.dma("sp", xt[:], x_src[ti * 512:(ti + 1) * 512, :].rearrange("(s p) d -> p s d", p=128),
                  reads=[b_xsrc], writes=[b_xt], sb=b_xt)
            return xt, b_xt

        def load_w(e):
            wg, b_wg = wg_r.next()
            wu, b_wu = wu_r.next()
            wd, b_wd = wd_r.next()
            wb_ = WB[i % 2]
            T.dma("sp", wg[:], wb_[0][e].rearrange("(c p) f -> p c f", p=128), reads=[bWB[i % 2]], writes=[b_wg], sb=b_wg)
            T.dma("sp", wu[:], wb_[1][e].rearrange("(c p) f -> p c f", p=128), reads=[bWB[i % 2]], writes=[b_wu], sb=b_wu)
            T.dma("sp", wd[:], wb_[2][e].rearrange("(c p) f -> p c f", p=128), reads=[bWB[i % 2]], writes=[b_wd], sb=b_wd)
            return wg, b_wg, wu, b_wu, wd, b_wd

        nxt = load(0)
        nw = load_w(0)
        TOK_DBG = int(os.environ.get("TOK_DBG", "0"))
        for sti in range(NT // NH):
            for hh in range(NH):
                ti = sti * NH + hh
                oT, b_oT = nxt
                xt, b_xt = load_x(ti)
                if ti + 1 < NT:
                    nxt = load(ti + 1)
                for s in range(4):
                    sg_ = hh * 4 + s
                    for hf in range(2):
                        ph, b_ph = ph_r.next()
                        for fc in range(8):
                            T.op("pe", lambda: pe.matmul(ph[:], lhsT=oT[:, fc, s * 128:(s + 1) * 128], rhs=wo[:, fc, hf * 512:(hf + 1) * 512],
                                                         start=(fc == 0), stop=(fc == 7)),
                                 reads=[b_oT, b_wo], writes=[b_ph], sig=(fc == 7))
                        T.op("dve", lambda: dve.scalar_tensor_tensor(out=ya[:, sg_, hf * 512:(hf + 1) * 512], in0=xt[:, s, hf * 512:(hf + 1) * 512],
                                                                     scalar=ALPHA, in1=ph[:], op0=ALU.mult, op1=ALU.add),
                             reads=[b_xt, b_ph], writes=[b_ya])
                    layer_norm(ya[:, sg_, :], b_ya, 0, ya[:, sg_, :], b_ya)
                    T.op("act", lambda: act.copy(out=x1b[:, s, :], in_=ya[:, sg_, :]), reads=[b_ya], writes=[b_x1b])
                for kc in range(8):
                    tp, b_tp = tp_r.next()
                    for s in range(4):
                        T.op("pe", lambda: pe.transpose(out=tp[:, s * 128:(s + 1) * 128], in_=x1b[:, s, kc * 128:(kc + 1) * 128],
                                                        identity=ident_bf[:]),
                             reads=[b_x1b, b_ident_bf], writes=[b_tp], sig=(s == 3))
                    T.op("act", lambda: act.copy(out=x1T[:, kc, hh * 512:(hh + 1) * 512], in_=tp[:]), reads=[b_tp], writes=[b_x1T])
                for s in range(4):
                    sg_ = hh * 4 + s
                    for kc in range(8):
                        T.op("pe", lambda: pe.matmul(px[:, 0:20], lhsT=x1T[:, kc, hh * 512 + s * 128:hh * 512 + (s + 1) * 128], rhs=wr[:, kc, :],
                                                     start=(kc == 0), stop=(kc == 7)),
                             reads=[b_x1T, b_wr], writes=[b_px], sig=(kc == 7))
                    R = lambda a, b: rt[:, a:b]
                    V = lambda fn: T.op("dve", fn, reads=[b_rt], writes=[b_rt])
                    T.op("dve", lambda: dve.tensor_copy(out=R(0, 20), in_=px[:, 0:20]), reads=[b_px], writes=[b_rt])
                    V(lambda: dve.tensor_reduce(out=R(20, 21), in_=R(0, 4), axis=AX.X, op=ALU.max))
                    V(lambda: dve.tensor_scalar(out=R(21, 25), in0=R(0, 4), scalar1=R(20, 21), scalar2=None, op0=ALU.is_equal))
                    V(lambda: dve.tensor_scalar(out=R(25, 26), in0=R(20, 21), scalar1=-1.0, scalar2=None, op0=ALU.mult))
                    T.op("act", lambda: act.activation(out=R(26, 30), in_=R(0, 4), func=AF.Exp, bias=R(25, 26), scale=1.0),
                         reads=[b_rt], writes=[b_rt])
                    V(lambda: dve.tensor_reduce(out=R(30, 31), in_=R(26, 30), axis=AX.X, op=ALU.add))
                    V(lambda: dve.reciprocal(out=R(31, 32), in_=R(30, 31)))
                    V(lambda: dve.tensor_scalar(out=R(32, 36), in0=R(4, 8), scalar1=R(21, 22), scalar2=None, op0=ALU.mult))
                    for gg in range(1, 4):
                        V(lambda: dve.scalar_tensor_tensor(out=R(32, 36), in0=R(4 + 4 * gg, 8 + 4 * gg), scalar=R(21 + gg, 22 + gg),
                                                           in1=R(32, 36), op0=ALU.mult, op1=ALU.add))
                    V(lambda: dve.tensor_reduce(out=R(36, 37), in_=R(32, 36), axis=AX.X, op=ALU.max))
                    V(lambda: dve.tensor_scalar(out=R(37, 41), in0=R(32, 36), scalar1=R(36, 37), scalar2=None, op0=ALU.is_equal))
                    V(lambda: dve.scalar_tensor_tensor(out=R(41, 45), in0=R(37, 41), scalar=-1e30, in1=R(32, 36), op0=ALU.mult, op1=ALU.add))
                    V(lambda: dve.tensor_reduce(out=R(45, 46), in_=R(41, 45), axis=AX.X, op=ALU.max))
                    V(lambda: dve.tensor_scalar(out=R(46, 50), in0=R(41, 45), scalar1=R(45, 46), scalar2=None, op0=ALU.is_equal))
                    V(lambda: dve.tensor_tensor(out=R(50, 51), in0=R(45, 46), in1=R(36, 37), op=ALU.subtract))
                    T.op("act", lambda: act.activation(out=R(51, 52), in_=R(50, 51), func=AF.Exp), reads=[b_rt], writes=[b_rt])
                    V(lambda: dve.tensor_scalar(out=R(52, 53), in0=R(51, 52), scalar1=1.0, scalar2=None, op0=ALU.add))
                    V(lambda: dve.reciprocal(out=R(53, 54), in_=R(52, 53)))
                    V(lambda: dve.tensor_tensor(out=R(54, 55), in0=R(53, 54), in1=R(31, 32), op=ALU.mult))
                    V(lambda: dve.tensor_tensor(out=R(55, 56), in0=R(31, 32), in1=R(54, 55), op=ALU.subtract))
                    V(lambda: dve.tensor_scalar(out=R(56, 60), in0=R(37, 41), scalar1=R(54, 55), scalar2=None, op0=ALU.mult))
                    V(lambda: dve.scalar_tensor_tensor(out=R(56, 60), in0=R(46, 50), scalar=R(55, 56), in1=R(56, 60), op0=ALU.mult, op1=ALU.add))
                    for gg in range(4):
                        T.op("dve", lambda: dve.tensor_scalar(out=cw[:, sg_, gg * 4:(gg + 1) * 4], in0=R(56, 60), scalar1=R(21 + gg, 22 + gg),
                                                              scalar2=None, op0=ALU.mult), reads=[b_rt], writes=[b_cw])
                for s in range(4):
                    T.op("pe", lambda: pe.transpose(out=px[0:16, s * 128:(s + 1) * 128], in_=cw[:, hh * 4 + s, :], identity=ident_f[:]),
                         reads=[b_cw, b_ident_f], writes=[b_px], sig=(s == 3))
                T.op("dve", lambda: dve.tensor_copy(out=cwT[:, hh * 512:(hh + 1) * 512], in_=px[0:16, :]), reads=[b_px], writes=[b_cwT])
            if TOK_DBG == 1:
                for s in range(NS):
                    T.dma("sp", x_dst[sti * TM + s * 128:sti * TM + (s + 1) * 128, :], ya[:, s, :], reads=[b_ya], writes=[b_xdst], sb=b_ya)
                continue
            T.op("pool", lambda: pool.tensor_scalar(out=ya[:], in0=ya[:], scalar1=ALPHA, scalar2=None, op0=ALU.mult),
                 reads=[b_ya], writes=[b_ya])
            for e in range(16):
                wg, b_wg, wu, b_wu, wd, b_wd = nw
                if not (sti == NT // NH - 1 and e == 15):
                    nw = load_w((e + 1) % 16)
                cwb, b_cwb = cwb_r.next()
                for hh in range(NH):
                    T.op("pe", lambda: pe.matmul(px[:], lhsT=sel[0:16, e * 128:(e + 1) * 128], rhs=cwT[0:16, hh * 512:(hh + 1) * 512],
                                                 start=True, stop=True), reads=[b_sel, b_cwT], writes=[b_px])
                    T.op("act", lambda: act.copy(out=cwb[:, hh * 512:(hh + 1) * 512], in_=px[:]), reads=[b_px], writes=[b_cwb])
                aT, b_aT = aT_r.next()
                for fcn in range(4):
                    for hh in range(NH):
                        hs = slice(hh * 512, (hh + 1) * 512)
                        pg, b_pg = pg_r.next()
                        pu, b_pu = pu_r.next()
                        for kc in range(8):
                            T.op("pe", lambda: pe.matmul(pg[:], lhsT=wg[:, kc, fcn * 128:(fcn + 1) * 128], rhs=x1T[:, kc, hs],
                                                         start=(kc == 0), stop=(kc == 7)), reads=[b_wg, b_x1T], writes=[b_pg], sig=(kc == 7))
                        for kc in range(8):
                            T.op("pe", lambda: pe.matmul(pu[:], lhsT=wu[:, kc, fcn * 128:(fcn + 1) * 128], rhs=x1T[:, kc, hs],
                                                         start=(kc == 0), stop=(kc == 7)), reads=[b_wu, b_x1T], writes=[b_pu], sig=(kc == 7))
                        sg, b_sg = sg_r.next()
                        tt, b_tt = tt_r.next()
                        T.op("act", lambda: act.activation(out=sg[:], in_=pg[:], func=AF.Silu), reads=[b_pg], writes=[b_sg])
                        T.op("dve", lambda: dve.tensor_tensor(out=tt[:], in0=pu[:], in1=cwb[:, hs], op=ALU.mult), reads=[b_pu, b_cwb], writes=[b_tt])
                        T.op("pool", lambda: pool.tensor_tensor(out=aT[:, fcn, hs], in0=sg[:], in1=tt[:], op=ALU.mult),
                             reads=[b_sg, b_tt], writes=[b_aT])
                for s in range(NS):
                    for hf in range(2):
                        ph, b_ph = ph_r.next()
                        for fcn in range(4):
                            T.op("pe", lambda: pe.matmul(ph[:], lhsT=aT[:, fcn, s * 128:(s + 1) * 128], rhs=wd[:, fcn, hf * 512:(hf + 1) * 512],
                                                         start=(fcn == 0), stop=(fcn == 3)), reads=[b_aT, b_wd], writes=[b_ph], sig=(fcn == 3))
                        T.op("dve", lambda: dve.tensor_tensor(out=ya[:, s, hf * 512:(hf + 1) * 512], in0=ya[:, s, hf * 512:(hf + 1) * 512],
                                                              in1=ph[:], op=ALU.add), reads=[b_ph, b_ya], writes=[b_ya])
            if TOK_DBG == 2:
                for s in range(NS):
                    T.dma("sp", x_dst[sti * TM + s * 128:sti * TM + (s + 1) * 128, :], ya[:, s, :], reads=[b_ya], writes=[b_xdst], sb=b_ya)
                continue
            for s in range(NS):
                xo, b_xo = xo_r.next()
                layer_norm(ya[:, s, :], b_ya, 2, xo[:], b_xo)
                T.dma("sp", x_dst[sti * TM + s * 128:sti * TM + (s + 1) * 128, :], xo[:], reads=[b_xo], writes=[b_xdst], sb=b_xo)
        C.close()

    cur, b_cur = x_in, bIN
    for i in layers:
        j = i // 2
        last = (i == layers[-1])
        dst, b_dst = (y_out, Buf("yout")) if last else (XR[i % 2], bXR[i % 2])
        if i % 2 == 0:
            lambda_init = 0.8 - 0.6 * math.exp(-0.3 * i)
            proj_pass(cur, b_cur, da_wm[j], da_wp[j], 0)
            convert_weights(i)
            if KSTOP >= 2:
                da_attention(j, lambda_init)
            if KSTOP >= 3:
                tok_phase(i, cur, b_cur, da_wo[j], dst, b_dst)
        else:
            for g in range(3):
                proj_pass(cur, b_cur, dl_wm[j, g], dl_wp[j, g], g)
            convert_weights(i)
            dl_attention()
            tok_phase(i, cur, b_cur, dl_wo[j], dst, b_dst)
        cur, b_cur = dst, b_dst
    T.barrier()
    G.es.close()
    return nc, T


def _rope_tables(S):
    half = 8
    inv = (ROPE_THETA ** (-np.arange(half, dtype=np.float32) * np.float32(2.0 / 16))).astype(np.float32)
    ang = np.arange(S, dtype=np.float32)[:, None] * inv[None, :]
    cos = np.cos(ang).astype(np.float32).T
    sin = np.sin(ang).astype(np.float32).T
    C = np.ones((128, S), np.float32)
    Sg = np.zeros((128, S), np.float32)
    for hh in range(2):
        b = hh * 64
        C[b:b + 8] = cos
        C[b + 8:b + 16] = cos
        Sg[b:b + 8] = -sin
        Sg[b + 8:b + 16] = sin
    return C, Sg


def _partner_perm(ncols):
    idx = np.arange(ncols)
    i = idx % 64
    p = idx.copy()
    p[i < 8] = idx[i < 8] + 8
    m = (i >= 8) & (i < 16)
    p[m] = idx[m] - 8
    return p


def prep_shared(inp, S):
    f = lambda a: np.ascontiguousarray(np.asarray(a, dtype=np.float32))
    da_w_in = f(inp["da_w_in"])
    hidx = np.arange(8)[:, None, None]
    midx = np.arange(2)[None, :, None]
    didx = np.arange(64)[None, None, :]
    qcols = (midx * 512 + hidx * 64 + didx).reshape(-1)
    kcols = qcols + 1024
    pp = _partner_perm(1024)
    da_wm = np.concatenate([da_w_in[:, :, qcols], da_w_in[:, :, kcols], da_w_in[:, :, 2048:]], axis=2)
    da_wp = np.concatenate([da_w_in[:, :, qcols[pp]], da_w_in[:, :, kcols[pp]]], axis=2)
    dl = f(inp["dl_w_in"]).reshape(2, D, 3, 3, 1024)
    dl_wm = np.ascontiguousarray(dl.transpose(0, 2, 1, 3, 4).reshape(2, 3, D, 3072))
    dlq = dl[:, :, :, 0, :][..., pp]
    dlk = dl[:, :, :, 1, :][..., pp]
    dl_wp = np.ascontiguousarray(np.concatenate([dlq, dlk], axis=-1).transpose(0, 2, 1, 3))
    C, Sg = _rope_tables(S)
    ii = np.arange(128)[:, None]
    jj = np.arange(256)[None, :]
    bmask = ((ii >= jj - 128) & (ii <= jj)).astype(np.float32)
    sel = np.zeros((16, 16, 128), np.float32)
    for e in range(16):
        sel[e, e, :] = 1.0
    sh = {
        "ropec": C, "ropes": Sg, "ident": np.eye(128, dtype=np.float32), "bmask": bmask,
        "sel": sel.reshape(16, 16 * 128),
        "da_wm": np.ascontiguousarray(da_wm), "da_wp": np.ascontiguousarray(da_wp), "da_wo": f(inp["da_w_out"]),
        "da_lam": np.ascontiguousarray(np.stack([f(inp["da_lambda_q1"]), f(inp["da_lambda_k1"]),
                                                 f(inp["da_lambda_q2"]), f(inp["da_lambda_k2"])], axis=1)),
        "da_g": f(inp["da_subln_g"]),
        "dl_wm": dl_wm, "dl_wp": dl_wp, "dl_wo": f(inp["dl_w_out"]),
        "lnp": np.ascontiguousarray(np.stack([f(inp["ln1_g"]), f(inp["ln1_b"]), f(inp["ln2_g"]), f(inp["ln2_b"])], axis=1)),
        "w_r": np.ascontiguousarray(np.concatenate([f(inp["moe_router_group"]),
                                                    f(inp["moe_router_expert"]).reshape(4, D, 16)], axis=2)),
        "w_gate": f(inp["moe_w_gate"]).reshape(4, 16, D, 512),
        "w_up": f(inp["moe_w_up"]).reshape(4, 16, D, 512),
        "w_down": f(inp["moe_w_down"]).reshape(4, 16, 512, D),
    }
    return sh


def prep_core(xseq, S):
    L = xseq.shape[0]
    xp = np.zeros((S, D), np.float32)
    xp[:L] = xseq
    kb = np.zeros((S,), np.float32)
    kb[L:] = NEG
    NB = S // 128
    m = {"x": xp, "kbda": np.ascontiguousarray(kb.reshape(NB, 128).T)}
    for g, d in enumerate(DIL):
        M = S // d
        MC = M // 128
        t = kb.reshape(MC, 128, d)
        m["kbdl%d" % g] = np.ascontiguousarray(t.transpose(1, 2, 0).reshape(128, d * MC))
    return m


_CACHE = {}


def kernel(**inputs):
    S = 8192
    xp = np.asarray(inputs["x_prompt"], dtype=np.float32)
    xs = np.asarray(inputs["x_sample"], dtype=np.float32)
    sh = prep_shared(inputs, S)
    seqs = [xp[b] for b in range(4)] + [xs[b] for b in range(4)]
    in_maps = []
    for c in range(8):
        m = dict(sh)
        m.update(prep_core(seqs[c], S))
        in_maps.append(m)
    if "nc" not in _CACHE:
        _CACHE["nc"] = build_program(S, [0, 1, 2, 3])[0]
    nc = _CACHE["nc"]
    res = run_bass_kernel_spmd(nc, in_maps, core_ids=list(range(8)))
    ys = [np.asarray(r["y"], dtype=np.float32) for r in res.results]
    y_prompt = np.stack([ys[b] for b in range(4)], axis=0)
    y_sample = np.stack([ys[4 + b][:xs.shape[1]] for b in range(4)], axis=0)
    return (y_prompt, y_sample)
```
